# Optimizing a Trainium2 kernel written in Bass

```python
import functools
import jax, jax.numpy as jnp
from jax import lax
import numpy as np

D_MODEL = 1024
BATCH = 8
SEQ = 2048
DEPTH = 2
DEC_BATCH = 128
DEC_SEQ = 4
PAST_LEN = 8192
PAGE_SIZE = 128

BR_W = D_MODEL // 2
N_BRANCH = 4
CONV_K = 3
HG_HEADS = 4
HG_DK = BR_W // HG_HEADS
HG_CHUNK = 64
MLA_HEADS = 4
MLA_NOPE = 64
MLA_ROPE = 32
MLA_QK = MLA_NOPE + MLA_ROPE
MLA_V = BR_W // MLA_HEADS
MLA_Q_RANK = 192
MLA_KV_RANK = 128
MLA_SCALE = MLA_QK ** -0.5
ROPE_THETA = 10000.0
Q_BLOCK = 128
MEM_LEN = 256
MEM_HEADS = 4
MEM_DH = BR_W // MEM_HEADS
MEM_SCALE = MEM_DH ** -0.5
EPS = 1e-6
NEG = -1e30
IN_SIZES = (BR_W, BR_W, BR_W,
            BR_W, BR_W, BR_W,
            MLA_Q_RANK, MLA_KV_RANK, MLA_ROPE,
            BR_W,
            N_BRANCH * BR_W,
            N_BRANCH * D_MODEL)
N_IN = sum(IN_SIZES)

kernel_name = 'hybrid_conv_hgrn2_mla_mem_step'


def rmsnorm(x, g):
    xf = x.astype(jnp.float32)
    y = xf * lax.rsqrt(jnp.mean(xf * xf, axis=-1, keepdims=True) + EPS)
    return (y * g).astype(x.dtype)


def rope(x, pos):
    half = MLA_ROPE // 2
    freqs = ROPE_THETA ** (-jnp.arange(half, dtype=jnp.float32) / half)
    ang = pos.astype(jnp.float32)[:, None] * freqs[None, :]
    ang = ang.reshape((pos.shape[0],) + (1,) * (x.ndim - 3) + (half,))
    cos, sin = jnp.cos(ang), jnp.sin(ang)
    xf = x.astype(jnp.float32)
    x1, x2 = xf[..., :half], xf[..., half:]
    return jnp.concatenate([x1 * cos - x2 * sin, x1 * sin + x2 * cos], -1).astype(x.dtype)


def hgrn_lower_bounds(lb_param):
    p = jax.nn.softmax(lb_param.astype(jnp.float32), axis=0)
    return jnp.cumsum(p, axis=0) - p[0]


def hgrn_forget(z, lb):
    f = lb + (1.0 - lb) * jax.nn.sigmoid(z.astype(jnp.float32))
    return jnp.log(f), 1.0 - f


def hgrn2_chunked(q, k, v, logf, S0):
    B, T, H, DK = q.shape
    DV = v.shape[-1]
    L = min(HG_CHUNK, T)
    n = -(-T // L)
    pad = n * L - T

    def chunks(a):
        a = jnp.pad(a.astype(jnp.float32), ((0, 0), (0, pad), (0, 0), (0, 0)))
        return a.reshape(B, n, L, H, a.shape[-1]).transpose(1, 0, 3, 2, 4)

    causal = jnp.tril(jnp.ones((L, L), bool))[:, :, None]

    def step(S, inp):
        qc, kc, vc, gc = inp
        b = jnp.cumsum(gc, axis=2)
        o = jnp.einsum('bhtk,bhkv->bhtv', qc * jnp.exp(b), S)
        decay = jnp.exp(jnp.where(causal, b[:, :, :, None, :] - b[:, :, None, :, :], NEG))
        a = jnp.einsum('bhtk,bhsk,bhtsk->bhts', qc, kc, decay)
        o = o + jnp.einsum('bhts,bhsv->bhtv', a, vc)
        bl = b[:, :, -1:, :]
        S = jnp.exp(bl[:, :, 0, :])[..., None] * S + jnp.einsum('bhsk,bhsv->bhkv', kc * jnp.exp(bl - b), vc)
        return S, o

    S, o = lax.scan(step, S0.astype(jnp.float32), (chunks(q), chunks(k), chunks(v), chunks(logf)))
    o = o.transpose(1, 0, 3, 2, 4).reshape(B, n * L, H, DV)[:, :T]
    return o, S


def mla_keys(c, r, w_uk, gk):
    lead = c.shape[:-1]
    kn = (c @ w_uk).reshape(lead + (MLA_HEADS, MLA_NOPE))
    kr = jnp.broadcast_to(r[..., None, :], lead + (MLA_HEADS, MLA_ROPE)).astype(kn.dtype)
    return rmsnorm(jnp.concatenate([kn, kr], -1), gk)


def mla_prompt_attend(q, c, r, w_uk, gk):
    B, T = q.shape[:2]
    k = mla_keys(c, r, w_uk, gk)
    blk = min(Q_BLOCK, T)
    nb = T // blk
    qb = q.reshape(B, nb, blk, MLA_HEADS, MLA_QK).transpose(1, 0, 2, 3, 4)
    kpos = jnp.arange(T)

    def block(args):
        qi, i = args
        qpos = i * blk + jnp.arange(blk)
        s = jnp.einsum('bqhd,bkhd->bhqk', qi, k).astype(jnp.float32) * MLA_SCALE
        s = jnp.where((kpos[None, :] <= qpos[:, None])[None, None], s, NEG)
        p = jax.nn.softmax(s, axis=-1).astype(c.dtype)
        return jnp.einsum('bhqk,bkr->bqhr', p, c)

    o = lax.map(block, (qb, jnp.arange(nb)))
    return o.transpose(1, 0, 2, 3, 4).reshape(B, T, MLA_HEADS, MLA_KV_RANK)


def mla_sample_attend(q, c, r, w_uk, gk, layer, pool_c, pool_r, page_table):
    Tq = q.shape[1]
    past = page_table.shape[1] * pool_c.shape[2]
    qpos = past + jnp.arange(Tq)
    kpos = jnp.arange(past + Tq)
    mask = kpos[None, :] <= qpos[:, None]

    def one(args):
        qb, cb, rb, pages = args
        c_all = jnp.concatenate([pool_c[layer, pages].reshape(past, MLA_KV_RANK).astype(cb.dtype), cb], 0)
        r_all = jnp.concatenate([pool_r[layer, pages].reshape(past, MLA_ROPE).astype(rb.dtype), rb], 0)
        k = mla_keys(c_all, r_all, w_uk, gk)
        s = jnp.einsum('qhd,khd->hqk', qb, k).astype(jnp.float32) * MLA_SCALE
        s = jnp.where(mask[None], s, NEG)
        p = jax.nn.softmax(s, axis=-1).astype(c_all.dtype)
        return jnp.einsum('hqk,kr->qhr', p, c_all)

    return lax.map(one, (q, c, r, page_table))


def memory_kv(mem, g, w_k, w_v, gk):
    B, M, _ = mem.shape
    mn = rmsnorm(mem, g)
    k = rmsnorm((mn @ w_k).reshape(B, M, MEM_HEADS, MEM_DH), gk)
    v = (mn @ w_v).reshape(B, M, MEM_HEADS, MEM_DH)
    return k, v


def hybrid_layer(x, conv_hist, S0, mem_k, mem_v, pos, lb, attend, lw):
    (norm_g, w_in, conv_w, hg_g, q_norm_g, w_uq, kv_norm_g, w_uk, w_uv, gq, gk, mem_gq, w_bout, w_o) = lw
    B, T, _ = x.shape
    hn = rmsnorm(x, norm_g)
    z = hn @ w_in
    offs = [int(o) for o in np.cumsum(IN_SIZES)[:-1]]
    (c_h, c_b, c_c, hg_q, hg_f, hg_i, q_lat, kv_lat, k_pe, m_q, sg, mg) = jnp.split(z, offs, axis=-1)

    u = c_c * c_h
    upad = jnp.concatenate([conv_hist.astype(u.dtype), u], axis=1)
    conv = sum(conv_w[j] * upad[:, j:j + T] for j in range(CONV_K))
    y_conv = c_b * conv
    new_hist = upad[:, -(CONV_K - 1):]

    logf, kk = hgrn_forget(hg_f, lb)
    hs = (B, T, HG_HEADS, HG_DK)
    o_hg, S = hgrn2_chunked((hg_q * HG_DK ** -0.5).reshape(hs), kk.reshape(hs), hg_i.reshape(hs), logf.reshape(hs), S0)
    y_hg = rmsnorm(o_hg, hg_g.reshape(HG_HEADS, HG_DK)).astype(x.dtype).reshape(B, T, BR_W)

    qf = (rmsnorm(q_lat, q_norm_g) @ w_uq).reshape(B, T, MLA_HEADS, MLA_QK)
    q = rmsnorm(jnp.concatenate([qf[..., :MLA_NOPE], rope(qf[..., MLA_NOPE:], pos)], -1), gq)
    c = rmsnorm(kv_lat, kv_norm_g)
    r = rope(k_pe, pos)
    o_lat = attend(q, c, r, w_uk, gk)
    y_mla = jnp.einsum('bthr,rhv->bthv', o_lat, w_uv.reshape(MLA_KV_RANK, MLA_HEADS, MLA_V)).reshape(B, T, BR_W)

    mq = rmsnorm(m_q.reshape(B, T, MEM_HEADS, MEM_DH), mem_gq)
    s = jnp.einsum('bthd,bmhd->bhtm', mq, mem_k.astype(mq.dtype)).astype(jnp.float32) * MEM_SCALE
    p = jax.nn.softmax(s, axis=-1).astype(mem_v.dtype)
    y_mem = jnp.einsum('bhtm,bmhd->bthd', p, mem_v).reshape(B, T, BR_W).astype(x.dtype)

    ys = jnp.stack([y_conv, y_hg, y_mla, y_mem], axis=2) * jax.nn.silu(sg.reshape(B, T, N_BRANCH, BR_W))
    proj = jnp.einsum('btnw,nwd->btnd', ys, w_bout)
    merged = jnp.sum(jax.nn.sigmoid(mg.reshape(B, T, N_BRANCH, D_MODEL)) * proj, axis=2)
    return x + merged @ w_o, new_hist, S, c, r


def setup_inputs(seed: int = 0) -> dict:
    key = jax.random.key(seed)
    ks = list(jax.random.split(key, 32))

    def nrm(shape, scale=1.0):
        return jax.random.normal(ks.pop(), shape, jnp.float32) * scale

    def gain(shape):
        return 1.0 + 0.1 * jax.random.normal(ks.pop(), shape, jnp.float32)

    n_pages = PAST_LEN // PAGE_SIZE
    n_used = DEC_BATCH * n_pages
    n_phys = n_used + max(n_used // 4, 1)
    x_prompt = nrm((BATCH, SEQ, D_MODEL))
    x_sample = nrm((DEC_BATCH, DEC_SEQ, D_MODEL))
    mem_prompt = nrm((BATCH, MEM_LEN, D_MODEL))
    cache_mla_latent = nrm((DEPTH, n_phys, PAGE_SIZE, MLA_KV_RANK))
    cache_mla_rope = nrm((DEPTH, n_phys, PAGE_SIZE, MLA_ROPE))
    page_table = jax.random.permutation(ks.pop(), n_phys)[:n_used].reshape(DEC_BATCH, n_pages).astype(jnp.int32)
    state_hgrn = nrm((DEPTH, DEC_BATCH, HG_HEADS, HG_DK, HG_DK), 0.5)
    state_conv = nrm((DEPTH, DEC_BATCH, CONV_K - 1, BR_W))
    cache_mem_k = nrm((DEPTH, DEC_BATCH, MEM_LEN, MEM_HEADS, MEM_DH))
    cache_mem_v = nrm((DEPTH, DEC_BATCH, MEM_LEN, MEM_HEADS, MEM_DH))
    return {
        'x_prompt': x_prompt, 'x_sample': x_sample, 'mem_prompt': mem_prompt,
        'cache_mla_latent': cache_mla_latent, 'cache_mla_rope': cache_mla_rope, 'page_table': page_table,
        'state_hgrn': state_hgrn, 'state_conv': state_conv,
        'cache_mem_k': cache_mem_k, 'cache_mem_v': cache_mem_v,
        'norm_gain': gain((DEPTH, D_MODEL)),
        'w_in': nrm((DEPTH, D_MODEL, N_IN), D_MODEL ** -0.5),
        'conv_w': nrm((DEPTH, CONV_K, BR_W), CONV_K ** -0.5),
        'hgrn_lb': nrm((DEPTH, BR_W)),
        'hgrn_norm': gain((DEPTH, BR_W)),
        'mla_q_norm': gain((DEPTH, MLA_Q_RANK)),
        'mla_w_uq': nrm((DEPTH, MLA_Q_RANK, MLA_HEADS * MLA_QK), MLA_Q_RANK ** -0.5),
        'mla_kv_norm': gain((DEPTH, MLA_KV_RANK)),
        'mla_w_uk': nrm((DEPTH, MLA_KV_RANK, MLA_HEADS * MLA_NOPE), MLA_KV_RANK ** -0.5),
        'mla_w_uv': nrm((DEPTH, MLA_KV_RANK, MLA_HEADS * MLA_V), MLA_KV_RANK ** -0.5),
        'mla_q_gain': gain((DEPTH, MLA_QK)),
        'mla_k_gain': gain((DEPTH, MLA_QK)),
        'mem_norm': gain((DEPTH, D_MODEL)),
        'mem_w_k': nrm((DEPTH, D_MODEL, BR_W), D_MODEL ** -0.5),
        'mem_w_v': nrm((DEPTH, D_MODEL, BR_W), D_MODEL ** -0.5),
        'mem_q_gain': gain((DEPTH, MEM_DH)),
        'mem_k_gain': gain((DEPTH, MEM_DH)),
        'w_branch_out': nrm((DEPTH, N_BRANCH, BR_W, D_MODEL), BR_W ** -0.5),
        'w_out': nrm((DEPTH, D_MODEL, D_MODEL), D_MODEL ** -0.5),
    }


def reference(x_prompt, x_sample, mem_prompt, cache_mla_latent, cache_mla_rope, page_table, state_hgrn, state_conv,
              cache_mem_k, cache_mem_v, norm_gain, w_in, conv_w, hgrn_lb, hgrn_norm, mla_q_norm, mla_w_uq,
              mla_kv_norm, mla_w_uk, mla_w_uv, mla_q_gain, mla_k_gain, mem_norm, mem_w_k, mem_w_v, mem_q_gain,
              mem_k_gain, w_branch_out, w_out):
    lbs = hgrn_lower_bounds(hgrn_lb)
    Bp, Tp, _ = x_prompt.shape
    Td = x_sample.shape[1]
    past = page_table.shape[1] * cache_mla_latent.shape[2]
    pos_p = jnp.arange(Tp, dtype=jnp.int32)
    pos_s = past + jnp.arange(Td, dtype=jnp.int32)
    xp, xs = x_prompt, x_sample
    p_lat, p_rope, p_hg, p_conv, p_mk, p_mv = [], [], [], [], [], []
    s_lat, s_rope, s_hg, s_conv = [], [], [], []
    for l in range(DEPTH):
        lw = (norm_gain[l], w_in[l], conv_w[l], hgrn_norm[l], mla_q_norm[l], mla_w_uq[l], mla_kv_norm[l],
              mla_w_uk[l], mla_w_uv[l], mla_q_gain[l], mla_k_gain[l], mem_q_gain[l], w_branch_out[l], w_out[l])
        mk, mv = memory_kv(mem_prompt, mem_norm[l], mem_w_k[l], mem_w_v[l], mem_k_gain[l])
        conv0 = jnp.zeros((Bp, CONV_K - 1, BR_W), xp.dtype)
        S0 = jnp.zeros((Bp, HG_HEADS, HG_DK, HG_DK), jnp.float32)
        xp, hist, S, c, r = hybrid_layer(xp, conv0, S0, mk, mv, pos_p, lbs[l], mla_prompt_attend, lw)
        p_lat.append(c)
        p_rope.append(r)
        p_hg.append(S.astype(xp.dtype))
        p_conv.append(hist)
        p_mk.append(mk)
        p_mv.append(mv)
        attend_s = functools.partial(mla_sample_attend, layer=l, pool_c=cache_mla_latent, pool_r=cache_mla_rope,
                                     page_table=page_table)
        xs, hist_s, S_s, c_s, r_s = hybrid_layer(xs, state_conv[l], state_hgrn[l], cache_mem_k[l], cache_mem_v[l],
                                                 pos_s, lbs[l], attend_s, lw)
        s_lat.append(c_s)
        s_rope.append(r_s)
        s_hg.append(S_s.astype(state_hgrn.dtype))
        s_conv.append(hist_s.astype(state_conv.dtype))
    return (xp, xs, jnp.stack(p_lat), jnp.stack(p_rope), jnp.stack(p_hg), jnp.stack(p_conv), jnp.stack(p_mk),
            jnp.stack(p_mv), jnp.stack(s_lat), jnp.stack(s_rope), jnp.stack(s_hg), jnp.stack(s_conv))
```

```python
import numpy as np
from contextlib import ExitStack
import concourse.bass as bass
import concourse.mybir as mybir
from concourse.bass_utils import run_bass_kernel_spmd

F32 = mybir.dt.float32
BF16 = mybir.dt.bfloat16
I32 = mybir.dt.int32
AF = mybir.ActivationFunctionType
ALU = mybir.AluOpType
AX = mybir.AxisListType
ENGS = ['sync', 'scalar', 'vector', 'gpsimd', 'tensor']


class Prog:
    def __init__(self, nc, ndma=44):
        self.nc = nc
        self.q = {e: [] for e in ENGS}
        self.cnt = {e: 0 for e in ENGS}
        self.waited = {e: {} for e in ENGS}
        self.lastw = {}
        self.readers = {}
        self.ndma = ndma
        self.dma_val = [0] * ndma
        self.dma_i = 0
        self.es = None
        self.sems = {}
        self.all_dma_events = {}

    def sb(self, name, shape, dt):
        return self.es.enter_context(self.nc.sbuf_tensor(name, list(shape), dt))

    def ps(self, name, shape, dt):
        return self.es.enter_context(self.nc.psum_tensor(name, list(shape), dt))

    @staticmethod
    def _key(t):
        if isinstance(t, (str, tuple)):
            return t
        return t.name

    def _deps(self, eng, reads, writes):
        deps = []
        for k in reads:
            k = self._key(k)
            if k in self.lastw:
                deps.append(self.lastw[k])
        for k in writes:
            k = self._key(k)
            if k in self.lastw:
                deps.append(self.lastw[k])
            deps += self.readers.get(k, [])
        waits = {}
        for (s, v) in deps:
            if s == 'tensor' and eng == 'tensor':
                continue
            if self.waited[eng].get(s, 0) < v:
                waits[s] = max(waits.get(s, 0), v)
        for s, v in waits.items():
            self.waited[eng][s] = v
        return waits

    def _commit(self, ev, reads, writes):
        for k in reads:
            k = self._key(k)
            self.readers.setdefault(k, []).append(ev)
        for k in writes:
            k = self._key(k)
            self.lastw[k] = ev
            self.readers[k] = []

    def op(self, eng, fn, reads=(), writes=()):
        waits = self._deps(eng, reads, writes)
        self.cnt[eng] += 1
        ev = (eng, self.cnt[eng])
        self.q[eng].append((waits, fn, eng, 1))
        self._commit(ev, reads, writes)

    def dma(self, eng, out, in_, reads=(), writes=(), indirect=None, noncontig=False, eoff=0):
        waits = self._deps(eng, reads, writes)
        i = self.dma_i
        self.dma_i = (self.dma_i + 1) % self.ndma
        s = ('d', i)
        if self.dma_val[i] > 0 and self.waited[eng].get(s, 0) < self.dma_val[i]:
            waits[s] = self.dma_val[i]
            self.waited[eng][s] = self.dma_val[i]
        self.dma_val[i] += 16
        ev = (s, self.dma_val[i])
        self.all_dma_events[s] = self.dma_val[i]
        if indirect is not None:
            fn = lambda e: e.indirect_dma_start(out=out, out_offset=None, in_=in_, in_offset=indirect, element_offset=eoff)
        elif noncontig:
            fn = lambda e: e.dma_start(out=out, in_=in_, allow_slow_non_contiguous=True)
        else:
            fn = lambda e: e.dma_start(out=out, in_=in_)
        self.q[eng].append((waits, fn, s, 16))
        self._commit(ev, reads, writes)

    def barrier(self):
        for e in ENGS:
            w = {}
            for s_, v in self.all_dma_events.items():
                if self.waited[e].get(s_, 0) < v:
                    w[s_] = v
                    self.waited[e][s_] = v
            if w:
                self.q[e].append((w, None, None, 0))
        for e in ENGS:
            for o in ENGS:
                if o != e and self.cnt[o] > self.waited[e].get(o, 0):
                    self.q[e].append(({o: self.cnt[o]}, None, None, 0))
                    self.waited[e][o] = self.cnt[o]

    def finish(self):
        nc = self.nc
        fin = dict(self.all_dma_events)
        for e in ENGS:
            if e != 'sync' and self.cnt[e] > 0:
                fin[e] = self.cnt[e]
        self.q['sync'].append((fin, None, None, 0))
        for e in ENGS:
            self.sems[e] = self.es.enter_context(nc.semaphore("s_" + e))
        for i in range(self.ndma):
            self.sems[('d', i)] = self.es.enter_context(nc.semaphore("s_d%d" % i))
        block = self.es.enter_context(nc.Block())

        def replay(name):
            def f(eng):
                for (waits, fn, incs, incv) in self.q[name]:
                    for s, v in waits.items():
                        eng.wait_ge(self.sems[s], v)
                    if fn is not None:
                        ins = fn(eng)
                        ins.then_inc(self.sems[incs], incv)
            return f
        block.sync(replay('sync'))
        block.scalar(replay('scalar'))
        block.vector(replay('vector'))
        block.gpsimd(replay('gpsimd'))
        block.tensor(replay('tensor'))


D = 1024
SEQ = 2048
NT = 256
TPT = NT // 128
NTI = 2048 // NT
NS = 64
BRW = 512
EPS = 1e-6
O_CH, O_CB, O_CC, O_HQ, O_HF, O_HI, O_QL, O_KV, O_KPE, O_MQ, O_SG, O_MG = (
    0, 512, 1024, 1536, 2048, 2560, 3072, 3264, 3392, 3424, 3936, 5984)
NIN = 10080
NPHYS = 10240
PAST = 8192
C_ID = 0
C_TRI = 128
C_T64 = 256
C_BD = 320
C_PM = 384
C_IND = 416
C_SEGP = 432
C_SEGS = 944
C_E = 1008
C_SEL = 1104
C_ONE = 1136
C_Z = 1264
CW = 1552


def make_consts():
    c = np.zeros((128, CW), np.float32)
    c[:, C_ID:C_ID + 128] = np.eye(128)
    k = np.arange(128)
    c[:, C_TRI:C_TRI + 128] = (k[:, None] <= k[None, :])
    s = np.arange(64)
    c[:64, C_T64:C_T64 + 64] = (s[:, None] <= s[None, :])
    c[:64, C_BD:C_BD + 64] = (s[:, None] <= s[None, :]) & ((s[:, None] // 4) == (s[None, :] // 4))
    pm = np.zeros((128, 4, 2, 4), np.float32)
    pm[:64, :, 0, :] = 1
    pm[64:, :, 1, :] = 1
    c[:, C_PM:C_PM + 32] = pm.reshape(128, 32)
    c[:64, C_IND:C_IND + 16] = ((s[:, None] // 4) == np.arange(16)[None, :])
    c[:, C_SEGP:C_SEGP + 512] = (np.arange(512) % 64 != 0)[None, :]
    c[:, C_SEGS:C_SEGS + 64] = (np.arange(64) % 4 != 0)[None, :]
    for j in range(32):
        c[j, C_E + 64 + j] = 1
        c[64 + j, C_SEL + j] = 1
    c[:, C_ONE:C_ONE + 128] = 1
    for p_ in range(128):
        c[p_, C_Z + 96 + p_] = 1
    half = 16
    freqs = (10000.0 ** (-np.arange(half, dtype=np.float32) / half)).astype(np.float32)
    pos = np.concatenate([np.arange(SEQ), np.tile(PAST + np.arange(4), 16)]).astype(np.float32)
    ang = (pos[None, :] * freqs[:, None]).astype(np.float32)
    cos, sin = np.cos(ang).astype(np.float32), np.sin(ang).astype(np.float32)
    cc = np.concatenate([cos, cos], 0)
    ss = np.concatenate([-sin, sin], 0)
    r = np.zeros((128, 4, SEQ + NS), np.float32)
    r[:64, 0] = 1
    r[64:96, 0] = cc
    r[64:96, 1] = ss
    r[:32, 2] = cc
    r[:32, 3] = ss
    return c, r


CFG = {'nphys': NPHYS, 'stop': None, 'ncores': 8}


def build_nc():
    nc = bass.Bass("TRN2", target_bir_lowering=False)
    NPH = CFG['nphys']
    STOP = CFG['stop']

    def din(name, shape, dt=F32):
        return nc.dram_tensor(name, list(shape), dt, kind="ExternalInput").ap()

    def dout(name, shape):
        return nc.dram_tensor(name, list(shape), F32, kind="ExternalOutput").ap()

    xp = din("xp", [SEQ, D]); xs = din("xs", [NS, D]); memp = din("memp", [256, D])
    clats = [din("clat%d" % i, [NPH * 8, 2048]) for i in range(2)]; cropes = [din("crope%d" % i, [NPH * 8, 512]) for i in range(2)]
    ptab = din("ptab", [128, 8], I32)
    shg = din("shg", [2, 16, 4, 128, 128]); scv = din("scv", [2, 16, 2, 512])
    cmk = din("cmk", [2, 16, 256, 512]); cmv = din("cmv", [2, 16, 256, 512])
    w_in = din("w_in", [2, D, NIN]); w_bo = din("w_bo", [2, 4, 512, D]); w_o = din("w_o", [2, D, D])
    w_uq = din("w_uq", [2, 192, 384]); w_uk = din("w_uk", [2, 128, 256]); w_uv = din("w_uv", [2, 128, 512])
    mwk = din("mwk", [2, D, 512]); mwv = din("mwv", [2, D, 512])
    vecs = din("vecs", [128, 2, 48])
    cst_d = din("cst", [128, CW]); rope_d = din("rope", [128, 4, SEQ + NS])

    yp = dout("yp", [SEQ, D]); ysm = dout("ysm", [NS, D])
    o_latp = dout("o_latp", [2, SEQ, 128]); o_ropep = dout("o_ropep", [2, SEQ, 32])
    o_hgp = dout("o_hgp", [2, 4, 128, 128]); o_cvp = dout("o_cvp", [2, 2, 512])
    o_mkp = dout("o_mkp", [2, 256, 512]); o_mvp = dout("o_mvp", [2, 256, 512])
    o_lats = dout("o_lats", [2, NS, 128]); o_ropes = dout("o_ropes", [2, NS, 32])
    o_hgs = dout("o_hgs", [2, 16, 4, 128, 128]); o_cvs = dout("o_cvs", [2, 16, 2, 512])

    P = Prog(nc)
    with ExitStack() as es:
        P.es = es
        sb, ps = P.sb, P.ps
        xtok = sb("xtok", [128, 17, D], F32)
        cst = sb("cstt", [128, CW], F32)
        cbf = sb("cbf", [128, CW], BF16)
        ropet = sb("ropet", [128, 4, NT], F32)
        vec = sb("vec", [128, 2, 48], F32)
        lbv = sb("lbv", [128, 2, 8], F32)
        wsts = [sb("wstA", [128, 2048], F32), sb("wstB", [128, 2048], F32)]
        wbfs = [sb("wbf0", [128, 4096], BF16), sb("wbf1", [128, 4096], BF16)]
        hnT = sb("hnT", [128, 8, NT], BF16)
        ys = sb("ys", [128, 16, NT], BF16)
        merged = sb("merged", [128, 8, NT], F32)
        un = sb("un", [128, 10368], BF16)
        unf = un[:, :].bitcast(F32)

        class View:
            def __init__(self, ap, name):
                self.ap = ap
                self.name = name

            def __getitem__(self, k):
                return self.ap[k]
        KnAll = View(un[0:96, 0:8192].rearrange("p (h t) -> p h t", t=SEQ), "KnAll")
        ctokAll = View(un[:, 8192:8192 + 2176].rearrange("p (j r) -> p j r", r=136), "ctokAll")
        S = [sb("Sst%d" % i, [128, 128], F32) for i in range(4)]
        Sb = [sb("Sbf%d" % i, [128, 128], BF16) for i in range(4)]
        sc = [sb("sc%d" % i, [128, NT], F32) for i in range(10)]
        sh = [sb("sh%d" % i, [128, NT], BF16) for i in range(6)]
        uext = sb("uext", [128, 4, NT + 2], F32)
        vtok = sb("vtok", [64, 8, 128], BF16)
        ktok = sb("ktok", [64, 8, 128], BF16)
        aTm = sb("aTm", [64, 64], BF16)
        small = sb("small", [128, 64], F32)
        mkT = sb("mkT", [128, 4, 256], BF16)
        mvtok = sb("mvtok", [128, 2, 512], BF16)
        wq_b = sb("wq_b", [128, 2, 768], BF16)
        wuke = sb("wuke", [128, 4, 96], BF16)
        wuk_b = sb("wuk_b", [128, 256], BF16)
        wukT = sb("wukT", [64, 4, 128], BF16)
        wuv_b = sb("wuv_b", [128, 512], BF16)
        QnT = sb("QnT", [96, 4, NT], BF16)
        onT = sb("onT", [128, 4, NT], BF16)
        mqT = sb("mqT", [128, 4, NT], BF16)
        graw = View(unf[:, 0:2048], "graw")
        rraw = View(unf[:, 2048:2560], "rraw")
        gbf = View(un[:, 5120:5120 + 2176].rearrange("p (a r) -> p a r", r=136), "gbf")
        rbf = View(un[:, 7296:7296 + 512].rearrange("p (a r) -> p a r", r=32), "rbf")
        cTs = View(un[:, 7808:7808 + 512].rearrange("p (a r) -> p a r", r=128), "cTs")
        rTs = View(un[:, 8320:8320 + 128], "rTs")
        mraw = View(unf[:, 0:1024].rearrange("p (a r) -> p a r", r=512), "mraw")
        mkb = View(un[:, 2048:3072].rearrange("p (a r) -> p a r", r=512), "mkb")
        mvb = View(un[:, 3072:4096].rearrange("p (a r) -> p a r", r=512), "mvb")
        mkTs = View(un[:, 4096:5120].rearrange("p (a r) -> p a r", r=256), "mkTs")
        s0 = View(unf[:, 4416:4416 + 512].rearrange("p (a r) -> p a r", r=128), "s0")
        s0b = View(un[:, 9856:9856 + 512].rearrange("p (a r) -> p a r", r=128), "s0b")
        idx = sb("idx", [128, 8], I32)
        idx2 = sb("idx2", [128, 8, 8], I32)
        QaC = sb("QaC", [128, 8, 32], BF16)
        QaR = sb("QaR", [128, 8, 4, 32], BF16)
        QgT = sb("QgT", [96, 4, NS], BF16)
        pTn = sb("pTn", [64, 8, 32], BF16)
        cnew = sb("cnew", [64, 136], BF16)
        pTs = sb("pTs", [128, 4, 32], BF16)
        stat = sb("stat", [128, 64], F32)
        khm = sb("khm", [64, 128], BF16)
        pms = sb("pms", [128, 32], BF16)
        pb = [ps("pb%d" % i, [128, 512], F32) for i in range(6)]
        pbt = ps("pbt", [128, 1024], BF16)
        pb7 = ps("pb7", [128, 512], F32)

        def act(out, in_, func, r, w, scale=1.0, bias=None):
            if bias is None:
                P.op('scalar', lambda e: e.activation(out=out, in_=in_, func=func, scale=scale), r, w)
            else:
                P.op('scalar', lambda e: e.activation(out=out, in_=in_, func=func, scale=scale, bias=bias), r, w)

        def tt(eng, out, in0, in1, op, r, w):
            P.op(eng, lambda e: e.tensor_tensor(out=out, in0=in0, in1=in1, op=op), r, w)

        def ts(eng, out, in0, s1, s2, op0, op1, r, w):
            if s2 is None:
                P.op(eng, lambda e: e.tensor_scalar(out=out, in0=in0, scalar1=s1, scalar2=None, op0=op0), r, w)
            else:
                P.op(eng, lambda e: e.tensor_scalar(out=out, in0=in0, scalar1=s1, scalar2=s2, op0=op0, op1=op1), r, w)

        def stt(out, in0, scalar, in1, op0, op1, r, w):
            P.op('vector', lambda e: e.scalar_tensor_tensor(out=out, in0=in0, scalar=scalar, in1=in1, op0=op0, op1=op1), r, w)

        def cp(eng, out, in_, r, w):
            P.op(eng, lambda e: e.tensor_copy(out=out, in_=in_), r, w)

        def mm(out, lhsT, rhs, start, stop, r, w):
            P.op('tensor', lambda e: e.matmul(out, lhsT=lhsT, rhs=rhs, start=start, stop=stop), r, w)

        def tr(out, in_, ident, r, w):
            P.op('tensor', lambda e: e.transpose(out=out, in_=in_, identity=ident), r, w)

        def recip(out, in_, r, w):
            P.op('vector', lambda e: e.reciprocal(out=out, in_=in_), r, w)

        def rsqrt_to(out, in_, n, scale, r, w):
            act(out, in_, AF.Ln, r, w, scale=scale, bias=vec[0:n, 0, 47:48])
            act(out, out, AF.Exp, w, w, scale=-0.5)

        identb = lambda n: cbf[0:n, C_ID:C_ID + n]
        identf = lambda n: cst[0:n, C_ID:C_ID + n]
        onesb = lambda k, m: cbf[0:k, C_ONE:C_ONE + m]

        def V(l, j, n=128):
            return vec[0:n, l, j:j + 1]

        P.dma('sync', cst[:], cst_d, writes=[cst])
        P.dma('sync', vec[:], vecs, writes=[vec])
        P.dma('sync', idx[:], ptab, writes=[idx])
        cp('vector', cbf[:], cst[:], [cst], [cbf])
        for ch in range(8):
            ts('vector', idx2[:, :, ch], idx[:], 8.0, float(ch), ALU.mult, ALU.add, [idx], [idx2])
        pass
        P.op('gpsimd', lambda e: e.memset(cnew[:, 128:136], 1.0), (), [cnew])
        pass
        for t in range(16):
            P.dma('sync', xtok[:, t, :], xp[t * 128:(t + 1) * 128, :], writes=[('x', t)])
        P.dma('sync', xtok[0:64, 16, :], xs, writes=[('x', 16)])
        junk = sb("junk", [128, D], BF16)
        hnb = sb("hnb", [128, D], BF16)
        P.op('gpsimd', lambda e: e.memset(lbv[:], 0.0), (), [lbv])
        tt('vector', lbv[:, 1, 0:4], vec[:, 0, 32:36], vec[:, 0, 28:32], ALU.subtract, [vec], [lbv])
        act(lbv[:, 1, 0:4], lbv[:, 1, 0:4], AF.Sigmoid, [lbv], [lbv])
        ts('vector', lbv[:, :, 4:8], lbv[:, :, 0:4], -1.0, 1.0, ALU.mult, ALU.add, [lbv], [lbv])
        P.op('gpsimd', lambda e: e.memset(wq_b[:], 0.0), (), [wq_b])
        P.op('gpsimd', lambda e: e.memset(wuke[:], 0.0), (), [wuke])

        wctr = [0]

        def WSTK(k):
            return [('wst', k, i) for i in range(8)]

        def wk(wb):
            return [(wb.name, 0), (wb.name, 1)]

        def stage(parts, nk, W):
            k = wctr[0] % 2
            wctr[0] += 1
            wb = wbfs[k]
            hk = nk // 2
            for hf in range(2):
                wsv = wsts[hf][:, 0:hk * W].rearrange("p (k n) -> p k n", n=W)
                for i, (off, n, src) in enumerate(parts):
                    P.dma('sync', wsv[:, :, off:off + n], src[:, hf * hk:(hf + 1) * hk, :], writes=[('wst', hf, i)])
            act(wb[:, 0:hk * W], wsts[0][:, 0:hk * W], AF.Copy, WSTK(0), [(wb.name, 0)])
            cp('vector', wb[:, hk * W:nk * W], wsts[1][:, 0:hk * W], WSTK(1), [(wb.name, 1)])
            return wb[:, 0:nk * W].rearrange("p (k n) -> p k n", n=W), wk(wb)

        def stage_rows(src2d, nk, ncols):
            return stage([(0, ncols, src2d.rearrange("(k p) n -> p k n", p=128))], nk, ncols)

        def stage_win(l, segs):
            W = sum(n for _, n in segs)
            src = w_in[l].rearrange("(k p) n -> p k n", p=128)
            parts = []
            off = 0
            for (c0, n) in segs:
                parts.append((off, n, src[:, :, c0:c0 + n]))
                off += n
            return stage(parts, 8, W)

        def make_hnT(xin, n, j, l, gcol, xkeys):
            act(junk[0:n, :], xin, AF.Square, xkeys, [junk, 'ssq'], scale=1.0)
            P.op('vector', lambda e: e.tensor_reduce(out=small[0:n, 0:1], in_=junk[0:n, :], axis=AX.X, op=ALU.add), [junk], ['ssq'])
            rsqrt_to(small[0:n, 0:1], small[0:n, 0:1], n, 1.0 / D, ['ssq'], ['ssq'])
            ts('vector', hnb[0:n, :], xin, small[0:n, 0:1], None, ALU.mult, None, xkeys + ['ssq'], [hnb])
            for kc in range(8):
                tr(pbt[:, kc * 128:kc * 128 + n], hnb[0:n, kc * 128:(kc + 1) * 128], identb(n), [hnb], [pbt])
            for kc in range(8):
                o_ = hnT[:, kc, j * 128:j * 128 + n]
                i_ = pbt[:, kc * 128:kc * 128 + n]
                if kc % 2 == 0:
                    ts('vector', o_, i_, V(l, gcol + kc), None, ALU.mult, None, [pbt], [hnT])
                else:
                    act(o_, i_, AF.Copy, [pbt], [hnT], scale=V(l, gcol + kc))

        def load_small(l):
            k = 0
            wst = wsts[0]
            WST = WSTK(0)
            parts = [(wst[:, 0:384], w_uq[l, 0:128, :]), (wst[0:64, 384:768], w_uq[l, 128:192, :]),
                     (wst[:, 768:1024], w_uk[l]), (wst[:, 1024:1536], w_uv[l])]
            for i, (d_, s_) in enumerate(parts):
                P.dma('sync', d_, s_, writes=[('wst', k, i)])
            cp('vector', wq_b[:, 0, 0:384], wst[:, 0:384], WST, [wq_b])
            cp('vector', wq_b[0:64, 1, 0:384], wst[0:64, 384:768], WST, [wq_b])
            for kc, n in ((0, 128), (1, 64)):
                sv = wst[0:n, kc * 384:(kc + 1) * 384].rearrange("p (h d) -> p h d", d=96)
                dv = wq_b[0:n, kc, 384:768].rearrange("p (h d) -> p h d", d=96)
                cp('vector', dv[:, :, 64:80], sv[:, :, 80:96], WST, [wq_b])
                cp('vector', dv[:, :, 80:96], sv[:, :, 64:80], WST, [wq_b])
            cp('vector', wuk_b[:], wst[:, 768:1024], WST, [wuk_b])
            cp('vector', wuke[:, :, 0:64], wst[:, 768:1024].rearrange("p (h d) -> p h d", d=64), WST, [wuke])
            cp('vector', wuv_b[:], wst[:, 1024:1536], WST, [wuv_b])
            for h in range(4):
                tr(pbt[0:64, h * 128:(h + 1) * 128], wuk_b[:, h * 64:(h + 1) * 64], identb(128), [wuk_b], [pbt])
            cp('vector', wukT[:], pbt[0:64, 0:512].rearrange("p (h r) -> p h r", r=128), [pbt], [wukT])

        def mem_kv(l):
            mt = merged[:, 0:4, :].rearrange("p a b -> p (a b)")
            for j in range(2):
                P.dma('sync', mt, memp[j * 128:(j + 1) * 128, :], writes=[merged])
                make_hnT(mt, 128, j, l, 8, [merged])
            wv, wkeys = stage_rows(mwk[l], 8, 512)
            for h in range(4):
                for kc in range(8):
                    mm(pb[0][:, 0:256], wv[:, kc, h * 128:(h + 1) * 128], hnT[:, kc, 0:256], kc == 0, kc == 7, wkeys + [hnT], [pb[0]])
                act(sh[0][:, 0:256], pb[0][:, 0:256], AF.Square, [pb[0]], [sh[0]])
                mm(pb7[:, 0:256], onesb(128, 128), sh[0][:, 0:256], True, True, [sh[0]], [pb7])
                rsqrt_to(sc[0][:, 0:256], pb7[:, 0:256], 128, 1.0 / 128, [pb7], [sc[0]])
                stt(sc[1][:, 0:256], pb[0][:, 0:256], V(l, 46), sc[0][:, 0:256], ALU.mult, ALU.mult, [pb[0], sc[0]], [sc[1]])
                cp('gpsimd', mkT[:, h, :], sc[1][:, 0:256], [sc[1]], [mkT])
                for mc in range(2):
                    tr(pb[1][:, mc * 128:(mc + 1) * 128], sc[1][:, mc * 128:(mc + 1) * 128], identf(128), [sc[1]], [pb[1]])
                cp('vector', sc[2][:, 0:256], pb[1][:, 0:256], [pb[1]], [sc[2]])
                for mc in range(2):
                    P.dma('gpsimd', o_mkp[l, mc * 128:(mc + 1) * 128, h * 128:(h + 1) * 128], sc[2][:, mc * 128:(mc + 1) * 128], reads=[sc[2]])
            wv, wkeys = stage_rows(mwv[l], 8, 512)
            for mc in range(2):
                for kc in range(8):
                    mm(pb[0][:, :], hnT[:, kc, mc * 128:(mc + 1) * 128], wv[:, kc, :], kc == 0, kc == 7, wkeys + [hnT], [pb[0]])
                for hf in range(2):
                    cp('vector', sc[3 + hf][:, :], pb[0][:, hf * 256:(hf + 1) * 256], [pb[0]], [sc[3 + hf]])
                    cp('gpsimd', mvtok[:, mc, hf * 256:(hf + 1) * 256], sc[3 + hf][:, :], [sc[3 + hf]], [mvtok])
                    P.dma('gpsimd', o_mvp[l, mc * 128:(mc + 1) * 128, hf * 256:(hf + 1) * 256], sc[3 + hf][:, :], reads=[sc[3 + hf]])
        KnS = sb("KnS", [96, 4, NS], BF16)
        MLA_SCALE = 96 ** -0.5
        MEM_SCALE = 128 ** -0.5

        def tile_pass(l, kind, ti):
            N = NT if kind == 'p' else NS
            smp = kind == 's'
            p0 = ti * NT if not smp else SEQ
            P.dma('sync', ropet[:, :, 0:N], rope_d[:, :, p0:p0 + N], writes=[ropet])
            if not smp:
                subt = [(ti * TPT + j, 128) for j in range(TPT)]
            else:
                subt = [(16, 64)]
            for j, (t, n) in enumerate(subt):
                make_hnT(xtok[0:n, t, :], n, j, l, 0, [('x', t)])
            zb = [0]

            def zchunk(wv, wkeys, off, m):
                p_ = pb[zb[0] % 2]
                zb[0] += 1
                for kc in range(8):
                    mm(p_[0:m, 0:N], wv[:, kc, off:off + m], hnT[:, kc, 0:N], kc == 0, kc == 7, wkeys + [hnT], [p_])
                return p_

            def v3(ap):
                return ap.rearrange("p (b s) -> p b s", s=4)

            for c in range(4):
                wv, wkeys = stage_win(l, [(O_CH + c * 128, 128), (O_CB + c * 128, 128), (O_CC + c * 128, 128), (O_SG + c * 128, 128)])
                ph = zchunk(wv, wkeys, 0, 128)
                act(sc[0][:, 0:N], ph[:, 0:N], AF.Copy, [ph], [sc[0]])
                pc_ = zchunk(wv, wkeys, 256, 128)
                ukey = ('u', c)
                if not smp:
                    if ti == 0:
                        P.op('gpsimd', lambda e, c=c: e.memset(uext[:, c, 0:2], 0.0), (), [ukey])
                    tt('vector', uext[:, c, 2:2 + N], pc_[:, 0:N], sc[0][:, 0:N], ALU.mult, [pc_, sc[0]], [ukey])
                    u0, u1, u2 = uext[:, c, 0:N], uext[:, c, 1:N + 1], uext[:, c, 2:N + 2]
                    cv, yv = sc[1][:, 0:N], sc[2][:, 0:N]
                else:
                    uv = uext[:, c, 0:96].rearrange("p (b s) -> p b s", s=6)
                    if c == 0:
                        mh = merged[0:32, 0:2, :].rearrange("p a b -> p (a b)")
                        mh2 = merged[0:32, 2:4, :].rearrange("p a b -> p (a b)")
                        P.dma('sync', mh, scv[l].rearrange("b j c -> (b j) c"), writes=[('mh', 0)])
                    tr(pb[2][:, 0:32], mh[:, c * 128:(c + 1) * 128], identf(32), [('mh', 0)], [pb[2]])
                    cp('vector', uv[:, :, 0:2], pb[2][:, 0:32].rearrange("p (b j) -> p b j", j=2), [pb[2]], [('uh', c, 0)])
                    tt('vector', uv[:, :, 2:6], v3(pc_[:, 0:N]), v3(sc[0][:, 0:N]), ALU.mult, [pc_, sc[0], ('uh', c, 0)], [ukey])
                    u0, u1, u2 = uv[:, :, 0:4], uv[:, :, 1:5], uv[:, :, 2:6]
                    cv, yv = v3(sc[1][:, 0:N]), v3(sc[2][:, 0:N])
                ts('vector', cv, u0, V(l, 16 + c), None, ALU.mult, None, [ukey], [sc[1]])
                stt(cv, u1, V(l, 20 + c), cv, ALU.mult, ALU.add, [ukey, sc[1]], [sc[1]])
                stt(cv, u2, V(l, 24 + c), cv, ALU.mult, ALU.add, [ukey, sc[1]], [sc[1]])
                pb_ = zchunk(wv, wkeys, 128, 128)
                tt('vector', sc[2][:, 0:N], pb_[:, 0:N], sc[1][:, 0:N], ALU.mult, [pb_, sc[1]], [sc[2]])
                psg = zchunk(wv, wkeys, 384, 128)
                act(sc[3][:, 0:N], psg[:, 0:N], AF.Silu, [psg], [sc[3]])
                tt('gpsimd', ys[:, c, 0:N], sc[2][:, 0:N], sc[3][:, 0:N], ALU.mult, [sc[2], sc[3]], [ys])
                if not smp:
                    if ti == NTI - 1:
                        P.dma('gpsimd', o_cvp[l][:, c * 128:(c + 1) * 128].rearrange("j p -> p j"), uext[:, c, N:N + 2],
                              reads=[ukey], noncontig=True)
                    else:
                        cp('gpsimd', uext[:, c, 0:2], uext[:, c, N:N + 2], [ukey], [ukey])
                else:
                    cp('vector', sc[8][:, 0:32].rearrange("p (b j) -> p b j", j=2), uv[:, :, 4:6], [ukey], [sc[8]])
                    tr(pb[3][0:32, c * 128:(c + 1) * 128], sc[8][:, 0:32], identf(128), [sc[8]], [pb[3]])
                    if c == 3:
                        cp('vector', mh2, pb[3][0:32, 0:512], [pb[3]], [('mh', 1)])
                        P.dma('gpsimd', o_cvs[l].rearrange("b j c -> (b j) c"), mh2, reads=[('mh', 1)])

            if smp:
                chk('s%d_conv' % l)
            for h in range(4):
                wv, wkeys = stage_win(l, [(O_HQ + h * 128, 128), (O_HF + h * 128, 128), (O_HI + h * 128, 128), (O_SG + 512 + h * 128, 128)])
                pq = zchunk(wv, wkeys, 0, 128)
                act(sc[0][:, 0:N], pq[:, 0:N], AF.Copy, [pq], [sc[0]], scale=128 ** -0.5)
                pf = zchunk(wv, wkeys, 128, 128)
                act(sc[1][:, 0:N], pf[:, 0:N], AF.Sigmoid, [pf], [sc[1]])
                ts('vector', sc[1][:, 0:N], sc[1][:, 0:N], lbv[:, l, 4 + h:5 + h], lbv[:, l, h:h + 1], ALU.mult, ALU.add, [sc[1], lbv], [sc[1]])
                act(sc[2][:, 0:N], sc[1][:, 0:N], AF.Ln, [sc[1]], [sc[2]])
                ts('gpsimd', sc[1][:, 0:N], sc[1][:, 0:N], -1.0, 1.0, ALU.mult, ALU.add, [sc[1], sc[2]], [sc[1]])
                segm = cst[:, C_SEGP:C_SEGP + N] if not smp else cst[:, C_SEGS:C_SEGS + N]
                P.op('vector', lambda e, segm=segm: e.tensor_tensor_scan(out=sc[3][:, 0:N], data0=segm, data1=sc[2][:, 0:N], initial=0.0,
                                                                         op0=ALU.mult, op1=ALU.add), [sc[2]], [sc[3]])
                act(sc[4][:, 0:N], sc[3][:, 0:N], AF.Exp, [sc[3]], [sc[4]])
                if not smp:
                    bv = sc[3][:, 0:N].rearrange("p (c s) -> p c s", s=64)
                    dv = sc[2][:, 0:N].rearrange("p (c s) -> p c s", s=64)
                    tt('vector', dv, bv, bv[:, :, 31:32].broadcast_to([128, N // 64, 64]), ALU.subtract, [sc[3]], [sc[2]])
                    act(sc[5][:, 0:N], sc[2][:, 0:N], AF.Exp, [sc[2]], [sc[5]])
                    act(sc[6][:, 0:N], sc[2][:, 0:N], AF.Exp, [sc[2]], [sc[6]], scale=-1.0)
                    E1 = sc[5]
                else:
                    E1 = sc[4]
                    act(sc[6][:, 0:N], sc[3][:, 0:N], AF.Exp, [sc[3]], [sc[6]], scale=-1.0)
                    bv = v3(sc[3][:, 0:N])
                    tt('vector', v3(sc[2][:, 0:N]), bv[:, :, 3:4].broadcast_to([128, 16, 4]), bv, ALU.subtract, [sc[3]], [sc[2]])
                    act(sc[5][:, 0:N], sc[2][:, 0:N], AF.Exp, [sc[2]], [sc[5]])
                    tt('gpsimd', sh[3][:, 0:N], sc[1][:, 0:N], sc[5][:, 0:N], ALU.mult, [sc[1], sc[5]], [sh[3]])
                tt('vector', sh[0][:, 0:N], sc[0][:, 0:N], E1[:, 0:N], ALU.mult, [sc[0], E1], [sh[0]])
                tt('gpsimd', sh[1][:, 0:N], sc[0][:, 0:N], sc[4][:, 0:N], ALU.mult, [sc[0], sc[4]], [sh[1]])
                tt('vector', sh[2][:, 0:N], sc[1][:, 0:N], sc[6][:, 0:N], ALU.mult, [sc[1], sc[6]], [sh[2]])
                nch = N // 64
                for half in range((nch + 3) // 4):
                    ncc = min(4, nch - half * 4)
                    for cc in range(ncc):
                        c_ = half * 4 + cc
                        for kc in range(8):
                            mm(pb[2][0:64, cc * 128:(cc + 1) * 128], hnT[:, kc, c_ * 64:(c_ + 1) * 64], wv[:, kc, 256:384], kc == 0, kc == 7,
                               wkeys + [hnT], [pb[2]])
                    act(vtok[:, half * 4:half * 4 + ncc, :], pb[2][0:64, 0:ncc * 128].rearrange("p (c v) -> p c v", v=128), AF.Copy, [pb[2]], [vtok])
                ksrc = sh[2] if not smp else sh[3]
                for c_ in range(nch):
                    tr(pbt[0:64, c_ * 128:(c_ + 1) * 128], ksrc[:, c_ * 64:(c_ + 1) * 64], identb(128), [ksrc], [pbt])
                cp('vector', ktok[:, 0:nch, :], pbt[0:64, 0:nch * 128].rearrange("p (c v) -> p c v", v=128), [pbt], [ktok])
                oT = pb[4]
                if not smp:
                    for c_ in range(N // 64):
                        first = (ti == 0 and c_ == 0)
                        cs_ = slice(c_ * 64, (c_ + 1) * 64)
                        mm(pb[3][0:64, 0:64], sh[2][:, cs_], sh[0][:, cs_], True, True, [sh[2], sh[0]], [pb[3]])
                        tt('vector', aTm[:, :], pb[3][0:64, 0:64], cst[0:64, C_T64:C_T64 + 64], ALU.mult, [pb[3]], [aTm])
                        if not first:
                            mm(oT[:, cs_], Sb[h][:, :], sh[1][:, cs_], True, False, [Sb[h], sh[1]], [oT])
                        mm(oT[:, cs_], vtok[:, c_, :], aTm[:, :], first, True, [vtok, aTm], [oT])
                        mm(pb[5][:, 0:128], ktok[:, c_, :], vtok[:, c_, :], True, True, [ktok, vtok], [pb[5]])
                        eL = sc[4][:, c_ * 64 + 63:c_ * 64 + 64]
                        eLm = E1[:, c_ * 64 + 63:c_ * 64 + 64]
                        if first:
                            ts('vector', S[h][:, :], pb[5][:, 0:128], eLm, None, ALU.mult, None, [pb[5], E1], [S[h]])
                        else:
                            ts('gpsimd', S[h][:, :], S[h][:, :], eL, None, ALU.mult, None, [S[h], sc[4]], [S[h]])
                            stt(S[h][:, :], pb[5][:, 0:128], eLm, S[h][:, :], ALU.mult, ALU.add, [pb[5], E1, S[h]], [S[h]])
                        act(Sb[h][:, :], S[h][:, :], AF.Copy, [S[h]], [Sb[h]])
                    if ti == NTI - 1:
                        P.dma('gpsimd', o_hgp[l, h], S[h][:, :], reads=[S[h]])
                else:
                    mm(pb[3][0:64, 0:64], sh[2][:, 0:64], sh[0][:, 0:64], True, True, [sh[2], sh[0]], [pb[3]])
                    tt('vector', aTm[:, :], pb[3][0:64, 0:64], cst[0:64, C_BD:C_BD + 64], ALU.mult, [pb[3]], [aTm])
                    mm(oT[:, 0:64], vtok[:, 0, :], aTm[:, :], True, False, [vtok, aTm], [oT])
                    for b in range(16):
                        slot = b % 4
                        k0, k0b = ('s0', slot), ('s0b', slot)
                        P.dma('sync', s0[:, slot, :], shg[l, b, h], writes=[k0])
                        cp('gpsimd', s0b[:, slot, :], s0[:, slot, :], [k0], [k0b])
                        mm(oT[:, 4 * b:4 * b + 4], s0b[:, slot, :], sh[1][:, 4 * b:4 * b + 4], False, b == 15, [k0b, sh[1]], [oT])
                        ts('gpsimd', khm[:, :], ktok[:, 0, :], cst[0:64, C_IND + b:C_IND + b + 1], None, ALU.mult, None, [ktok], [khm])
                        mm(pb[5][:, 0:128], khm[:, :], vtok[:, 0, :], True, True, [khm, vtok], [pb[5]])
                        stt(s0[:, slot, :], s0[:, slot, :], sc[4][:, 4 * b + 3:4 * b + 4], pb[5][:, 0:128], ALU.mult, ALU.add,
                            [k0, sc[4], pb[5]], [k0])
                        P.dma('gpsimd', o_hgs[l, b, h], s0[:, slot, :], reads=[k0])
                act(sc[7][:, 0:N], oT[:, 0:N], AF.Copy, [oT], [sc[7]])
                act(sh[4][:, 0:N], oT[:, 0:N], AF.Square, [oT], [sh[4]])
                mm(pb7[:, 0:N], onesb(128, 128), sh[4][:, 0:N], True, True, [sh[4]], [pb7])
                rsqrt_to(sc[8][:, 0:N], pb7[:, 0:N], 128, 1.0 / 128, [pb7], [sc[8]])
                stt(sc[9][:, 0:N], sc[7][:, 0:N], V(l, 36 + h), sc[8][:, 0:N], ALU.mult, ALU.mult, [sc[7], sc[8]], [sc[9]])
                psg = zchunk(wv, wkeys, 384, 128)
                act(sc[7][:, 0:N], psg[:, 0:N], AF.Silu, [psg], [sc[7]])
                tt('gpsimd', ys[:, 4 + h, 0:N], sc[9][:, 0:N], sc[7][:, 0:N], ALU.mult, [sc[9], sc[7]], [ys])

            if smp:
                chk('s%d_hgrn' % l)
            wv, wkeys = stage_win(l, [(O_QL, 192), (O_KV, 128), (O_KPE, 32), (O_KPE + 16, 16), (O_KPE, 16)])
            p_ = zchunk(wv, wkeys, 0, 128)
            act(sc[0][:, 0:N], p_[:, 0:N], AF.Copy, [p_], [sc[0]])
            act(sh[0][:, 0:N], p_[:, 0:N], AF.Square, [p_], [sh[0]])
            p_ = zchunk(wv, wkeys, 128, 64)
            act(sc[1][0:64, 0:N], p_[0:64, 0:N], AF.Copy, [p_], [sc[1]])
            act(sh[1][0:64, 0:N], p_[0:64, 0:N], AF.Square, [p_], [sh[1]])
            mm(pb7[:, 0:N], onesb(128, 128), sh[0][:, 0:N], True, False, [sh[0]], [pb7])
            mm(pb7[:, 0:N], onesb(64, 128), sh[1][0:64, 0:N], False, True, [sh[1]], [pb7])
            rsqrt_to(sc[2][:, 0:N], pb7[:, 0:N], 128, 1.0 / 192, [pb7], [sc[2]])
            stt(sh[0][:, 0:N], sc[0][:, 0:N], V(l, 40), sc[2][:, 0:N], ALU.mult, ALU.mult, [sc[0], sc[2]], [sh[0]])
            stt(sh[1][0:64, 0:N], sc[1][0:64, 0:N], V(l, 41, 64), sc[2][0:64, 0:N], ALU.mult, ALU.mult, [sc[1], sc[2]], [sh[1]])
            for h in range(4):
                for (dst, woff) in ((pb[2], 0), (pb[3], 384)):
                    mm(dst[0:96, 0:N], wq_b[:, 0, woff + h * 96:woff + (h + 1) * 96], sh[0][:, 0:N], True, False, [wq_b, sh[0]], [dst])
                    mm(dst[0:96, 0:N], wq_b[0:64, 1, woff + h * 96:woff + (h + 1) * 96], sh[1][0:64, 0:N], False, True, [wq_b, sh[1]], [dst])
                tt('vector', sc[3][0:96, 0:N], pb[2][0:96, 0:N], ropet[0:96, 0, 0:N], ALU.mult, [pb[2], ropet], [sc[3]])
                tt('vector', sc[4][0:96, 0:N], pb[3][0:96, 0:N], ropet[0:96, 1, 0:N], ALU.mult, [pb[3], ropet], [sc[4]])
                tt('gpsimd', sc[3][0:96, 0:N], sc[3][0:96, 0:N], sc[4][0:96, 0:N], ALU.add, [sc[3], sc[4]], [sc[3]])
                act(sh[2][0:96, 0:N], sc[3][0:96, 0:N], AF.Square, [sc[3]], [sh[2]])
                mm(pb7[0:96, 0:N], onesb(96, 96), sh[2][0:96, 0:N], True, True, [sh[2]], [pb7])
                rsqrt_to(sc[4][0:96, 0:N], pb7[0:96, 0:N], 96, 1.0 / 96, [pb7], [sc[4]])
                stt(QnT[:, h, 0:N], sc[3][0:96, 0:N], V(l, 43, 96), sc[4][0:96, 0:N], ALU.mult, ALU.mult, [sc[3], sc[4]], [QnT])
            p_ = zchunk(wv, wkeys, 192, 128)
            act(sc[0][:, 0:N], p_[:, 0:N], AF.Copy, [p_], [sc[0]])
            act(sh[2][:, 0:N], p_[:, 0:N], AF.Square, [p_], [sh[2]])
            mm(pb7[:, 0:N], onesb(128, 128), sh[2][:, 0:N], True, True, [sh[2]], [pb7])
            rsqrt_to(sc[1][:, 0:N], pb7[:, 0:N], 128, 1.0 / 128, [pb7], [sc[1]])
            stt(sc[2][:, 0:N], sc[0][:, 0:N], V(l, 42), sc[1][:, 0:N], ALU.mult, ALU.mult, [sc[0], sc[1]], [sc[2]])
            cp('gpsimd', sh[3][:, 0:N], sc[2][:, 0:N], [sc[2]], [sh[3]])
            nsub = len(subt)
            nn = subt[0][1]
            for j in range(nsub):
                tr(pb[2][0:nn, j * 128:(j + 1) * 128], sc[2][:, j * 128:j * 128 + nn], identf(128), [sc[2]], [pb[2]])
            cp('vector', sc[5][0:nn, 0:nsub * 128], pb[2][0:nn, 0:nsub * 128], [pb[2]], [sc[5]])
            c3 = sc[5][0:nn, 0:nsub * 128].rearrange("p (j r) -> p j r", r=128)
            if not smp:
                P.dma('gpsimd', o_latp[l, p0:p0 + N, :].rearrange("(j p) r -> p j r", p=128), c3, reads=[sc[5]])
                cp('gpsimd', ctokAll[:, ti * TPT:ti * TPT + TPT, 0:128], c3, [sc[5]], [ctokAll])
            else:
                P.dma('gpsimd', o_lats[l], sc[5][0:64, 0:128], reads=[sc[5]])
                cp('gpsimd', cnew[:, 0:128], sc[5][0:64, 0:128], [sc[5]], [cnew])
            pk = zchunk(wv, wkeys, 320, 32)
            pks = zchunk(wv, wkeys, 352, 32)
            tt('vector', sc[6][0:32, 0:N], pk[0:32, 0:N], ropet[0:32, 2, 0:N], ALU.mult, [pk, ropet], [sc[6]])
            tt('vector', sc[7][0:32, 0:N], pks[0:32, 0:N], ropet[0:32, 3, 0:N], ALU.mult, [pks, ropet], [sc[7]])
            tt('gpsimd', sc[6][0:32, 0:N], sc[6][0:32, 0:N], sc[7][0:32, 0:N], ALU.add, [sc[6], sc[7]], [sc[6]])
            cp('gpsimd', sh[4][0:32, 0:N], sc[6][0:32, 0:N], [sc[6]], [sh[4]])
            for j in range(nsub):
                tr(pb[3][0:nn, j * 32:(j + 1) * 32], sc[6][0:32, j * 128:j * 128 + nn], identf(32), [sc[6]], [pb[3]])
            cp('vector', sc[7][0:nn, 0:nsub * 32], pb[3][0:nn, 0:nsub * 32], [pb[3]], [sc[7]])
            if not smp:
                P.dma('gpsimd', o_ropep[l, p0:p0 + N, :].rearrange("(j p) r -> p j r", p=128),
                      sc[7][:, 0:TPT * 32].rearrange("p (j r) -> p j r", r=32), reads=[sc[7]])
            else:
                P.dma('gpsimd', o_ropes[l], sc[7][0:64, 0:32], reads=[sc[7]])
            for h in range(4):
                mm(pb[2][0:96, 0:N], wuke[:, h, :], sh[3][:, 0:N], True, False, [wuke, sh[3]], [pb[2]])
                mm(pb[2][0:96, 0:N], cbf[0:32, C_E:C_E + 96], sh[4][0:32, 0:N], False, True, [sh[4]], [pb[2]])
                act(sh[5][0:96, 0:N], pb[2][0:96, 0:N], AF.Square, [pb[2]], [sh[5]])
                mm(pb7[0:96, 0:N], onesb(96, 96), sh[5][0:96, 0:N], True, True, [sh[5]], [pb7])
                rsqrt_to(sc[8][0:96, 0:N], pb7[0:96, 0:N], 96, 1.0 / 96, [pb7], [sc[8]])
                dstK = KnAll[:, h, p0:p0 + N] if not smp else KnS[:, h, :]
                stt(dstK, pb[2][0:96, 0:N], V(l, 44, 96), sc[8][0:96, 0:N], ALU.mult, ALU.mult, [pb[2], sc[8]], [KnAll if not smp else KnS])
            if not smp:
                nk = TPT * ti + TPT
                for h in range(4):
                    for j in range(nk):
                        lo = max(0, j - TPT * ti) * 128
                        spb = pb[2 + (j % 2)]
                        pT = sh[j % 2]
                        mm(spb[:, lo:N], KnAll[:, h, j * 128:(j + 1) * 128], QnT[:, h, lo:N], True, True, [KnAll, QnT], [spb])
                        act(pT[:, lo:N], spb[:, lo:N], AF.Exp, [spb], [pT], scale=MLA_SCALE)
                        if j >= TPT * ti:
                            tt('gpsimd', pT[:, lo:lo + 128], pT[:, lo:lo + 128], cbf[:, C_TRI:C_TRI + 128], ALU.mult, [pT], [pT])
                        mm(pb[4][:, lo:N], ctokAll[:, j, 0:128], pT[:, lo:N], j == 0, j == nk - 1, [ctokAll, pT], [pb[4]])
                        mm(pb[5][:, lo:N], onesb(128, 128), pT[:, lo:N], j == 0, j == nk - 1, [pT], [pb[5]])
                    recip(sc[9][:, 0:N], pb[5][:, 0:N], [pb[5]], [sc[9]])
                    tt('vector', onT[:, h, 0:N], pb[4][:, 0:N], sc[9][:, 0:N], ALU.mult, [pb[4], sc[9]], [onT])
            else:
                chk('s%d_mla0' % l)
                P.barrier()
                mla_sample(l)
                P.barrier()
                chk('s%d_mla1' % l)
            for h in range(4):
                mm(pb[2][:, 0:N], wuv_b[:, h * 128:(h + 1) * 128], onT[:, h, 0:N], True, True, [wuv_b, onT], [pb[2]])
                act(sc[h][:, 0:N], pb[2][:, 0:N], AF.Copy, [pb[2]], [sc[h]])
            wv, wkeys = stage_win(l, [(O_SG + 1024, 512)])
            for h in range(4):
                psg = zchunk(wv, wkeys, h * 128, 128)
                act(sc[4 + h % 2][:, 0:N], psg[:, 0:N], AF.Silu, [psg], [sc[4 + h % 2]])
                tt('gpsimd', ys[:, 8 + h, 0:N], sc[h][:, 0:N], sc[4 + h % 2][:, 0:N], ALU.mult, [sc[h], sc[4 + h % 2]], [ys])

            wv, wkeys = stage_win(l, [(O_MQ, 512)])
            for h in range(4):
                p_ = zchunk(wv, wkeys, h * 128, 128)
                act(sc[0][:, 0:N], p_[:, 0:N], AF.Copy, [p_], [sc[0]])
                act(sh[0][:, 0:N], p_[:, 0:N], AF.Square, [p_], [sh[0]])
                mm(pb7[:, 0:N], onesb(128, 128), sh[0][:, 0:N], True, True, [sh[0]], [pb7])
                rsqrt_to(sc[1][:, 0:N], pb7[:, 0:N], 128, 1.0 / 128, [pb7], [sc[1]])
                stt(mqT[:, h, 0:N], sc[0][:, 0:N], V(l, 45), sc[1][:, 0:N], ALU.mult, ALU.mult, [sc[0], sc[1]], [mqT])
            if not smp:
                for h in range(4):
                    for mc in range(2):
                        spb = pb[2 + mc]
                        mm(spb[:, 0:N], mkT[:, h, mc * 128:(mc + 1) * 128], mqT[:, h, 0:N], True, True, [mkT, mqT], [spb])
                        act(sh[1 + mc][:, 0:N], spb[:, 0:N], AF.Exp, [spb], [sh[1 + mc]], scale=MEM_SCALE)
                        mm(pb[4][:, 0:N], mvtok[:, mc, h * 128:(h + 1) * 128], sh[1 + mc][:, 0:N], mc == 0, mc == 1, [mvtok, sh[1 + mc]], [pb[4]])
                        mm(pb[5][:, 0:N], onesb(128, 128), sh[1 + mc][:, 0:N], mc == 0, mc == 1, [sh[1 + mc]], [pb[5]])
                    recip(sc[2][:, 0:N], pb[5][:, 0:N], [pb[5]], [sc[2]])
                    tt('vector', sc[4 + h][:, 0:N], pb[4][:, 0:N], sc[2][:, 0:N], ALU.mult, [pb[4], sc[2]], [sc[4 + h]])
            else:
                chk('s%d_mem0' % l)
                P.barrier()
                mem_sample(l)
                P.barrier()
                chk('s%d_mem1' % l)
            wv, wkeys = stage_win(l, [(O_SG + 1536, 512)])
            for h in range(4):
                psg = zchunk(wv, wkeys, h * 128, 128)
                act(sc[h % 2][:, 0:N], psg[:, 0:N], AF.Silu, [psg], [sc[h % 2]])
                tt('gpsimd', ys[:, 12 + h, 0:N], sc[4 + h][:, 0:N], sc[h % 2][:, 0:N], ALU.mult, [sc[4 + h], sc[h % 2]], [ys])

            for n in range(4):
                for half in range(2):
                    wbv, wbk = stage_rows(w_bo[l, n][:, half * 512:(half + 1) * 512], 4, 512)
                    wv, wkeys = stage_win(l, [(O_MG + n * 1024 + half * 512, 512)])
                    for dc in range(4):
                        dch = half * 4 + dc
                        pg = zchunk(wv, wkeys, dc * 128, 128)
                        g_, t_ = sc[2 * (dc % 2)], sc[2 * (dc % 2) + 1]
                        act(g_[:, 0:N], pg[:, 0:N], AF.Sigmoid, [pg], [g_])
                        pp = pb[2 + dc % 2]
                        for kc in range(4):
                            mm(pp[:, 0:N], wbv[:, kc, dc * 128:(dc + 1) * 128], ys[:, n * 4 + kc, 0:N], kc == 0, kc == 3, wbk + [ys], [pp])
                        if n == 0:
                            tt('vector', merged[:, dch, 0:N], pp[:, 0:N], g_[:, 0:N], ALU.mult, [pp, g_], [merged])
                        else:
                            tt('vector', t_[:, 0:N], pp[:, 0:N], g_[:, 0:N], ALU.mult, [pp, g_], [t_])
                            tt('gpsimd', merged[:, dch, 0:N], merged[:, dch, 0:N], t_[:, 0:N], ALU.add, [merged, t_], [merged])
            cp('vector', hnT[:, 0:4, 0:N], merged[:, 0:4, 0:N], [merged], [hnT])
            cp('gpsimd', hnT[:, 4:8, 0:N], merged[:, 4:8, 0:N], [merged], [hnT])
            for half in range(2):
                wov, wok = stage_rows(w_o[l][:, half * 512:(half + 1) * 512], 8, 512)
                for j, (t, n) in enumerate(subt):
                    pp = pb[2 + j % 2]
                    for kc in range(8):
                        mm(pp[0:n, :], hnT[:, kc, j * 128:j * 128 + n], wov[:, kc, :], kc == 0, kc == 7, wok + [hnT], [pp])
                    xt_ = xtok[0:n, t, half * 512:(half + 1) * 512]
                    tt('vector', xt_, pp[0:n, :], xt_, ALU.add, [pp, ('x', t)], [('x', t)])
            if l == 1:
                for j, (t, n) in enumerate(subt):
                    if not smp:
                        P.dma('gpsimd', yp[t * 128:(t + 1) * 128, :], xtok[:, t, :], reads=[('x', t)])
                    else:
                        P.dma('gpsimd', ysm, xtok[0:64, 16, :], reads=[('x', 16)])
        def mla_sample(l):
            N = NS
            clat_l = clats[l]
            crope_l = cropes[l]
            mq_ = [merged[:, 2 * i:2 * i + 2, :].rearrange("p a b -> p (a b)") for i in range(3)]
            mk_ = [('mgs', i) for i in range(3)]
            for h in range(4):
                mm(pb[2][0:64, 0:64], KnS[:, h, :], QnT[:, h, 0:N], True, True, [KnS, QnT], [pb[2]])
                act(sc[0][0:64, 0:64], pb[2][0:64, 0:64], AF.Exp, [pb[2]], [sc[0]], scale=MLA_SCALE)
                tt('vector', pTn[:, :, h * 8:(h + 1) * 8], sc[0][0:64, 0:64].rearrange("p (a t) -> p a t", t=8),
                   cst[0:64, C_BD:C_BD + 64].rearrange("p (a t) -> p a t", t=8), ALU.mult, [sc[0]], [pTn])
                ts('vector', QgT[:, h, :], QnT[:, h, 0:N], V(l, 44, 96), None, ALU.mult, None, [QnT], [QgT])
                mm(pb[3][:, 0:64], wukT[:, h, :], QgT[0:64, h, :], True, True, [wukT, QgT], [pb[3]])
                cp('vector', QaC[:, :, h * 8:(h + 1) * 8], pb[3][:, 0:64].rearrange("p (a t) -> p a t", t=8), [pb[3]], [QaC])
                for i in range(4):
                    mm(pb[3][:, 64:128], cbf[64:96, C_Z + 160 - 32 * i:C_Z + 288 - 32 * i], QgT[64:96, h, :], True, True, [QgT], [pb[3]])
                    cp('vector', QaR[:, :, i, h * 8:(h + 1) * 8], pb[3][:, 64:128].rearrange("p (a t) -> p a t", t=8), [pb[3]], [QaR])
            P.op('gpsimd', lambda e: e.memset(gbf[:, :, 128:136], 1.0), (), [gbf])
            chk('x1')
            for bp in range(8):
                for ch in range(8):
                    P.dma('gpsimd', graw[:, :], clat_l, reads=[idx2], writes=[graw],
                          indirect=bass.IndirectOffsetOnAxis(ap=idx2[:, bp, ch:ch + 1], axis=0))
                    P.dma('gpsimd', rraw[:, :], crope_l, reads=[idx2], writes=[rraw],
                          indirect=bass.IndirectOffsetOnAxis(ap=idx2[:, bp, ch:ch + 1], axis=0))
                    g3 = graw[:, :].rearrange("p (a r) -> p a r", r=128)
                    act(gbf[:, 0:7, 0:128], g3[:, 0:7, :], AF.Copy, [graw], [gbf])
                    cp('vector', gbf[:, 7:16, 0:128], g3[:, 7:16, :], [graw], [gbf])
                    cp('vector', rbf[:, :, :], rraw[:, :].rearrange("p (a r) -> p a r", r=32), [rraw], [rbf])
                    tt('gpsimd', mq_[0], rraw[:, :], rraw[:, :], ALU.mult, [rraw], [mk_[0]])
                    P.op('vector', lambda e: e.tensor_reduce(out=stat[:, 0:16], in_=mq_[0].rearrange("p (a r) -> p a r", r=32),
                                                             axis=AX.X, op=ALU.add), [mk_[0]], [('stat', 0)])
                    chk('x2')
                    for g in range(4):
                        for i in range(4):
                            row = g * 4 + i
                            tr(pbt[:, i * 128:(i + 1) * 128], gbf[:, row, 0:128], identb(128), [gbf], [pbt])
                        tr(pbt[:, 512:640], rbf[:, g * 4:(g + 1) * 4, :].rearrange("p a r -> p (a r)"), identb(128), [rbf], [pbt])
                        cp('vector', cTs[:, :, :], pbt[:, 0:512].rearrange("p (a r) -> p a r", r=128), [pbt], [cTs])
                        cp('vector', rTs[:, :], pbt[:, 512:640], [pbt], [rTs])
                        chk('g1')
                        for i in range(4):
                            pk_ = pb[i // 2]
                            mm(pk_[:, (i % 2) * 256:(i % 2) * 256 + 256], cTs[:, i, :], wuk_b[:, :], True, True, [cTs, wuk_b], [pk_])
                            if i == 0:
                                mm(pb[2][:, 0:128], rTs[:, :], QaR[:, bp, :, :].rearrange("p a t -> p (a t)"), True, False, [rTs, QaR], [pb[2]])
                            mm(pb[2][:, i * 32:(i + 1) * 32], cTs[:, i, :], QaC[:, bp, :], False, i == 3, [cTs, QaC], [pb[2]])
                        chk('g2')
                        act(mq_[1], pb[0][:, 0:512], AF.Square, [pb[0]], [mk_[1]])
                        act(mq_[2], pb[1][:, 0:512], AF.Square, [pb[1]], [mk_[2]])
                        P.op('vector', lambda e: e.tensor_reduce(out=stat[:, 16:24], in_=mq_[1].rearrange("p (a r) -> p a r", r=64),
                                                                 axis=AX.X, op=ALU.add), [mk_[1]], [('stat', 1)])
                        P.op('vector', lambda e: e.tensor_reduce(out=stat[:, 24:32], in_=mq_[2].rearrange("p (a r) -> p a r", r=64),
                                                                 axis=AX.X, op=ALU.add), [mk_[2]], [('stat', 1)])
                        tt('vector', stat[:, 32:48].rearrange("p (a h) -> p a h", h=4), stat[:, 16:32].rearrange("p (a h) -> p a h", h=4),
                           stat[:, g * 4:(g + 1) * 4].rearrange("p (a o) -> p a o", o=1).broadcast_to([128, 4, 4]), ALU.add,
                           [('stat', 0), ('stat', 1)], [('stat', 2)])
                        rsqrt_to(stat[:, 32:48], stat[:, 32:48], 128, 1.0 / 96, [('stat', 2)], [('stat', 2)])
                        tt('vector', sc[4][:, 0:128].rearrange("p (a t) -> p a t", t=8), pb[2][:, 0:128].rearrange("p (a t) -> p a t", t=8),
                           stat[:, 32:48].rearrange("p (a o) -> p a o", o=1).broadcast_to([128, 16, 8]), ALU.mult,
                           [pb[2], ('stat', 2)], [sc[4]])
                        chk('g3')
                        act(sc[5][:, 0:128], sc[4][:, 0:128], AF.Exp, [sc[4]], [sc[5]], scale=MLA_SCALE)
                        tt('gpsimd', pTs[:, :, :], sc[5][:, 0:128].rearrange("p (a t) -> p a t", t=32),
                           cst[:, C_PM:C_PM + 32].rearrange("p (o t) -> p o t", o=1).broadcast_to([128, 4, 32]), ALU.mult, [sc[5]], [pTs])
                        chk('g4')
                        for i in range(4):
                            row = g * 4 + i
                            mm(pb[4][0:32, 0:130], pTs[:, i, :], gbf[:, row, 0:130], ch == 0 and g == 0 and i == 0, False, [pTs, gbf], [pb[4]])
                        chk('x3')
                mm(pb[4][0:32, 0:130], pTn[:, bp, :], cnew[:, 0:130], False, True, [pTn, cnew], [pb[4]])
                recip(small[0:32, 1:2], pb[4][0:32, 128:129], [pb[4]], ['sm1'])
                ts('vector', sc[6][0:32, 0:128], pb[4][0:32, 0:128], small[0:32, 1:2], None, ALU.mult, None, [pb[4], 'sm1'], [sc[6]])
                tr(pb[5][:, 0:32], sc[6][0:32, 0:128], identf(32), [sc[6]], [pb[5]])
                cp('vector', onT[:, :, bp * 8:(bp + 1) * 8], pb[5][:, 0:32].rearrange("p (h t) -> p h t", t=8), [pb[5]], [onT])
                chk('x4')

        def mem_sample(l):
            for b in range(16):
                P.dma('sync', mraw[:, :, :], cmk[l, b].rearrange("(mc p) d -> p mc d", p=128), writes=[mraw])
                act(mkb[:], mraw[:], AF.Copy, [mraw], [mkb])
                P.dma('sync', mraw[:, :, :], cmv[l, b].rearrange("(mc p) d -> p mc d", p=128), writes=[mraw])
                cp('vector', mvb[:], mraw[:], [mraw], [mvb])
                for h in range(4):
                    for mc in range(2):
                        tr(pbt[:, (h * 2 + mc) * 128:(h * 2 + mc + 1) * 128], mkb[:, mc, h * 128:(h + 1) * 128], identb(128), [mkb], [pbt])
                cp('vector', mkTs[:, :, :], pbt[:, 0:1024].rearrange("p (h m) -> p h m", m=256), [pbt], [mkTs])
                for h in range(4):
                    for mc in range(2):
                        o4 = (h * 2 + mc) * 4
                        mm(pb[2][:, o4:o4 + 4], mkTs[:, h, mc * 128:(mc + 1) * 128], mqT[:, h, 4 * b:4 * b + 4], True, True, [mkTs, mqT], [pb[2]])
                act(pms[:, 0:32], pb[2][:, 0:32], AF.Exp, [pb[2]], [pms], scale=MEM_SCALE)
                for h in range(4):
                    for mc in range(2):
                        o4 = (h * 2 + mc) * 4
                        mm(pb[4][:, h * 64 + 4 * b:h * 64 + 4 * b + 4], mvb[:, mc, h * 128:(h + 1) * 128], pms[:, o4:o4 + 4], mc == 0, mc == 1,
                           [mvb, pms], [pb[4]])
                    for mc in range(2):
                        o4 = (h * 2 + mc) * 4
                        mm(pb[5][:, h * 64 + 4 * b:h * 64 + 4 * b + 4], onesb(128, 128), pms[:, o4:o4 + 4], mc == 0, mc == 1, [pms], [pb[5]])
            for h in range(4):
                recip(sc[2][:, 0:64], pb[5][:, h * 64:(h + 1) * 64], [pb[5]], [sc[2]])
                tt('vector', sc[4 + h][:, 0:64], pb[4][:, h * 64:(h + 1) * 64], sc[2][:, 0:64], ALU.mult, [pb[4], sc[2]], [sc[4 + h]])

        P.barrier()
        class _Stop(Exception):
            pass

        def chk(tag):
            if STOP == tag:
                raise _Stop()
        try:
            for l in range(2):
                load_small(l)
                mem_kv(l)
                chk('memkv%d' % l)
                for ti in range(NTI):
                    tile_pass(l, 'p', ti)
                    chk('p%d_%d' % (l, ti))
                P.barrier()
                tile_pass(l, 's', 0)
                P.barrier()
                chk('s%d' % l)
        except _Stop:
            pass
        P.finish()
        print('[build] inst counts', P.cnt, 'dma vals', max(P.dma_val), flush=True)
    return nc


_NC_CACHE = {}


def kernel(x_prompt, x_sample, mem_prompt, cache_mla_latent, cache_mla_rope, page_table, state_hgrn, state_conv,
           cache_mem_k, cache_mem_v, norm_gain, w_in, conv_w, hgrn_lb, hgrn_norm, mla_q_norm, mla_w_uq,
           mla_kv_norm, mla_w_uk, mla_w_uv, mla_q_gain, mla_k_gain, mem_norm, mem_w_k, mem_w_v, mem_q_gain,
           mem_k_gain, w_branch_out, w_out):
    f = lambda a: np.ascontiguousarray(np.asarray(a, dtype=np.float32))
    cst, rope = make_consts()
    vecs = np.zeros((128, 2, 48), np.float32)
    ng, mn, cw, lb, hn = f(norm_gain), f(mem_norm), f(conv_w), f(hgrn_lb), f(hgrn_norm)
    for l in range(2):
        vecs[:, l, 0:8] = ng[l].reshape(8, 128).T
        vecs[:, l, 8:16] = mn[l].reshape(8, 128).T
        for j in range(3):
            vecs[:, l, 16 + 4 * j:20 + 4 * j] = cw[l, j].reshape(4, 128).T
        vecs[:, l, 28:32] = lb[0].reshape(4, 128).T
        vecs[:, l, 32:36] = lb[1].reshape(4, 128).T
        vecs[:, l, 36:40] = hn[l].reshape(4, 128).T
        qn = f(mla_q_norm)[l]
        vecs[:, l, 40] = qn[0:128]
        vecs[0:64, l, 41] = qn[128:192]
        vecs[:, l, 42] = f(mla_kv_norm)[l]
        vecs[0:96, l, 43] = f(mla_q_gain)[l]
        vecs[0:96, l, 44] = f(mla_k_gain)[l]
        vecs[:, l, 45] = f(mem_q_gain)[l]
        vecs[:, l, 46] = f(mem_k_gain)[l]
        vecs[:, l, 47] = EPS
    clat = f(cache_mla_latent).reshape(2, NPHYS * 8, 2048)
    crope = f(cache_mla_rope).reshape(2, NPHYS * 8, 512)
    pt = np.asarray(page_table, dtype=np.int32)
    shared = dict(clat0=clat[0], clat1=clat[1], crope0=crope[0], crope1=crope[1],
                  w_in=f(w_in), w_bo=f(w_branch_out), w_o=f(w_out), w_uq=f(mla_w_uq), w_uk=f(mla_w_uk), w_uv=f(mla_w_uv),
                  mwk=f(mem_w_k), mwv=f(mem_w_v), vecs=vecs, cst=cst, rope=rope)
    xp_, xs_, mp_ = f(x_prompt), f(x_sample), f(mem_prompt)
    shg_, scv_, cmk_, cmv_ = f(state_hgrn), f(state_conv), f(cache_mem_k), f(cache_mem_v)
    in_maps = []
    for c in range(8):
        bs = slice(16 * c, 16 * c + 16)
        m = dict(shared)
        m.update(xp=xp_[c], xs=xs_[bs].reshape(64, D), memp=mp_[c],
                 ptab=np.ascontiguousarray(pt[bs].reshape(8, 128).T),
                 shg=np.ascontiguousarray(shg_[:, bs]), scv=np.ascontiguousarray(scv_[:, bs]),
                 cmk=np.ascontiguousarray(cmk_[:, bs].reshape(2, 16, 256, 512)),
                 cmv=np.ascontiguousarray(cmv_[:, bs].reshape(2, 16, 256, 512)))
        in_maps.append(m)
    if 'nc' not in _NC_CACHE:
        _NC_CACHE['nc'] = build_nc()
    ncr = CFG['ncores']
    res = run_bass_kernel_spmd(_NC_CACHE['nc'], in_maps[:ncr], core_ids=list(range(ncr)))
    R = list(res.results)
    while len(R) < 8:
        R.append(R[0])
    cat = lambda k: np.stack([r[k] for r in R], 0)
    y_prompt = cat("yp")
    y_sample = cat("ysm").reshape(128, 4, D)
    latp = cat("o_latp").transpose(1, 0, 2, 3)
    ropep = cat("o_ropep").transpose(1, 0, 2, 3)
    hgp = cat("o_hgp").transpose(1, 0, 2, 3, 4)
    cvp = cat("o_cvp").transpose(1, 0, 2, 3)
    mkp = cat("o_mkp").transpose(1, 0, 2, 3).reshape(2, 8, 256, 4, 128)
    mvp = cat("o_mvp").transpose(1, 0, 2, 3).reshape(2, 8, 256, 4, 128)
    lats = cat("o_lats").transpose(1, 0, 2, 3).reshape(2, 128, 4, 128)
    ropes = cat("o_ropes").transpose(1, 0, 2, 3).reshape(2, 128, 4, 32)
    hgs = cat("o_hgs").transpose(1, 0, 2, 3, 4, 5).reshape(2, 128, 4, 128, 128)
    cvs = cat("o_cvs").transpose(1, 0, 2, 3, 4).reshape(2, 128, 2, 512)
    outs = (y_prompt, y_sample, latp, ropep, hgp, cvp, mkp, mvp, lats, ropes, hgs, cvs)
    return tuple(np.ascontiguousarray(o, dtype=np.float32) for o in outs)
```

```python
import numpy as np
from contextlib import ExitStack
import concourse.bass as bass
import concourse.mybir as mybir
from concourse.bass_utils import run_bass_kernel_spmd

F32 = mybir.dt.float32
BF16 = mybir.dt.bfloat16
I32 = mybir.dt.int32
AF = mybir.ActivationFunctionType
ALU = mybir.AluOpType
AX = mybir.AxisListType
ENGS = ['sync', 'scalar', 'vector', 'gpsimd', 'tensor']


class Prog:
    def __init__(self, nc, ndma=44):
        self.nc = nc
        self.q = {e: [] for e in ENGS}
        self.cnt = {e: 0 for e in ENGS}
        self.waited = {e: {} for e in ENGS}
        self.lastw = {}
        self.readers = {}
        self.ndma = ndma
        self.dma_val = [0] * ndma
        self.dma_i = 0
        self.es = None
        self.sems = {}
        self.all_dma_events = {}

    def sb(self, name, shape, dt):
        return self.es.enter_context(self.nc.sbuf_tensor(name, list(shape), dt))

    def ps(self, name, shape, dt):
        return self.es.enter_context(self.nc.psum_tensor(name, list(shape), dt))

    @staticmethod
    def _key(t):
        if isinstance(t, (str, tuple)):
            return t
        return t.name

    def _deps(self, eng, reads, writes):
        deps = []
        for k in reads:
            k = self._key(k)
            if k in self.lastw:
                deps.append(self.lastw[k])
        for k in writes:
            k = self._key(k)
            if k in self.lastw:
                deps.append(self.lastw[k])
            deps += self.readers.get(k, [])
        waits = {}
        for (s, v) in deps:
            if s == 'tensor' and eng == 'tensor':
                continue
            if self.waited[eng].get(s, 0) < v:
                waits[s] = max(waits.get(s, 0), v)
        for s, v in waits.items():
            self.waited[eng][s] = v
        return waits

    def _commit(self, ev, reads, writes):
        for k in reads:
            k = self._key(k)
            self.readers.setdefault(k, []).append(ev)
        for k in writes:
            k = self._key(k)
            self.lastw[k] = ev
            self.readers[k] = []

    def op(self, eng, fn, reads=(), writes=()):
        waits = self._deps(eng, reads, writes)
        self.cnt[eng] += 1
        ev = (eng, self.cnt[eng])
        self.q[eng].append((waits, fn, eng, 1))
        self._commit(ev, reads, writes)

    def dma(self, eng, out, in_, reads=(), writes=(), indirect=None, noncontig=False, eoff=0):
        waits = self._deps(eng, reads, writes)
        i = self.dma_i
        self.dma_i = (self.dma_i + 1) % self.ndma
        s = ('d', i)
        if self.dma_val[i] > 0 and self.waited[eng].get(s, 0) < self.dma_val[i]:
            waits[s] = self.dma_val[i]
            self.waited[eng][s] = self.dma_val[i]
        self.dma_val[i] += 16
        ev = (s, self.dma_val[i])
        self.all_dma_events[s] = self.dma_val[i]
        if indirect is not None:
            fn = lambda e: e.indirect_dma_start(out=out, out_offset=None, in_=in_, in_offset=indirect, element_offset=eoff)
        elif noncontig:
            fn = lambda e: e.dma_start(out=out, in_=in_, allow_slow_non_contiguous=True)
        else:
            fn = lambda e: e.dma_start(out=out, in_=in_)
        self.q[eng].append((waits, fn, s, 16))
        self._commit(ev, reads, writes)

    def barrier(self):
        for e in ENGS:
            w = {}
            for s_, v in self.all_dma_events.items():
                if self.waited[e].get(s_, 0) < v:
                    w[s_] = v
                    self.waited[e][s_] = v
            if w:
                self.q[e].append((w, None, None, 0))
        for e in ENGS:
            for o in ENGS:
                if o != e and self.cnt[o] > self.waited[e].get(o, 0):
                    self.q[e].append(({o: self.cnt[o]}, None, None, 0))
                    self.waited[e][o] = self.cnt[o]

    def finish(self):
        nc = self.nc
        fin = dict(self.all_dma_events)
        for e in ENGS:
            if e != 'sync' and self.cnt[e] > 0:
                fin[e] = self.cnt[e]
        self.q['sync'].append((fin, None, None, 0))
        for e in ENGS:
            self.sems[e] = self.es.enter_context(nc.semaphore("s_" + e))
        for i in range(self.ndma):
            self.sems[('d', i)] = self.es.enter_context(nc.semaphore("s_d%d" % i))
        block = self.es.enter_context(nc.Block())

        def replay(name):
            def f(eng):
                for (waits, fn, incs, incv) in self.q[name]:
                    for s, v in waits.items():
                        eng.wait_ge(self.sems[s], v)
                    if fn is not None:
                        ins = fn(eng)
                        ins.then_inc(self.sems[incs], incv)
            return f
        block.sync(replay('sync'))
        block.scalar(replay('scalar'))
        block.vector(replay('vector'))
        block.gpsimd(replay('gpsimd'))
        block.tensor(replay('tensor'))


D = 1024
SEQ = 2048
NT = 256
TPT = NT // 128
NTI = 2048 // NT
NS = 64
BRW = 512
EPS = 1e-6
O_CH, O_CB, O_CC, O_HQ, O_HF, O_HI, O_QL, O_KV, O_KPE, O_MQ, O_SG, O_MG = (
    0, 512, 1024, 1536, 2048, 2560, 3072, 3264, 3392, 3424, 3936, 5984)
NIN = 10080
NPHYS = 10240
PAST = 8192
C_ID = 0
C_TRI = 128
C_T64 = 256
C_BD = 320
C_PM = 384
C_IND = 416
C_SEGP = 432
C_SEGS = 944
C_E = 1008
C_SEL = 1104
C_ONE = 1136
C_Z = 1264
CW = 1552


def make_consts():
    c = np.zeros((128, CW), np.float32)
    c[:, C_ID:C_ID + 128] = np.eye(128)
    k = np.arange(128)
    c[:, C_TRI:C_TRI + 128] = (k[:, None] <= k[None, :])
    s = np.arange(64)
    c[:64, C_T64:C_T64 + 64] = (s[:, None] <= s[None, :])
    c[:64, C_BD:C_BD + 64] = (s[:, None] <= s[None, :]) & ((s[:, None] // 4) == (s[None, :] // 4))
    pm = np.zeros((128, 4, 2, 4), np.float32)
    pm[:64, :, 0, :] = 1
    pm[64:, :, 1, :] = 1
    c[:, C_PM:C_PM + 32] = pm.reshape(128, 32)
    c[:64, C_IND:C_IND + 16] = ((s[:, None] // 4) == np.arange(16)[None, :])
    c[:, C_SEGP:C_SEGP + 512] = (np.arange(512) % 64 != 0)[None, :]
    c[:, C_SEGS:C_SEGS + 64] = (np.arange(64) % 4 != 0)[None, :]
    for j in range(32):
        c[j, C_E + 64 + j] = 1
        c[64 + j, C_SEL + j] = 1
    c[:, C_ONE:C_ONE + 128] = 1
    for p_ in range(128):
        c[p_, C_Z + 96 + p_] = 1
    half = 16
    freqs = (10000.0 ** (-np.arange(half, dtype=np.float32) / half)).astype(np.float32)
    pos = np.concatenate([np.arange(SEQ), np.tile(PAST + np.arange(4), 16)]).astype(np.float32)
    ang = (pos[None, :] * freqs[:, None]).astype(np.float32)
    cos, sin = np.cos(ang).astype(np.float32), np.sin(ang).astype(np.float32)
    cc = np.concatenate([cos, cos], 0)
    ss = np.concatenate([-sin, sin], 0)
    r = np.zeros((128, 4, SEQ + NS), np.float32)
    r[:64, 0] = 1
    r[64:96, 0] = cc
    r[64:96, 1] = ss
    r[:32, 2] = cc
    r[:32, 3] = ss
    return c, r


CFG = {'nphys': NPHYS, 'stop': None, 'ncores': 8}


def build_nc():
    nc = bass.Bass("TRN2", target_bir_lowering=False)
    NPH = CFG['nphys']
    STOP = CFG['stop']

    def din(name, shape, dt=F32):
        return nc.dram_tensor(name, list(shape), dt, kind="ExternalInput").ap()

    def dout(name, shape):
        return nc.dram_tensor(name, list(shape), F32, kind="ExternalOutput").ap()

    xp = din("xp", [SEQ, D]); xs = din("xs", [NS, D]); memp = din("memp", [256, D])
    clats = [din("clat%d" % i, [NPH * 8, 2048]) for i in range(2)]; cropes = [din("crope%d" % i, [NPH * 8, 512]) for i in range(2)]
    ptab = din("ptab", [128, 8], I32)
    shg = din("shg", [2, 16, 4, 128, 128]); scv = din("scv", [2, 16, 2, 512])
    cmk = din("cmk", [2, 16, 256, 512]); cmv = din("cmv", [2, 16, 256, 512])
    w_in = din("w_in", [2, D, NIN]); w_bo = din("w_bo", [2, 4, 512, D]); w_o = din("w_o", [2, D, D])
    w_uq = din("w_uq", [2, 192, 384]); w_uk = din("w_uk", [2, 128, 256]); w_uv = din("w_uv", [2, 128, 512])
    mwk = din("mwk", [2, D, 512]); mwv = din("mwv", [2, D, 512])
    vecs = din("vecs", [128, 2, 48])
    cst_d = din("cst", [128, CW]); rope_d = din("rope", [128, 4, SEQ + NS])

    yp = dout("yp", [SEQ, D]); ysm = dout("ysm", [NS, D])
    o_latp = dout("o_latp", [2, SEQ, 128]); o_ropep = dout("o_ropep", [2, SEQ, 32])
    o_hgp = dout("o_hgp", [2, 4, 128, 128]); o_cvp = dout("o_cvp", [2, 2, 512])
    o_mkp = dout("o_mkp", [2, 256, 512]); o_mvp = dout("o_mvp", [2, 256, 512])
    o_lats = dout("o_lats", [2, NS, 128]); o_ropes = dout("o_ropes", [2, NS, 32])
    o_hgs = dout("o_hgs", [2, 16, 4, 128, 128]); o_cvs = dout("o_cvs", [2, 16, 2, 512])

    P = Prog(nc)
    with ExitStack() as es:
        P.es = es
        sb, ps = P.sb, P.ps
        xtok = sb("xtok", [128, 17, D], F32)
        cst = sb("cstt", [128, CW], F32)
        cbf = sb("cbf", [128, CW], BF16)
        ropet = sb("ropet", [128, 4, NT], F32)
        vec = sb("vec", [128, 2, 48], F32)
        lbv = sb("lbv", [128, 2, 8], F32)
        wsts = [sb("wstA", [128, 2048], F32), sb("wstB", [128, 2048], F32)]
        wbfs = [sb("wbf0", [128, 4096], BF16), sb("wbf1", [128, 4096], BF16), sb("wbf2", [128, 4096], BF16)]
        hnT = sb("hnT", [128, 8, NT], BF16)
        ys = sb("ys", [128, 16, NT], BF16)
        merged = sb("merged", [128, 8, NT], F32)
        un = sb("un", [128, 10368], BF16)
        unf = un[:, :].bitcast(F32)

        class View:
            def __init__(self, ap, name):
                self.ap = ap
                self.name = name

            def __getitem__(self, k):
                return self.ap[k]
        KnAll = View(un[0:96, 0:8192].rearrange("p (h t) -> p h t", t=SEQ), "KnAll")
        ctokAll = View(un[:, 8192:8192 + 2176].rearrange("p (j r) -> p j r", r=136), "ctokAll")
        S = [sb("Sst%d" % i, [128, 128], F32) for i in range(4)]
        Sb = [sb("Sbf%d" % i, [128, 128], BF16) for i in range(4)]
        sc = [sb("sc%d" % i, [128, NT], F32) for i in range(10)]
        sh = [sb("sh%d" % i, [128, NT], BF16) for i in range(6)]
        uext = sb("uext", [128, 4, NT + 2], F32)
        vtok = sb("vtok", [64, 8, 128], BF16)
        ktok = sb("ktok", [64, 8, 128], BF16)
        aTm = sb("aTm", [64, 64], BF16)
        small = sb("small", [128, 64], F32)
        mkT = sb("mkT", [128, 4, 256], BF16)
        mvtok = sb("mvtok", [128, 2, 512], BF16)
        wq_b = sb("wq_b", [128, 2, 768], BF16)
        wuke = sb("wuke", [128, 4, 96], BF16)
        wuk_b = sb("wuk_b", [128, 256], BF16)
        wukT = sb("wukT", [64, 4, 128], BF16)
        wuv_b = sb("wuv_b", [128, 512], BF16)
        QnT = sb("QnT", [96, 4, NT], BF16)
        onT = sb("onT", [128, 4, NT], BF16)
        mqT = sb("mqT", [128, 4, NT], BF16)
        graw = View(unf[:, 0:2048], "graw")
        rraw = View(unf[:, 2048:2560], "rraw")
        gbf = View(un[:, 5120:5120 + 2176].rearrange("p (a r) -> p a r", r=136), "gbf")
        rbf = View(un[:, 7296:7296 + 512].rearrange("p (a r) -> p a r", r=32), "rbf")
        cTs = View(un[:, 7808:7808 + 512].rearrange("p (a r) -> p a r", r=128), "cTs")
        rTs = View(un[:, 8320:8320 + 128], "rTs")
        mraw = View(unf[:, 0:1024].rearrange("p (a r) -> p a r", r=512), "mraw")
        mkb = View(un[:, 2048:3072].rearrange("p (a r) -> p a r", r=512), "mkb")
        mvb = View(un[:, 3072:4096].rearrange("p (a r) -> p a r", r=512), "mvb")
        mkTs = View(un[:, 4096:5120].rearrange("p (a r) -> p a r", r=256), "mkTs")
        s0 = View(unf[:, 4416:4416 + 512].rearrange("p (a r) -> p a r", r=128), "s0")
        s0b = View(un[:, 9856:9856 + 512].rearrange("p (a r) -> p a r", r=128), "s0b")
        idx = sb("idx", [128, 8], I32)
        idx2 = sb("idx2", [128, 8, 8], I32)
        QaC = sb("QaC", [128, 8, 32], BF16)
        QaR = sb("QaR", [128, 8, 4, 32], BF16)
        QgT = sb("QgT", [96, 4, NS], BF16)
        pTn = sb("pTn", [64, 8, 32], BF16)
        cnew = sb("cnew", [64, 136], BF16)
        pTs = sb("pTs", [128, 4, 32], BF16)
        stat = sb("stat", [128, 64], F32)
        khm = sb("khm", [64, 128], BF16)
        pms = sb("pms", [128, 32], BF16)
        pb = [ps("pb%d" % i, [128, 512], F32) for i in range(6)]
        pbt = ps("pbt", [128, 1024], BF16)
        pb7 = ps("pb7", [128, 512], F32)

        def act(out, in_, func, r, w, scale=1.0, bias=None):
            if bias is None:
                P.op('scalar', lambda e: e.activation(out=out, in_=in_, func=func, scale=scale), r, w)
            else:
                P.op('scalar', lambda e: e.activation(out=out, in_=in_, func=func, scale=scale, bias=bias), r, w)

        def tt(eng, out, in0, in1, op, r, w):
            P.op(eng, lambda e: e.tensor_tensor(out=out, in0=in0, in1=in1, op=op), r, w)

        def ts(eng, out, in0, s1, s2, op0, op1, r, w):
            if s2 is None:
                P.op(eng, lambda e: e.tensor_scalar(out=out, in0=in0, scalar1=s1, scalar2=None, op0=op0), r, w)
            else:
                P.op(eng, lambda e: e.tensor_scalar(out=out, in0=in0, scalar1=s1, scalar2=s2, op0=op0, op1=op1), r, w)

        def stt(out, in0, scalar, in1, op0, op1, r, w):
            P.op('vector', lambda e: e.scalar_tensor_tensor(out=out, in0=in0, scalar=scalar, in1=in1, op0=op0, op1=op1), r, w)

        def cp(eng, out, in_, r, w):
            P.op(eng, lambda e: e.tensor_copy(out=out, in_=in_), r, w)

        def mm(out, lhsT, rhs, start, stop, r, w):
            P.op('tensor', lambda e: e.matmul(out, lhsT=lhsT, rhs=rhs, start=start, stop=stop), r, w)

        def tr(out, in_, ident, r, w):
            P.op('tensor', lambda e: e.transpose(out=out, in_=in_, identity=ident), r, w)

        def recip(out, in_, r, w):
            P.op('vector', lambda e: e.reciprocal(out=out, in_=in_), r, w)

        def rsqrt_to(out, in_, n, scale, r, w):
            act(out, in_, AF.Ln, r, w, scale=scale, bias=vec[0:n, 0, 47:48])
            act(out, out, AF.Exp, w, w, scale=-0.5)

        identb = lambda n: cbf[0:n, C_ID:C_ID + n]
        identf = lambda n: cst[0:n, C_ID:C_ID + n]
        onesb = lambda k, m: cbf[0:k, C_ONE:C_ONE + m]

        def V(l, j, n=128):
            return vec[0:n, l, j:j + 1]

        P.dma('sync', cst[:], cst_d, writes=[cst])
        P.dma('sync', vec[:], vecs, writes=[vec])
        P.dma('sync', idx[:], ptab, writes=[idx])
        cp('vector', cbf[:], cst[:], [cst], [cbf])
        for ch in range(8):
            ts('vector', idx2[:, :, ch], idx[:], 8.0, float(ch), ALU.mult, ALU.add, [idx], [idx2])
        pass
        P.op('gpsimd', lambda e: e.memset(cnew[:, 128:136], 1.0), (), [cnew])
        pass
        for t in range(16):
            P.dma('sync', xtok[:, t, :], xp[t * 128:(t + 1) * 128, :], writes=[('x', t)])
        P.dma('sync', xtok[0:64, 16, :], xs, writes=[('x', 16)])
        mbf = merged[:, :, :].rearrange("p a b -> p (a b)").bitcast(BF16)
        junk = View(mbf[:, 2048:3072], "merged")
        hnb = View(mbf[:, 3072:4096], "merged")
        P.op('gpsimd', lambda e: e.memset(lbv[:], 0.0), (), [lbv])
        tt('vector', lbv[:, 1, 0:4], vec[:, 0, 32:36], vec[:, 0, 28:32], ALU.subtract, [vec], [lbv])
        act(lbv[:, 1, 0:4], lbv[:, 1, 0:4], AF.Sigmoid, [lbv], [lbv])
        ts('vector', lbv[:, :, 4:8], lbv[:, :, 0:4], -1.0, 1.0, ALU.mult, ALU.add, [lbv], [lbv])
        P.op('gpsimd', lambda e: e.memset(wq_b[:], 0.0), (), [wq_b])
        P.op('gpsimd', lambda e: e.memset(wuke[:], 0.0), (), [wuke])

        wctr = [0]

        def WSTK(k):
            return [('wst', k, i) for i in range(8)]

        def wk(wb):
            return [(wb.name, 0), (wb.name, 1)]

        def stage(parts, nk, W):
            k = wctr[0] % 3
            wctr[0] += 1
            wb = wbfs[k]
            hk = nk // 2
            for hf in range(2):
                wsv = wsts[hf][:, 0:hk * W].rearrange("p (k n) -> p k n", n=W)
                for i, (off, n, src) in enumerate(parts):
                    P.dma('sync', wsv[:, :, off:off + n], src[:, hf * hk:(hf + 1) * hk, :], writes=[('wst', hf, i)])
            act(wb[:, 0:hk * W], wsts[0][:, 0:hk * W], AF.Copy, WSTK(0), [(wb.name, 0)])
            cp('vector', wb[:, hk * W:nk * W], wsts[1][:, 0:hk * W], WSTK(1), [(wb.name, 1)])
            return wb[:, 0:nk * W].rearrange("p (k n) -> p k n", n=W), wk(wb)

        def stage_rows(src2d, nk, ncols):
            return stage([(0, ncols, src2d.rearrange("(k p) n -> p k n", p=128))], nk, ncols)

        def tile_plan(l):
            pl = []
            for c in range(4):
                pl.append(('win', l, ((O_CH + c * 128, 128), (O_CB + c * 128, 128), (O_CC + c * 128, 128), (O_SG + c * 128, 128))))
            for h in range(4):
                pl.append(('win', l, ((O_HQ + h * 128, 128), (O_HF + h * 128, 128), (O_HI + h * 128, 128), (O_SG + 512 + h * 128, 128))))
            pl.append(('win', l, ((O_QL, 192), (O_KV, 128), (O_KPE, 32), (O_KPE + 16, 16), (O_KPE, 16))))
            pl.append(('win', l, ((O_SG + 1024, 512),)))
            pl.append(('win', l, ((O_MQ, 512),)))
            pl.append(('win', l, ((O_SG + 1536, 512),)))
            for n in range(4):
                for half in range(2):
                    pl.append(('wbo', l, n, half))
                    pl.append(('win', l, ((O_MG + n * 1024 + half * 512, 512),)))
            for half in range(2):
                pl.append(('wo', l, half))
            return pl

        PLAN = []
        for l_ in range(2):
            PLAN += [('mwk', l_), ('mwv', l_)]
            for t_ in range(NTI + 1):
                PLAN += tile_plan(l_)
        wp_state = {'i': 0, 'issued': {}}

        def wp_issue(j):
            if j < len(PLAN) and j not in wp_state['issued']:
                d = PLAN[j]
                if d[0] == 'win':
                    r_ = stage_win_now(d[1], list(d[2]))
                elif d[0] == 'wbo':
                    r_ = stage_rows(w_bo[d[1], d[2]][:, d[3] * 512:(d[3] + 1) * 512], 4, 512)
                elif d[0] == 'wo':
                    r_ = stage_rows(w_o[d[1]][:, d[2] * 512:(d[2] + 1) * 512], 8, 512)
                elif d[0] == 'mwk':
                    r_ = stage_rows(mwk[d[1]], 8, 512)
                else:
                    r_ = stage_rows(mwv[d[1]], 8, 512)
                wp_state['issued'][j] = r_

        def wnext(desc):
            j = wp_state['i']
            wp_state['i'] += 1
            assert PLAN[j] == desc, (j, PLAN[j], desc)
            wp_issue(j)
            wp_issue(j + 1)
            return wp_state['issued'].pop(j)

        def stage_win(l, segs):
            return wnext(('win', l, tuple(segs)))

        def stage_win_now(l, segs):
            W = sum(n for _, n in segs)
            src = w_in[l].rearrange("(k p) n -> p k n", p=128)
            parts = []
            off = 0
            for (c0, n) in segs:
                parts.append((off, n, src[:, :, c0:c0 + n]))
                off += n
            return stage(parts, 8, W)

        def make_hnT(xin, n, j, l, gcol, xkeys):
            act(junk[0:n, :], xin, AF.Square, xkeys, [junk, 'ssq'], scale=1.0)
            P.op('vector', lambda e: e.tensor_reduce(out=small[0:n, 0:1], in_=junk[0:n, :], axis=AX.X, op=ALU.add), [junk], ['ssq'])
            rsqrt_to(small[0:n, 0:1], small[0:n, 0:1], n, 1.0 / D, ['ssq'], ['ssq'])
            ts('vector', hnb[0:n, :], xin, small[0:n, 0:1], None, ALU.mult, None, xkeys + ['ssq'], [hnb])
            for kc in range(8):
                tr(pbt[:, kc * 128:kc * 128 + n], hnb[0:n, kc * 128:(kc + 1) * 128], identb(n), [hnb], [pbt])
            for kc in range(8):
                o_ = hnT[:, kc, j * 128:j * 128 + n]
                i_ = pbt[:, kc * 128:kc * 128 + n]
                if kc % 2 == 0:
                    ts('vector', o_, i_, V(l, gcol + kc), None, ALU.mult, None, [pbt], [hnT])
                else:
                    act(o_, i_, AF.Copy, [pbt], [hnT], scale=V(l, gcol + kc))

        def load_small(l):
            k = 0
            wst = wsts[0]
            WST = WSTK(0)
            parts = [(wst[:, 0:384], w_uq[l, 0:128, :]), (wst[0:64, 384:768], w_uq[l, 128:192, :]),
                     (wst[:, 768:1024], w_uk[l]), (wst[:, 1024:1536], w_uv[l])]
            for i, (d_, s_) in enumerate(parts):
                P.dma('sync', d_, s_, writes=[('wst', k, i)])
            cp('vector', wq_b[:, 0, 0:384], wst[:, 0:384], WST, [wq_b])
            cp('vector', wq_b[0:64, 1, 0:384], wst[0:64, 384:768], WST, [wq_b])
            for kc, n in ((0, 128), (1, 64)):
                sv = wst[0:n, kc * 384:(kc + 1) * 384].rearrange("p (h d) -> p h d", d=96)
                dv = wq_b[0:n, kc, 384:768].rearrange("p (h d) -> p h d", d=96)
                cp('vector', dv[:, :, 64:80], sv[:, :, 80:96], WST, [wq_b])
                cp('vector', dv[:, :, 80:96], sv[:, :, 64:80], WST, [wq_b])
            cp('vector', wuk_b[:], wst[:, 768:1024], WST, [wuk_b])
            cp('vector', wuke[:, :, 0:64], wst[:, 768:1024].rearrange("p (h d) -> p h d", d=64), WST, [wuke])
            cp('vector', wuv_b[:], wst[:, 1024:1536], WST, [wuv_b])
            for h in range(4):
                tr(pbt[0:64, h * 128:(h + 1) * 128], wuk_b[:, h * 64:(h + 1) * 64], identb(128), [wuk_b], [pbt])
            cp('vector', wukT[:], pbt[0:64, 0:512].rearrange("p (h r) -> p h r", r=128), [pbt], [wukT])

        def mem_kv(l):
            mt = merged[:, 0:4, :].rearrange("p a b -> p (a b)")
            for j in range(2):
                P.dma('sync', mt, memp[j * 128:(j + 1) * 128, :], writes=[merged])
                make_hnT(mt, 128, j, l, 8, [merged])
            wv, wkeys = wnext(('mwk', l))
            for h in range(4):
                for kc in range(8):
                    mm(pb[0][:, 0:256], wv[:, kc, h * 128:(h + 1) * 128], hnT[:, kc, 0:256], kc == 0, kc == 7, wkeys + [hnT], [pb[0]])
                act(sh[0][:, 0:256], pb[0][:, 0:256], AF.Square, [pb[0]], [sh[0]])
                mm(pb7[:, 0:256], onesb(128, 128), sh[0][:, 0:256], True, True, [sh[0]], [pb7])
                rsqrt_to(sc[0][:, 0:256], pb7[:, 0:256], 128, 1.0 / 128, [pb7], [sc[0]])
                stt(sc[1][:, 0:256], pb[0][:, 0:256], V(l, 46), sc[0][:, 0:256], ALU.mult, ALU.mult, [pb[0], sc[0]], [sc[1]])
                cp('gpsimd', mkT[:, h, :], sc[1][:, 0:256], [sc[1]], [mkT])
                for mc in range(2):
                    tr(pb[1][:, mc * 128:(mc + 1) * 128], sc[1][:, mc * 128:(mc + 1) * 128], identf(128), [sc[1]], [pb[1]])
                cp('vector', sc[2][:, 0:256], pb[1][:, 0:256], [pb[1]], [sc[2]])
                for mc in range(2):
                    P.dma('gpsimd', o_mkp[l, mc * 128:(mc + 1) * 128, h * 128:(h + 1) * 128], sc[2][:, mc * 128:(mc + 1) * 128], reads=[sc[2]])
            wv, wkeys = wnext(('mwv', l))
            for mc in range(2):
                for kc in range(8):
                    mm(pb[0][:, :], hnT[:, kc, mc * 128:(mc + 1) * 128], wv[:, kc, :], kc == 0, kc == 7, wkeys + [hnT], [pb[0]])
                for hf in range(2):
                    cp('vector', sc[3 + hf][:, :], pb[0][:, hf * 256:(hf + 1) * 256], [pb[0]], [sc[3 + hf]])
                    cp('gpsimd', mvtok[:, mc, hf * 256:(hf + 1) * 256], sc[3 + hf][:, :], [sc[3 + hf]], [mvtok])
                    P.dma('gpsimd', o_mvp[l, mc * 128:(mc + 1) * 128, hf * 256:(hf + 1) * 256], sc[3 + hf][:, :], reads=[sc[3 + hf]])
        KnS = sb("KnS", [96, 4, NS], BF16)
        MLA_SCALE = 96 ** -0.5
        MEM_SCALE = 128 ** -0.5

        def tile_pass(l, kind, ti):
            N = NT if kind == 'p' else NS
            smp = kind == 's'
            p0 = ti * NT if not smp else SEQ
            P.dma('sync', ropet[:, :, 0:N], rope_d[:, :, p0:p0 + N], writes=[ropet])
            if not smp:
                subt = [(ti * TPT + j, 128) for j in range(TPT)]
            else:
                subt = [(16, 64)]
            for j, (t, n) in enumerate(subt):
                make_hnT(xtok[0:n, t, :], n, j, l, 0, [('x', t)])
            zb = [0]

            def zchunk(wv, wkeys, off, m):
                p_ = pb[zb[0] % 2]
                zb[0] += 1
                for kc in range(8):
                    mm(p_[0:m, 0:N], wv[:, kc, off:off + m], hnT[:, kc, 0:N], kc == 0, kc == 7, wkeys + [hnT], [p_])
                return p_

            def v3(ap):
                return ap.rearrange("p (b s) -> p b s", s=4)

            for c in range(4):
                wv, wkeys = stage_win(l, [(O_CH + c * 128, 128), (O_CB + c * 128, 128), (O_CC + c * 128, 128), (O_SG + c * 128, 128)])
                ph = zchunk(wv, wkeys, 0, 128)
                act(sc[0][:, 0:N], ph[:, 0:N], AF.Copy, [ph], [sc[0]])
                pc_ = zchunk(wv, wkeys, 256, 128)
                ukey = ('u', c)
                if not smp:
                    if ti == 0:
                        P.op('gpsimd', lambda e, c=c: e.memset(uext[:, c, 0:2], 0.0), (), [ukey])
                    tt('vector', uext[:, c, 2:2 + N], pc_[:, 0:N], sc[0][:, 0:N], ALU.mult, [pc_, sc[0]], [ukey])
                    u0, u1, u2 = uext[:, c, 0:N], uext[:, c, 1:N + 1], uext[:, c, 2:N + 2]
                    cv, yv = sc[1][:, 0:N], sc[2][:, 0:N]
                else:
                    uv = uext[:, c, 0:96].rearrange("p (b s) -> p b s", s=6)
                    if c == 0:
                        mh = merged[0:32, 0:2, :].rearrange("p a b -> p (a b)")
                        mh2 = merged[0:32, 2:4, :].rearrange("p a b -> p (a b)")
                        P.dma('sync', mh, scv[l].rearrange("b j c -> (b j) c"), writes=[('mh', 0)])
                    tr(pb[2][:, 0:32], mh[:, c * 128:(c + 1) * 128], identf(32), [('mh', 0)], [pb[2]])
                    cp('vector', uv[:, :, 0:2], pb[2][:, 0:32].rearrange("p (b j) -> p b j", j=2), [pb[2]], [('uh', c, 0)])
                    tt('vector', uv[:, :, 2:6], v3(pc_[:, 0:N]), v3(sc[0][:, 0:N]), ALU.mult, [pc_, sc[0], ('uh', c, 0)], [ukey])
                    u0, u1, u2 = uv[:, :, 0:4], uv[:, :, 1:5], uv[:, :, 2:6]
                    cv, yv = v3(sc[1][:, 0:N]), v3(sc[2][:, 0:N])
                ts('vector', cv, u0, V(l, 16 + c), None, ALU.mult, None, [ukey], [sc[1]])
                stt(cv, u1, V(l, 20 + c), cv, ALU.mult, ALU.add, [ukey, sc[1]], [sc[1]])
                stt(cv, u2, V(l, 24 + c), cv, ALU.mult, ALU.add, [ukey, sc[1]], [sc[1]])
                pb_ = zchunk(wv, wkeys, 128, 128)
                tt('vector', sc[2][:, 0:N], pb_[:, 0:N], sc[1][:, 0:N], ALU.mult, [pb_, sc[1]], [sc[2]])
                psg = zchunk(wv, wkeys, 384, 128)
                act(sc[3][:, 0:N], psg[:, 0:N], AF.Silu, [psg], [sc[3]])
                tt('gpsimd', ys[:, c, 0:N], sc[2][:, 0:N], sc[3][:, 0:N], ALU.mult, [sc[2], sc[3]], [ys])
                if not smp:
                    if ti == NTI - 1:
                        P.dma('gpsimd', o_cvp[l][:, c * 128:(c + 1) * 128].rearrange("j p -> p j"), uext[:, c, N:N + 2],
                              reads=[ukey], noncontig=True)
                    else:
                        cp('gpsimd', uext[:, c, 0:2], uext[:, c, N:N + 2], [ukey], [ukey])
                else:
                    cp('vector', sc[8][:, 0:32].rearrange("p (b j) -> p b j", j=2), uv[:, :, 4:6], [ukey], [sc[8]])
                    tr(pb[3][0:32, c * 128:(c + 1) * 128], sc[8][:, 0:32], identf(128), [sc[8]], [pb[3]])
                    if c == 3:
                        cp('vector', mh2, pb[3][0:32, 0:512], [pb[3]], [('mh', 1)])
                        P.dma('gpsimd', o_cvs[l].rearrange("b j c -> (b j) c"), mh2, reads=[('mh', 1)])

            if smp:
                chk('s%d_conv' % l)
            for h in range(4):
                wv, wkeys = stage_win(l, [(O_HQ + h * 128, 128), (O_HF + h * 128, 128), (O_HI + h * 128, 128), (O_SG + 512 + h * 128, 128)])
                pq = zchunk(wv, wkeys, 0, 128)
                act(sc[0][:, 0:N], pq[:, 0:N], AF.Copy, [pq], [sc[0]], scale=128 ** -0.5)
                pf = zchunk(wv, wkeys, 128, 128)
                act(sc[1][:, 0:N], pf[:, 0:N], AF.Sigmoid, [pf], [sc[1]])
                ts('vector', sc[1][:, 0:N], sc[1][:, 0:N], lbv[:, l, 4 + h:5 + h], lbv[:, l, h:h + 1], ALU.mult, ALU.add, [sc[1], lbv], [sc[1]])
                act(sc[2][:, 0:N], sc[1][:, 0:N], AF.Ln, [sc[1]], [sc[2]])
                ts('gpsimd', sc[1][:, 0:N], sc[1][:, 0:N], -1.0, 1.0, ALU.mult, ALU.add, [sc[1], sc[2]], [sc[1]])
                segm = cst[:, C_SEGP:C_SEGP + N] if not smp else cst[:, C_SEGS:C_SEGS + N]
                P.op('vector', lambda e, segm=segm: e.tensor_tensor_scan(out=sc[3][:, 0:N], data0=segm, data1=sc[2][:, 0:N], initial=0.0,
                                                                         op0=ALU.mult, op1=ALU.add), [sc[2]], [sc[3]])
                act(sc[4][:, 0:N], sc[3][:, 0:N], AF.Exp, [sc[3]], [sc[4]])
                if not smp:
                    bv = sc[3][:, 0:N].rearrange("p (c s) -> p c s", s=64)
                    dv = sc[2][:, 0:N].rearrange("p (c s) -> p c s", s=64)
                    tt('vector', dv, bv, bv[:, :, 31:32].broadcast_to([128, N // 64, 64]), ALU.subtract, [sc[3]], [sc[2]])
                    act(sc[5][:, 0:N], sc[2][:, 0:N], AF.Exp, [sc[2]], [sc[5]])
                    act(sc[6][:, 0:N], sc[2][:, 0:N], AF.Exp, [sc[2]], [sc[6]], scale=-1.0)
                    E1 = sc[5]
                else:
                    E1 = sc[4]
                    act(sc[6][:, 0:N], sc[3][:, 0:N], AF.Exp, [sc[3]], [sc[6]], scale=-1.0)
                    bv = v3(sc[3][:, 0:N])
                    tt('vector', v3(sc[2][:, 0:N]), bv[:, :, 3:4].broadcast_to([128, 16, 4]), bv, ALU.subtract, [sc[3]], [sc[2]])
                    act(sc[5][:, 0:N], sc[2][:, 0:N], AF.Exp, [sc[2]], [sc[5]])
                    tt('gpsimd', sh[3][:, 0:N], sc[1][:, 0:N], sc[5][:, 0:N], ALU.mult, [sc[1], sc[5]], [sh[3]])
                tt('vector', sh[0][:, 0:N], sc[0][:, 0:N], E1[:, 0:N], ALU.mult, [sc[0], E1], [sh[0]])
                tt('gpsimd', sh[1][:, 0:N], sc[0][:, 0:N], sc[4][:, 0:N], ALU.mult, [sc[0], sc[4]], [sh[1]])
                tt('vector', sh[2][:, 0:N], sc[1][:, 0:N], sc[6][:, 0:N], ALU.mult, [sc[1], sc[6]], [sh[2]])
                nch = N // 64
                for half in range((nch + 3) // 4):
                    ncc = min(4, nch - half * 4)
                    for cc in range(ncc):
                        c_ = half * 4 + cc
                        for kc in range(8):
                            mm(pb[2][0:64, cc * 128:(cc + 1) * 128], hnT[:, kc, c_ * 64:(c_ + 1) * 64], wv[:, kc, 256:384], kc == 0, kc == 7,
                               wkeys + [hnT], [pb[2]])
                    act(vtok[:, half * 4:half * 4 + ncc, :], pb[2][0:64, 0:ncc * 128].rearrange("p (c v) -> p c v", v=128), AF.Copy, [pb[2]], [vtok])
                ksrc = sh[2] if not smp else sh[3]
                for c_ in range(nch):
                    tr(pbt[0:64, c_ * 128:(c_ + 1) * 128], ksrc[:, c_ * 64:(c_ + 1) * 64], identb(128), [ksrc], [pbt])
                cp('vector', ktok[:, 0:nch, :], pbt[0:64, 0:nch * 128].rearrange("p (c v) -> p c v", v=128), [pbt], [ktok])
                oT = pb[4]
                if not smp:
                    for c_ in range(N // 64):
                        first = (ti == 0 and c_ == 0)
                        cs_ = slice(c_ * 64, (c_ + 1) * 64)
                        mm(pb[3][0:64, 0:64], sh[2][:, cs_], sh[0][:, cs_], True, True, [sh[2], sh[0]], [pb[3]])
                        tt('vector', aTm[:, :], pb[3][0:64, 0:64], cst[0:64, C_T64:C_T64 + 64], ALU.mult, [pb[3]], [aTm])
                        if not first:
                            mm(oT[:, cs_], Sb[h][:, :], sh[1][:, cs_], True, False, [Sb[h], sh[1]], [oT])
                        mm(oT[:, cs_], vtok[:, c_, :], aTm[:, :], first, True, [vtok, aTm], [oT])
                        mm(pb[5][:, 0:128], ktok[:, c_, :], vtok[:, c_, :], True, True, [ktok, vtok], [pb[5]])
                        eL = sc[4][:, c_ * 64 + 63:c_ * 64 + 64]
                        eLm = E1[:, c_ * 64 + 63:c_ * 64 + 64]
                        if first:
                            ts('vector', S[h][:, :], pb[5][:, 0:128], eLm, None, ALU.mult, None, [pb[5], E1], [S[h]])
                        else:
                            ts('gpsimd', S[h][:, :], S[h][:, :], eL, None, ALU.mult, None, [S[h], sc[4]], [S[h]])
                            stt(S[h][:, :], pb[5][:, 0:128], eLm, S[h][:, :], ALU.mult, ALU.add, [pb[5], E1, S[h]], [S[h]])
                        act(Sb[h][:, :], S[h][:, :], AF.Copy, [S[h]], [Sb[h]])
                    if ti == NTI - 1:
                        P.dma('gpsimd', o_hgp[l, h], S[h][:, :], reads=[S[h]])
                else:
                    mm(pb[3][0:64, 0:64], sh[2][:, 0:64], sh[0][:, 0:64], True, True, [sh[2], sh[0]], [pb[3]])
                    tt('vector', aTm[:, :], pb[3][0:64, 0:64], cst[0:64, C_BD:C_BD + 64], ALU.mult, [pb[3]], [aTm])
                    mm(oT[:, 0:64], vtok[:, 0, :], aTm[:, :], True, False, [vtok, aTm], [oT])
                    for b in range(16):
                        slot = b % 4
                        k0, k0b = ('s0', slot), ('s0b', slot)
                        P.dma('sync', s0[:, slot, :], shg[l, b, h], writes=[k0])
                        cp('gpsimd', s0b[:, slot, :], s0[:, slot, :], [k0], [k0b])
                        mm(oT[:, 4 * b:4 * b + 4], s0b[:, slot, :], sh[1][:, 4 * b:4 * b + 4], False, b == 15, [k0b, sh[1]], [oT])
                        ts('gpsimd', khm[:, :], ktok[:, 0, :], cst[0:64, C_IND + b:C_IND + b + 1], None, ALU.mult, None, [ktok], [khm])
                        mm(pb[5][:, 0:128], khm[:, :], vtok[:, 0, :], True, True, [khm, vtok], [pb[5]])
                        stt(s0[:, slot, :], s0[:, slot, :], sc[4][:, 4 * b + 3:4 * b + 4], pb[5][:, 0:128], ALU.mult, ALU.add,
                            [k0, sc[4], pb[5]], [k0])
                        P.dma('gpsimd', o_hgs[l, b, h], s0[:, slot, :], reads=[k0])
                act(sc[7][:, 0:N], oT[:, 0:N], AF.Copy, [oT], [sc[7]])
                act(sh[4][:, 0:N], oT[:, 0:N], AF.Square, [oT], [sh[4]])
                mm(pb7[:, 0:N], onesb(128, 128), sh[4][:, 0:N], True, True, [sh[4]], [pb7])
                rsqrt_to(sc[8][:, 0:N], pb7[:, 0:N], 128, 1.0 / 128, [pb7], [sc[8]])
                stt(sc[9][:, 0:N], sc[7][:, 0:N], V(l, 36 + h), sc[8][:, 0:N], ALU.mult, ALU.mult, [sc[7], sc[8]], [sc[9]])
                psg = zchunk(wv, wkeys, 384, 128)
                act(sc[7][:, 0:N], psg[:, 0:N], AF.Silu, [psg], [sc[7]])
                tt('gpsimd', ys[:, 4 + h, 0:N], sc[9][:, 0:N], sc[7][:, 0:N], ALU.mult, [sc[9], sc[7]], [ys])

            if smp:
                chk('s%d_hgrn' % l)
            wv, wkeys = stage_win(l, [(O_QL, 192), (O_KV, 128), (O_KPE, 32), (O_KPE + 16, 16), (O_KPE, 16)])
            p_ = zchunk(wv, wkeys, 0, 128)
            act(sc[0][:, 0:N], p_[:, 0:N], AF.Copy, [p_], [sc[0]])
            act(sh[0][:, 0:N], p_[:, 0:N], AF.Square, [p_], [sh[0]])
            p_ = zchunk(wv, wkeys, 128, 64)
            act(sc[1][0:64, 0:N], p_[0:64, 0:N], AF.Copy, [p_], [sc[1]])
            act(sh[1][0:64, 0:N], p_[0:64, 0:N], AF.Square, [p_], [sh[1]])
            mm(pb7[:, 0:N], onesb(128, 128), sh[0][:, 0:N], True, False, [sh[0]], [pb7])
            mm(pb7[:, 0:N], onesb(64, 128), sh[1][0:64, 0:N], False, True, [sh[1]], [pb7])
            rsqrt_to(sc[2][:, 0:N], pb7[:, 0:N], 128, 1.0 / 192, [pb7], [sc[2]])
            stt(sh[0][:, 0:N], sc[0][:, 0:N], V(l, 40), sc[2][:, 0:N], ALU.mult, ALU.mult, [sc[0], sc[2]], [sh[0]])
            stt(sh[1][0:64, 0:N], sc[1][0:64, 0:N], V(l, 41, 64), sc[2][0:64, 0:N], ALU.mult, ALU.mult, [sc[1], sc[2]], [sh[1]])
            for h in range(4):
                for (dst, woff) in ((pb[2], 0), (pb[3], 384)):
                    mm(dst[0:96, 0:N], wq_b[:, 0, woff + h * 96:woff + (h + 1) * 96], sh[0][:, 0:N], True, False, [wq_b, sh[0]], [dst])
                    mm(dst[0:96, 0:N], wq_b[0:64, 1, woff + h * 96:woff + (h + 1) * 96], sh[1][0:64, 0:N], False, True, [wq_b, sh[1]], [dst])
                tt('vector', sc[3][0:96, 0:N], pb[2][0:96, 0:N], ropet[0:96, 0, 0:N], ALU.mult, [pb[2], ropet], [sc[3]])
                tt('vector', sc[4][0:96, 0:N], pb[3][0:96, 0:N], ropet[0:96, 1, 0:N], ALU.mult, [pb[3], ropet], [sc[4]])
                tt('gpsimd', sc[3][0:96, 0:N], sc[3][0:96, 0:N], sc[4][0:96, 0:N], ALU.add, [sc[3], sc[4]], [sc[3]])
                act(sh[2][0:96, 0:N], sc[3][0:96, 0:N], AF.Square, [sc[3]], [sh[2]])
                mm(pb7[0:96, 0:N], onesb(96, 96), sh[2][0:96, 0:N], True, True, [sh[2]], [pb7])
                rsqrt_to(sc[4][0:96, 0:N], pb7[0:96, 0:N], 96, 1.0 / 96, [pb7], [sc[4]])
                stt(QnT[:, h, 0:N], sc[3][0:96, 0:N], V(l, 43, 96), sc[4][0:96, 0:N], ALU.mult, ALU.mult, [sc[3], sc[4]], [QnT])
            p_ = zchunk(wv, wkeys, 192, 128)
            act(sc[0][:, 0:N], p_[:, 0:N], AF.Copy, [p_], [sc[0]])
            act(sh[2][:, 0:N], p_[:, 0:N], AF.Square, [p_], [sh[2]])
            mm(pb7[:, 0:N], onesb(128, 128), sh[2][:, 0:N], True, True, [sh[2]], [pb7])
            rsqrt_to(sc[1][:, 0:N], pb7[:, 0:N], 128, 1.0 / 128, [pb7], [sc[1]])
            stt(sc[2][:, 0:N], sc[0][:, 0:N], V(l, 42), sc[1][:, 0:N], ALU.mult, ALU.mult, [sc[0], sc[1]], [sc[2]])
            cp('gpsimd', sh[3][:, 0:N], sc[2][:, 0:N], [sc[2]], [sh[3]])
            nsub = len(subt)
            nn = subt[0][1]
            for j in range(nsub):
                tr(pb[2][0:nn, j * 128:(j + 1) * 128], sc[2][:, j * 128:j * 128 + nn], identf(128), [sc[2]], [pb[2]])
            cp('vector', sc[5][0:nn, 0:nsub * 128], pb[2][0:nn, 0:nsub * 128], [pb[2]], [sc[5]])
            c3 = sc[5][0:nn, 0:nsub * 128].rearrange("p (j r) -> p j r", r=128)
            if not smp:
                P.dma('gpsimd', o_latp[l, p0:p0 + N, :].rearrange("(j p) r -> p j r", p=128), c3, reads=[sc[5]])
                cp('gpsimd', ctokAll[:, ti * TPT:ti * TPT + TPT, 0:128], c3, [sc[5]], [ctokAll])
            else:
                P.dma('gpsimd', o_lats[l], sc[5][0:64, 0:128], reads=[sc[5]])
                cp('gpsimd', cnew[:, 0:128], sc[5][0:64, 0:128], [sc[5]], [cnew])
            pk = zchunk(wv, wkeys, 320, 32)
            pks = zchunk(wv, wkeys, 352, 32)
            tt('vector', sc[6][0:32, 0:N], pk[0:32, 0:N], ropet[0:32, 2, 0:N], ALU.mult, [pk, ropet], [sc[6]])
            tt('vector', sc[7][0:32, 0:N], pks[0:32, 0:N], ropet[0:32, 3, 0:N], ALU.mult, [pks, ropet], [sc[7]])
            tt('gpsimd', sc[6][0:32, 0:N], sc[6][0:32, 0:N], sc[7][0:32, 0:N], ALU.add, [sc[6], sc[7]], [sc[6]])
            cp('gpsimd', sh[4][0:32, 0:N], sc[6][0:32, 0:N], [sc[6]], [sh[4]])
            for j in range(nsub):
                tr(pb[3][0:nn, j * 32:(j + 1) * 32], sc[6][0:32, j * 128:j * 128 + nn], identf(32), [sc[6]], [pb[3]])
            cp('vector', sc[7][0:nn, 0:nsub * 32], pb[3][0:nn, 0:nsub * 32], [pb[3]], [sc[7]])
            if not smp:
                P.dma('gpsimd', o_ropep[l, p0:p0 + N, :].rearrange("(j p) r -> p j r", p=128),
                      sc[7][:, 0:TPT * 32].rearrange("p (j r) -> p j r", r=32), reads=[sc[7]])
            else:
                P.dma('gpsimd', o_ropes[l], sc[7][0:64, 0:32], reads=[sc[7]])
            for h in range(4):
                mm(pb[2][0:96, 0:N], wuke[:, h, :], sh[3][:, 0:N], True, False, [wuke, sh[3]], [pb[2]])
                mm(pb[2][0:96, 0:N], cbf[0:32, C_E:C_E + 96], sh[4][0:32, 0:N], False, True, [sh[4]], [pb[2]])
                act(sh[5][0:96, 0:N], pb[2][0:96, 0:N], AF.Square, [pb[2]], [sh[5]])
                mm(pb7[0:96, 0:N], onesb(96, 96), sh[5][0:96, 0:N], True, True, [sh[5]], [pb7])
                rsqrt_to(sc[8][0:96, 0:N], pb7[0:96, 0:N], 96, 1.0 / 96, [pb7], [sc[8]])
                dstK = KnAll[:, h, p0:p0 + N] if not smp else KnS[:, h, :]
                stt(dstK, pb[2][0:96, 0:N], V(l, 44, 96), sc[8][0:96, 0:N], ALU.mult, ALU.mult, [pb[2], sc[8]], [KnAll if not smp else KnS])
            if not smp:
                nk = TPT * ti + TPT
                for h in range(4):
                    for j in range(nk):
                        lo = max(0, j - TPT * ti) * 128
                        spb = pb[2 + (j % 2)]
                        pT = sh[j % 2]
                        mm(spb[:, lo:N], KnAll[:, h, j * 128:(j + 1) * 128], QnT[:, h, lo:N], True, True, [KnAll, QnT], [spb])
                        act(pT[:, lo:N], spb[:, lo:N], AF.Exp, [spb], [pT], scale=MLA_SCALE)
                        if j >= TPT * ti:
                            tt('gpsimd', pT[:, lo:lo + 128], pT[:, lo:lo + 128], cbf[:, C_TRI:C_TRI + 128], ALU.mult, [pT], [pT])
                        mm(pb[4][:, lo:N], ctokAll[:, j, 0:128], pT[:, lo:N], j == 0, j == nk - 1, [ctokAll, pT], [pb[4]])
                        mm(pb[5][:, lo:N], onesb(128, 128), pT[:, lo:N], j == 0, j == nk - 1, [pT], [pb[5]])
                    recip(sc[9][:, 0:N], pb[5][:, 0:N], [pb[5]], [sc[9]])
                    tt('vector', onT[:, h, 0:N], pb[4][:, 0:N], sc[9][:, 0:N], ALU.mult, [pb[4], sc[9]], [onT])
            else:
                chk('s%d_mla0' % l)
                P.barrier()
                mla_sample(l)
                P.barrier()
                chk('s%d_mla1' % l)
            for h in range(4):
                mm(pb[2][:, 0:N], wuv_b[:, h * 128:(h + 1) * 128], onT[:, h, 0:N], True, True, [wuv_b, onT], [pb[2]])
                act(sc[h][:, 0:N], pb[2][:, 0:N], AF.Copy, [pb[2]], [sc[h]])
            wv, wkeys = stage_win(l, [(O_SG + 1024, 512)])
            for h in range(4):
                psg = zchunk(wv, wkeys, h * 128, 128)
                act(sc[4 + h % 2][:, 0:N], psg[:, 0:N], AF.Silu, [psg], [sc[4 + h % 2]])
                tt('gpsimd', ys[:, 8 + h, 0:N], sc[h][:, 0:N], sc[4 + h % 2][:, 0:N], ALU.mult, [sc[h], sc[4 + h % 2]], [ys])

            wv, wkeys = stage_win(l, [(O_MQ, 512)])
            for h in range(4):
                p_ = zchunk(wv, wkeys, h * 128, 128)
                act(sc[0][:, 0:N], p_[:, 0:N], AF.Copy, [p_], [sc[0]])
                act(sh[0][:, 0:N], p_[:, 0:N], AF.Square, [p_], [sh[0]])
                mm(pb7[:, 0:N], onesb(128, 128), sh[0][:, 0:N], True, True, [sh[0]], [pb7])
                rsqrt_to(sc[1][:, 0:N], pb7[:, 0:N], 128, 1.0 / 128, [pb7], [sc[1]])
                stt(mqT[:, h, 0:N], sc[0][:, 0:N], V(l, 45), sc[1][:, 0:N], ALU.mult, ALU.mult, [sc[0], sc[1]], [mqT])
            if not smp:
                for h in range(4):
                    for mc in range(2):
                        spb = pb[2 + mc]
                        mm(spb[:, 0:N], mkT[:, h, mc * 128:(mc + 1) * 128], mqT[:, h, 0:N], True, True, [mkT, mqT], [spb])
                        act(sh[1 + mc][:, 0:N], spb[:, 0:N], AF.Exp, [spb], [sh[1 + mc]], scale=MEM_SCALE)
                        mm(pb[4][:, 0:N], mvtok[:, mc, h * 128:(h + 1) * 128], sh[1 + mc][:, 0:N], mc == 0, mc == 1, [mvtok, sh[1 + mc]], [pb[4]])
                        mm(pb[5][:, 0:N], onesb(128, 128), sh[1 + mc][:, 0:N], mc == 0, mc == 1, [sh[1 + mc]], [pb[5]])
                    recip(sc[2][:, 0:N], pb[5][:, 0:N], [pb[5]], [sc[2]])
                    tt('vector', sc[4 + h][:, 0:N], pb[4][:, 0:N], sc[2][:, 0:N], ALU.mult, [pb[4], sc[2]], [sc[4 + h]])
            else:
                chk('s%d_mem0' % l)
                P.barrier()
                mem_sample(l)
                P.barrier()
                chk('s%d_mem1' % l)
            wv, wkeys = stage_win(l, [(O_SG + 1536, 512)])
            for h in range(4):
                psg = zchunk(wv, wkeys, h * 128, 128)
                act(sc[h % 2][:, 0:N], psg[:, 0:N], AF.Silu, [psg], [sc[h % 2]])
                tt('gpsimd', ys[:, 12 + h, 0:N], sc[4 + h][:, 0:N], sc[h % 2][:, 0:N], ALU.mult, [sc[4 + h], sc[h % 2]], [ys])

            for n in range(4):
                for half in range(2):
                    wbv, wbk = wnext(('wbo', l, n, half))
                    wv, wkeys = stage_win(l, [(O_MG + n * 1024 + half * 512, 512)])
                    for dc in range(4):
                        dch = half * 4 + dc
                        pg = zchunk(wv, wkeys, dc * 128, 128)
                        g_, t_ = sc[2 * (dc % 2)], sc[2 * (dc % 2) + 1]
                        act(g_[:, 0:N], pg[:, 0:N], AF.Sigmoid, [pg], [g_])
                        pp = pb[2 + dc % 2]
                        for kc in range(4):
                            mm(pp[:, 0:N], wbv[:, kc, dc * 128:(dc + 1) * 128], ys[:, n * 4 + kc, 0:N], kc == 0, kc == 3, wbk + [ys], [pp])
                        if n == 0:
                            tt('vector', merged[:, dch, 0:N], pp[:, 0:N], g_[:, 0:N], ALU.mult, [pp, g_], [merged])
                        else:
                            tt('vector', t_[:, 0:N], pp[:, 0:N], g_[:, 0:N], ALU.mult, [pp, g_], [t_])
                            tt('gpsimd', merged[:, dch, 0:N], merged[:, dch, 0:N], t_[:, 0:N], ALU.add, [merged, t_], [merged])
            cp('vector', hnT[:, 0:4, 0:N], merged[:, 0:4, 0:N], [merged], [hnT])
            cp('gpsimd', hnT[:, 4:8, 0:N], merged[:, 4:8, 0:N], [merged], [hnT])
            for half in range(2):
                wov, wok = wnext(('wo', l, half))
                for j, (t, n) in enumerate(subt):
                    pp = pb[2 + j % 2]
                    for kc in range(8):
                        mm(pp[0:n, :], hnT[:, kc, j * 128:j * 128 + n], wov[:, kc, :], kc == 0, kc == 7, wok + [hnT], [pp])
                    xt_ = xtok[0:n, t, half * 512:(half + 1) * 512]
                    tt('vector', xt_, pp[0:n, :], xt_, ALU.add, [pp, ('x', t)], [('x', t)])
            if l == 1:
                for j, (t, n) in enumerate(subt):
                    if not smp:
                        P.dma('gpsimd', yp[t * 128:(t + 1) * 128, :], xtok[:, t, :], reads=[('x', t)])
                    else:
                        P.dma('gpsimd', ysm, xtok[0:64, 16, :], reads=[('x', 16)])
        def mla_sample(l):
            N = NS
            clat_l = clats[l]
            crope_l = cropes[l]
            mq_ = [merged[:, 2 * i:2 * i + 2, :].rearrange("p a b -> p (a b)") for i in range(3)]
            mk_ = [('mgs', i) for i in range(3)]
            for h in range(4):
                mm(pb[2][0:64, 0:64], KnS[:, h, :], QnT[:, h, 0:N], True, True, [KnS, QnT], [pb[2]])
                act(sc[0][0:64, 0:64], pb[2][0:64, 0:64], AF.Exp, [pb[2]], [sc[0]], scale=MLA_SCALE)
                tt('vector', pTn[:, :, h * 8:(h + 1) * 8], sc[0][0:64, 0:64].rearrange("p (a t) -> p a t", t=8),
                   cst[0:64, C_BD:C_BD + 64].rearrange("p (a t) -> p a t", t=8), ALU.mult, [sc[0]], [pTn])
                ts('vector', QgT[:, h, :], QnT[:, h, 0:N], V(l, 44, 96), None, ALU.mult, None, [QnT], [QgT])
                mm(pb[3][:, 0:64], wukT[:, h, :], QgT[0:64, h, :], True, True, [wukT, QgT], [pb[3]])
                cp('vector', QaC[:, :, h * 8:(h + 1) * 8], pb[3][:, 0:64].rearrange("p (a t) -> p a t", t=8), [pb[3]], [QaC])
                for i in range(4):
                    mm(pb[3][:, 64:128], cbf[64:96, C_Z + 160 - 32 * i:C_Z + 288 - 32 * i], QgT[64:96, h, :], True, True, [QgT], [pb[3]])
                    cp('vector', QaR[:, :, i, h * 8:(h + 1) * 8], pb[3][:, 64:128].rearrange("p (a t) -> p a t", t=8), [pb[3]], [QaR])
            P.op('gpsimd', lambda e: e.memset(gbf[:, :, 128:136], 1.0), (), [gbf])
            chk('x1')
            for bp in range(8):
                for ch in range(8):
                    P.dma('gpsimd', graw[:, :], clat_l, reads=[idx2], writes=[graw],
                          indirect=bass.IndirectOffsetOnAxis(ap=idx2[:, bp, ch:ch + 1], axis=0))
                    P.dma('gpsimd', rraw[:, :], crope_l, reads=[idx2], writes=[rraw],
                          indirect=bass.IndirectOffsetOnAxis(ap=idx2[:, bp, ch:ch + 1], axis=0))
                    g3 = graw[:, :].rearrange("p (a r) -> p a r", r=128)
                    act(gbf[:, 0:7, 0:128], g3[:, 0:7, :], AF.Copy, [graw], [gbf])
                    cp('vector', gbf[:, 7:16, 0:128], g3[:, 7:16, :], [graw], [gbf])
                    cp('vector', rbf[:, :, :], rraw[:, :].rearrange("p (a r) -> p a r", r=32), [rraw], [rbf])
                    tt('gpsimd', mq_[0], rraw[:, :], rraw[:, :], ALU.mult, [rraw], [mk_[0]])
                    P.op('vector', lambda e: e.tensor_reduce(out=stat[:, 0:16], in_=mq_[0].rearrange("p (a r) -> p a r", r=32),
                                                             axis=AX.X, op=ALU.add), [mk_[0]], [('stat', 0)])
                    chk('x2')
                    for g in range(4):
                        for i in range(4):
                            row = g * 4 + i
                            tr(pbt[:, i * 128:(i + 1) * 128], gbf[:, row, 0:128], identb(128), [gbf], [pbt])
                        tr(pbt[:, 512:640], rbf[:, g * 4:(g + 1) * 4, :].rearrange("p a r -> p (a r)"), identb(128), [rbf], [pbt])
                        cp('vector', cTs[:, :, :], pbt[:, 0:512].rearrange("p (a r) -> p a r", r=128), [pbt], [cTs])
                        cp('vector', rTs[:, :], pbt[:, 512:640], [pbt], [rTs])
                        chk('g1')
                        for i in range(4):
                            pk_ = pb[i // 2]
                            mm(pk_[:, (i % 2) * 256:(i % 2) * 256 + 256], cTs[:, i, :], wuk_b[:, :], True, True, [cTs, wuk_b], [pk_])
                            if i == 0:
                                mm(pb[2][:, 0:128], rTs[:, :], QaR[:, bp, :, :].rearrange("p a t -> p (a t)"), True, False, [rTs, QaR], [pb[2]])
                            mm(pb[2][:, i * 32:(i + 1) * 32], cTs[:, i, :], QaC[:, bp, :], False, i == 3, [cTs, QaC], [pb[2]])
                        chk('g2')
                        act(mq_[1], pb[0][:, 0:512], AF.Square, [pb[0]], [mk_[1]])
                        act(mq_[2], pb[1][:, 0:512], AF.Square, [pb[1]], [mk_[2]])
                        P.op('vector', lambda e: e.tensor_reduce(out=stat[:, 16:24], in_=mq_[1].rearrange("p (a r) -> p a r", r=64),
                                                                 axis=AX.X, op=ALU.add), [mk_[1]], [('stat', 1)])
                        P.op('vector', lambda e: e.tensor_reduce(out=stat[:, 24:32], in_=mq_[2].rearrange("p (a r) -> p a r", r=64),
                                                                 axis=AX.X, op=ALU.add), [mk_[2]], [('stat', 1)])
                        tt('vector', stat[:, 32:48].rearrange("p (a h) -> p a h", h=4), stat[:, 16:32].rearrange("p (a h) -> p a h", h=4),
                           stat[:, g * 4:(g + 1) * 4].rearrange("p (a o) -> p a o", o=1).broadcast_to([128, 4, 4]), ALU.add,
                           [('stat', 0), ('stat', 1)], [('stat', 2)])
                        rsqrt_to(stat[:, 32:48], stat[:, 32:48], 128, 1.0 / 96, [('stat', 2)], [('stat', 2)])
                        tt('vector', sc[4][:, 0:128].rearrange("p (a t) -> p a t", t=8), pb[2][:, 0:128].rearrange("p (a t) -> p a t", t=8),
                           stat[:, 32:48].rearrange("p (a o) -> p a o", o=1).broadcast_to([128, 16, 8]), ALU.mult,
                           [pb[2], ('stat', 2)], [sc[4]])
                        chk('g3')
                        act(sc[5][:, 0:128], sc[4][:, 0:128], AF.Exp, [sc[4]], [sc[5]], scale=MLA_SCALE)
                        tt('gpsimd', pTs[:, :, :], sc[5][:, 0:128].rearrange("p (a t) -> p a t", t=32),
                           cst[:, C_PM:C_PM + 32].rearrange("p (o t) -> p o t", o=1).broadcast_to([128, 4, 32]), ALU.mult, [sc[5]], [pTs])
                        chk('g4')
                        for i in range(4):
                            row = g * 4 + i
                            mm(pb[4][0:32, 0:130], pTs[:, i, :], gbf[:, row, 0:130], ch == 0 and g == 0 and i == 0, False, [pTs, gbf], [pb[4]])
                        chk('x3')
                mm(pb[4][0:32, 0:130], pTn[:, bp, :], cnew[:, 0:130], False, True, [pTn, cnew], [pb[4]])
                recip(small[0:32, 1:2], pb[4][0:32, 128:129], [pb[4]], ['sm1'])
                ts('vector', sc[6][0:32, 0:128], pb[4][0:32, 0:128], small[0:32, 1:2], None, ALU.mult, None, [pb[4], 'sm1'], [sc[6]])
                tr(pb[5][:, 0:32], sc[6][0:32, 0:128], identf(32), [sc[6]], [pb[5]])
                cp('vector', onT[:, :, bp * 8:(bp + 1) * 8], pb[5][:, 0:32].rearrange("p (h t) -> p h t", t=8), [pb[5]], [onT])
                chk('x4')

        def mem_sample(l):
            for b in range(16):
                P.dma('sync', mraw[:, :, :], cmk[l, b].rearrange("(mc p) d -> p mc d", p=128), writes=[mraw])
                act(mkb[:], mraw[:], AF.Copy, [mraw], [mkb])
                P.dma('sync', mraw[:, :, :], cmv[l, b].rearrange("(mc p) d -> p mc d", p=128), writes=[mraw])
                cp('vector', mvb[:], mraw[:], [mraw], [mvb])
                for h in range(4):
                    for mc in range(2):
                        tr(pbt[:, (h * 2 + mc) * 128:(h * 2 + mc + 1) * 128], mkb[:, mc, h * 128:(h + 1) * 128], identb(128), [mkb], [pbt])
                cp('vector', mkTs[:, :, :], pbt[:, 0:1024].rearrange("p (h m) -> p h m", m=256), [pbt], [mkTs])
                for h in range(4):
                    for mc in range(2):
                        o4 = (h * 2 + mc) * 4
                        mm(pb[2][:, o4:o4 + 4], mkTs[:, h, mc * 128:(mc + 1) * 128], mqT[:, h, 4 * b:4 * b + 4], True, True, [mkTs, mqT], [pb[2]])
                act(pms[:, 0:32], pb[2][:, 0:32], AF.Exp, [pb[2]], [pms], scale=MEM_SCALE)
                for h in range(4):
                    for mc in range(2):
                        o4 = (h * 2 + mc) * 4
                        mm(pb[4][:, h * 64 + 4 * b:h * 64 + 4 * b + 4], mvb[:, mc, h * 128:(h + 1) * 128], pms[:, o4:o4 + 4], mc == 0, mc == 1,
                           [mvb, pms], [pb[4]])
                    for mc in range(2):
                        o4 = (h * 2 + mc) * 4
                        mm(pb[5][:, h * 64 + 4 * b:h * 64 + 4 * b + 4], onesb(128, 128), pms[:, o4:o4 + 4], mc == 0, mc == 1, [pms], [pb[5]])
            for h in range(4):
                recip(sc[2][:, 0:64], pb[5][:, h * 64:(h + 1) * 64], [pb[5]], [sc[2]])
                tt('vector', sc[4 + h][:, 0:64], pb[4][:, h * 64:(h + 1) * 64], sc[2][:, 0:64], ALU.mult, [pb[4], sc[2]], [sc[4 + h]])

        P.barrier()
        class _Stop(Exception):
            pass

        def chk(tag):
            if STOP == tag:
                raise _Stop()
        try:
            for l in range(2):
                load_small(l)
                mem_kv(l)
                chk('memkv%d' % l)
                for ti in range(NTI):
                    tile_pass(l, 'p', ti)
                    chk('p%d_%d' % (l, ti))
                P.barrier()
                tile_pass(l, 's', 0)
                P.barrier()
                chk('s%d' % l)
        except _Stop:
            pass
        P.finish()
        print('[build] inst counts', P.cnt, 'dma vals', max(P.dma_val), flush=True)
    return nc


_NC_CACHE = {}


def kernel(x_prompt, x_sample, mem_prompt, cache_mla_latent, cache_mla_rope, page_table, state_hgrn, state_conv,
           cache_mem_k, cache_mem_v, norm_gain, w_in, conv_w, hgrn_lb, hgrn_norm, mla_q_norm, mla_w_uq,
           mla_kv_norm, mla_w_uk, mla_w_uv, mla_q_gain, mla_k_gain, mem_norm, mem_w_k, mem_w_v, mem_q_gain,
           mem_k_gain, w_branch_out, w_out):
    f = lambda a: np.ascontiguousarray(np.asarray(a, dtype=np.float32))
    cst, rope = make_consts()
    vecs = np.zeros((128, 2, 48), np.float32)
    ng, mn, cw, lb, hn = f(norm_gain), f(mem_norm), f(conv_w), f(hgrn_lb), f(hgrn_norm)
    for l in range(2):
        vecs[:, l, 0:8] = ng[l].reshape(8, 128).T
        vecs[:, l, 8:16] = mn[l].reshape(8, 128).T
        for j in range(3):
            vecs[:, l, 16 + 4 * j:20 + 4 * j] = cw[l, j].reshape(4, 128).T
        vecs[:, l, 28:32] = lb[0].reshape(4, 128).T
        vecs[:, l, 32:36] = lb[1].reshape(4, 128).T
        vecs[:, l, 36:40] = hn[l].reshape(4, 128).T
        qn = f(mla_q_norm)[l]
        vecs[:, l, 40] = qn[0:128]
        vecs[0:64, l, 41] = qn[128:192]
        vecs[:, l, 42] = f(mla_kv_norm)[l]
        vecs[0:96, l, 43] = f(mla_q_gain)[l]
        vecs[0:96, l, 44] = f(mla_k_gain)[l]
        vecs[:, l, 45] = f(mem_q_gain)[l]
        vecs[:, l, 46] = f(mem_k_gain)[l]
        vecs[:, l, 47] = EPS
    clat = f(cache_mla_latent).reshape(2, NPHYS * 8, 2048)
    crope = f(cache_mla_rope).reshape(2, NPHYS * 8, 512)
    pt = np.asarray(page_table, dtype=np.int32)
    shared = dict(clat0=clat[0], clat1=clat[1], crope0=crope[0], crope1=crope[1],
                  w_in=f(w_in), w_bo=f(w_branch_out), w_o=f(w_out), w_uq=f(mla_w_uq), w_uk=f(mla_w_uk), w_uv=f(mla_w_uv),
                  mwk=f(mem_w_k), mwv=f(mem_w_v), vecs=vecs, cst=cst, rope=rope)
    xp_, xs_, mp_ = f(x_prompt), f(x_sample), f(mem_prompt)
    shg_, scv_, cmk_, cmv_ = f(state_hgrn), f(state_conv), f(cache_mem_k), f(cache_mem_v)
    in_maps = []
    for c in range(8):
        bs = slice(16 * c, 16 * c + 16)
        m = dict(shared)
        m.update(xp=xp_[c], xs=xs_[bs].reshape(64, D), memp=mp_[c],
                 ptab=np.ascontiguousarray(pt[bs].reshape(8, 128).T),
                 shg=np.ascontiguousarray(shg_[:, bs]), scv=np.ascontiguousarray(scv_[:, bs]),
                 cmk=np.ascontiguousarray(cmk_[:, bs].reshape(2, 16, 256, 512)),
                 cmv=np.ascontiguousarray(cmv_[:, bs].reshape(2, 16, 256, 512)))
        in_maps.append(m)
    if 'nc' not in _NC_CACHE:
        _NC_CACHE['nc'] = build_nc()
    ncr = CFG['ncores']
    res = run_bass_kernel_spmd(_NC_CACHE['nc'], in_maps[:ncr], core_ids=list(range(ncr)))
    R = list(res.results)
    while len(R) < 8:
        R.append(R[0])
    cat = lambda k: np.stack([r[k] for r in R], 0)
    y_prompt = cat("yp")
    y_sample = cat("ysm").reshape(128, 4, D)
    latp = cat("o_latp").transpose(1, 0, 2, 3)
    ropep = cat("o_ropep").transpose(1, 0, 2, 3)
    hgp = cat("o_hgp").transpose(1, 0, 2, 3, 4)
    cvp = cat("o_cvp").transpose(1, 0, 2, 3)
    mkp = cat("o_mkp").transpose(1, 0, 2, 3).reshape(2, 8, 256, 4, 128)
    mvp = cat("o_mvp").transpose(1, 0, 2, 3).reshape(2, 8, 256, 4, 128)
    lats = cat("o_lats").transpose(1, 0, 2, 3).reshape(2, 128, 4, 128)
    ropes = cat("o_ropes").transpose(1, 0, 2, 3).reshape(2, 128, 4, 32)
    hgs = cat("o_hgs").transpose(1, 0, 2, 3, 4, 5).reshape(2, 128, 4, 128, 128)
    cvs = cat("o_cvs").transpose(1, 0, 2, 3, 4).reshape(2, 128, 2, 512)
    outs = (y_prompt, y_sample, latp, ropep, hgp, cvp, mkp, mvp, lats, ropes, hgs, cvs)
    return tuple(np.ascontiguousarray(o, dtype=np.float32) for o in outs)
```

```python
import numpy as np
from contextlib import ExitStack
import concourse.bass as bass
import concourse.mybir as mybir
from concourse.bass_utils import run_bass_kernel_spmd

F32 = mybir.dt.float32
BF16 = mybir.dt.bfloat16
I32 = mybir.dt.int32
AF = mybir.ActivationFunctionType
ALU = mybir.AluOpType
AX = mybir.AxisListType
ENGS = ['sync', 'scalar', 'vector', 'gpsimd', 'tensor']


class Prog:
    def __init__(self, nc, ndma=44):
        self.nc = nc
        self.q = {e: [] for e in ENGS}
        self.cnt = {e: 0 for e in ENGS}
        self.waited = {e: {} for e in ENGS}
        self.lastw = {}
        self.readers = {}
        self.ndma = ndma
        self.dma_val = [0] * ndma
        self.dma_i = 0
        self.es = None
        self.sems = {}
        self.all_dma_events = {}

    def sb(self, name, shape, dt):
        return self.es.enter_context(self.nc.sbuf_tensor(name, list(shape), dt))

    def ps(self, name, shape, dt):
        return self.es.enter_context(self.nc.psum_tensor(name, list(shape), dt))

    @staticmethod
    def _key(t):
        if isinstance(t, (str, tuple)):
            return t
        return t.name

    def _deps(self, eng, reads, writes):
        deps = []
        for k in reads:
            k = self._key(k)
            if k in self.lastw:
                deps.append(self.lastw[k])
        for k in writes:
            k = self._key(k)
            if k in self.lastw:
                deps.append(self.lastw[k])
            deps += self.readers.get(k, [])
        waits = {}
        for (s, v) in deps:
            if s == 'tensor' and eng == 'tensor':
                continue
            if self.waited[eng].get(s, 0) < v:
                waits[s] = max(waits.get(s, 0), v)
        for s, v in waits.items():
            self.waited[eng][s] = v
        return waits

    def _commit(self, ev, reads, writes):
        for k in reads:
            k = self._key(k)
            self.readers.setdefault(k, []).append(ev)
        for k in writes:
            k = self._key(k)
            self.lastw[k] = ev
            self.readers[k] = []

    def op(self, eng, fn, reads=(), writes=()):
        waits = self._deps(eng, reads, writes)
        self.cnt[eng] += 1
        ev = (eng, self.cnt[eng])
        self.q[eng].append((waits, fn, eng, 1))
        self._commit(ev, reads, writes)

    def dma(self, eng, out, in_, reads=(), writes=(), indirect=None, noncontig=False, eoff=0):
        waits = self._deps(eng, reads, writes)
        i = self.dma_i
        self.dma_i = (self.dma_i + 1) % self.ndma
        s = ('d', i)
        if self.dma_val[i] > 0 and self.waited[eng].get(s, 0) < self.dma_val[i]:
            waits[s] = self.dma_val[i]
            self.waited[eng][s] = self.dma_val[i]
        self.dma_val[i] += 16
        ev = (s, self.dma_val[i])
        self.all_dma_events[s] = self.dma_val[i]
        if indirect is not None:
            fn = lambda e: e.indirect_dma_start(out=out, out_offset=None, in_=in_, in_offset=indirect, element_offset=eoff)
        elif noncontig:
            fn = lambda e: e.dma_start(out=out, in_=in_, allow_slow_non_contiguous=True)
        else:
            fn = lambda e: e.dma_start(out=out, in_=in_)
        self.q[eng].append((waits, fn, s, 16))
        self._commit(ev, reads, writes)

    def barrier(self):
        for e in ENGS:
            w = {}
            for s_, v in self.all_dma_events.items():
                if self.waited[e].get(s_, 0) < v:
                    w[s_] = v
                    self.waited[e][s_] = v
            if w:
                self.q[e].append((w, None, None, 0))
        for e in ENGS:
            for o in ENGS:
                if o != e and self.cnt[o] > self.waited[e].get(o, 0):
                    self.q[e].append(({o: self.cnt[o]}, None, None, 0))
                    self.waited[e][o] = self.cnt[o]

    def finish(self):
        nc = self.nc
        fin = dict(self.all_dma_events)
        for e in ENGS:
            if e != 'sync' and self.cnt[e] > 0:
                fin[e] = self.cnt[e]
        self.q['sync'].append((fin, None, None, 0))
        for e in ENGS:
            self.sems[e] = self.es.enter_context(nc.semaphore("s_" + e))
        for i in range(self.ndma):
            self.sems[('d', i)] = self.es.enter_context(nc.semaphore("s_d%d" % i))
        block = self.es.enter_context(nc.Block())

        def replay(name):
            def f(eng):
                for (waits, fn, incs, incv) in self.q[name]:
                    for s, v in waits.items():
                        eng.wait_ge(self.sems[s], v)
                    if fn is not None:
                        ins = fn(eng)
                        ins.then_inc(self.sems[incs], incv)
            return f
        block.sync(replay('sync'))
        block.scalar(replay('scalar'))
        block.vector(replay('vector'))
        block.gpsimd(replay('gpsimd'))
        block.tensor(replay('tensor'))


D = 1024
SEQ = 2048
NT = 256
TPT = NT // 128
NTI = 2048 // NT
NS = 64
BRW = 512
EPS = 1e-6
O_CH, O_CB, O_CC, O_HQ, O_HF, O_HI, O_QL, O_KV, O_KPE, O_MQ, O_SG, O_MG = (
    0, 512, 1024, 1536, 2048, 2560, 3072, 3264, 3392, 3424, 3936, 5984)
NIN = 10080
NPHYS = 10240
PAST = 8192
C_ID = 0
C_TRI = 128
C_T64 = 256
C_BD = 320
C_PM = 384
C_IND = 416
C_SEGP = 432
C_SEGS = 944
C_E = 1008
C_SEL = 1104
C_ONE = 1136
C_Z = 1264
CW = 1552


def make_consts():
    c = np.zeros((128, CW), np.float32)
    c[:, C_ID:C_ID + 128] = np.eye(128)
    k = np.arange(128)
    c[:, C_TRI:C_TRI + 128] = (k[:, None] <= k[None, :])
    s = np.arange(64)
    c[:64, C_T64:C_T64 + 64] = (s[:, None] <= s[None, :])
    c[:64, C_BD:C_BD + 64] = (s[:, None] <= s[None, :]) & ((s[:, None] // 4) == (s[None, :] // 4))
    pm = np.zeros((128, 4, 2, 4), np.float32)
    pm[:64, :, 0, :] = 1
    pm[64:, :, 1, :] = 1
    c[:, C_PM:C_PM + 32] = pm.reshape(128, 32)
    c[:64, C_IND:C_IND + 16] = ((s[:, None] // 4) == np.arange(16)[None, :])
    c[:, C_SEGP:C_SEGP + 512] = (np.arange(512) % 64 != 0)[None, :]
    c[:, C_SEGS:C_SEGS + 64] = (np.arange(64) % 4 != 0)[None, :]
    for j in range(32):
        c[j, C_E + 64 + j] = 1
        c[64 + j, C_SEL + j] = 1
    c[:, C_ONE:C_ONE + 128] = 1
    for p_ in range(128):
        c[p_, C_Z + 96 + p_] = 1
    half = 16
    freqs = (10000.0 ** (-np.arange(half, dtype=np.float32) / half)).astype(np.float32)
    pos = np.concatenate([np.arange(SEQ), np.tile(PAST + np.arange(4), 16)]).astype(np.float32)
    ang = (pos[None, :] * freqs[:, None]).astype(np.float32)
    cos, sin = np.cos(ang).astype(np.float32), np.sin(ang).astype(np.float32)
    cc = np.concatenate([cos, cos], 0)
    ss = np.concatenate([-sin, sin], 0)
    r = np.zeros((128, 4, SEQ + NS), np.float32)
    r[:64, 0] = 1
    r[64:96, 0] = cc
    r[64:96, 1] = ss
    r[:32, 2] = cc
    r[:32, 3] = ss
    return c, r


CFG = {'nphys': NPHYS, 'stop': None, 'ncores': 8}


def build_nc():
    nc = bass.Bass("TRN2", target_bir_lowering=False)
    NPH = CFG['nphys']
    STOP = CFG['stop']

    def din(name, shape, dt=F32):
        return nc.dram_tensor(name, list(shape), dt, kind="ExternalInput").ap()

    def dout(name, shape):
        return nc.dram_tensor(name, list(shape), F32, kind="ExternalOutput").ap()

    xp = din("xp", [SEQ, D]); xs = din("xs", [NS, D]); memp = din("memp", [256, D])
    clats = [din("clat%d" % i, [NPH * 8, 2048]) for i in range(2)]; cropes = [din("crope%d" % i, [NPH * 8, 512]) for i in range(2)]
    ptab = din("ptab", [128, 8], I32)
    shg = din("shg", [2, 16, 4, 128, 128]); scv = din("scv", [2, 16, 2, 512])
    cmk = din("cmk", [2, 16, 256, 512]); cmv = din("cmv", [2, 16, 256, 512])
    w_in = din("w_in", [2, D, NIN]); w_bo = din("w_bo", [2, 4, 512, D]); w_o = din("w_o", [2, D, D])
    w_uq = din("w_uq", [2, 192, 384]); w_uk = din("w_uk", [2, 128, 256]); w_uv = din("w_uv", [2, 128, 512])
    mwk = din("mwk", [2, D, 512]); mwv = din("mwv", [2, D, 512])
    vecs = din("vecs", [128, 2, 48])
    cst_d = din("cst", [128, CW]); rope_d = din("rope", [128, 4, SEQ + NS])

    yp = dout("yp", [SEQ, D]); ysm = dout("ysm", [NS, D])
    o_latp = dout("o_latp", [2, SEQ, 128]); o_ropep = dout("o_ropep", [2, SEQ, 32])
    o_hgp = dout("o_hgp", [2, 4, 128, 128]); o_cvp = dout("o_cvp", [2, 2, 512])
    o_mkp = dout("o_mkp", [2, 256, 512]); o_mvp = dout("o_mvp", [2, 256, 512])
    o_lats = dout("o_lats", [2, NS, 128]); o_ropes = dout("o_ropes", [2, NS, 32])
    o_hgs = dout("o_hgs", [2, 16, 4, 128, 128]); o_cvs = dout("o_cvs", [2, 16, 2, 512])

    P = Prog(nc)
    with ExitStack() as es:
        P.es = es
        sb, ps = P.sb, P.ps
        xtok = sb("xtok", [128, 17, D], F32)
        cst = sb("cstt", [128, CW], F32)
        cbf = sb("cbf", [128, CW], BF16)
        ropet = sb("ropet", [128, 4, NT], F32)
        vec = sb("vec", [128, 2, 48], F32)
        lbv = sb("lbv", [128, 2, 8], F32)
        wsts = [sb("wstA", [128, 2048], F32), sb("wstB", [128, 2048], F32)]
        wbfs = [sb("wbf0", [128, 4096], BF16), sb("wbf1", [128, 4096], BF16), sb("wbf2", [128, 4096], BF16)]
        hnT = sb("hnT", [128, 8, NT], BF16)
        ys = sb("ys", [128, 16, NT], BF16)
        merged = sb("merged", [128, 8, NT], F32)
        un = sb("un", [128, 10368], BF16)
        unf = un[:, :].bitcast(F32)

        class View:
            def __init__(self, ap, name):
                self.ap = ap
                self.name = name

            def __getitem__(self, k):
                return self.ap[k]
        KnAll = View(un[0:96, 0:8192].rearrange("p (h t) -> p h t", t=SEQ), "KnAll")
        ctokAll = View(un[:, 8192:8192 + 2176].rearrange("p (j r) -> p j r", r=136), "ctokAll")
        S = [sb("Sst%d" % i, [128, 128], F32) for i in range(4)]
        Sb = [sb("Sbf%d" % i, [128, 128], BF16) for i in range(4)]
        sc = [sb("sc%d" % i, [128, NT], F32) for i in range(10)]
        sh = [sb("sh%d" % i, [128, NT], BF16) for i in range(6)]
        uext = sb("uext", [128, 4, NT + 2], F32)
        vtok = sb("vtok", [64, 8, 128], BF16)
        ktok = sb("ktok", [64, 8, 128], BF16)
        aTm = sb("aTm", [64, 64], BF16)
        small = sb("small", [128, 64], F32)
        mkT = sb("mkT", [128, 4, 256], BF16)
        mvtok = sb("mvtok", [128, 2, 512], BF16)
        wq_b = sb("wq_b", [128, 2, 768], BF16)
        wuke = sb("wuke", [128, 4, 96], BF16)
        wuk_b = sb("wuk_b", [128, 256], BF16)
        wukT = sb("wukT", [64, 4, 128], BF16)
        wuv_b = sb("wuv_b", [128, 512], BF16)
        QnT = sb("QnT", [96, 4, NT], BF16)
        onT = sb("onT", [128, 4, NT], BF16)
        mqT = sb("mqT", [128, 4, NT], BF16)
        graw = View(unf[:, 0:2048], "graw")
        rraw = View(unf[:, 2048:2560], "rraw")
        gbf = View(un[:, 5120:5120 + 2176].rearrange("p (a r) -> p a r", r=136), "gbf")
        rbf = View(un[:, 7296:7296 + 512].rearrange("p (a r) -> p a r", r=32), "rbf")
        cTs = View(un[:, 7808:7808 + 512].rearrange("p (a r) -> p a r", r=128), "cTs")
        rTs = View(un[:, 8320:8320 + 128], "rTs")
        mraw = View(unf[:, 0:1024].rearrange("p (a r) -> p a r", r=512), "mraw")
        mkb = View(un[:, 2048:3072].rearrange("p (a r) -> p a r", r=512), "mkb")
        mvb = View(un[:, 3072:4096].rearrange("p (a r) -> p a r", r=512), "mvb")
        mkTs = View(un[:, 4096:5120].rearrange("p (a r) -> p a r", r=256), "mkTs")
        s0 = View(unf[:, 4416:4416 + 512].rearrange("p (a r) -> p a r", r=128), "s0")
        s0b = View(un[:, 9856:9856 + 512].rearrange("p (a r) -> p a r", r=128), "s0b")
        idx = sb("idx", [128, 8], I32)
        idx2 = sb("idx2", [128, 8, 8], I32)
        QaC = sb("QaC", [128, 8, 32], BF16)
        QaR = sb("QaR", [128, 8, 4, 32], BF16)
        QgT = sb("QgT", [96, 4, NS], BF16)
        pTn = sb("pTn", [64, 8, 32], BF16)
        cnew = sb("cnew", [64, 136], BF16)
        pTs = sb("pTs", [128, 4, 32], BF16)
        stat = sb("stat", [128, 64], F32)
        khm = sb("khm", [64, 128], BF16)
        pms = sb("pms", [128, 32], BF16)
        pb = [ps("pb%d" % i, [128, 512], F32) for i in range(6)]
        pbt = ps("pbt", [128, 1024], BF16)
        pb7 = ps("pb7", [128, 512], F32)

        def act(out, in_, func, r, w, scale=1.0, bias=None):
            if bias is None:
                P.op('scalar', lambda e: e.activation(out=out, in_=in_, func=func, scale=scale), r, w)
            else:
                P.op('scalar', lambda e: e.activation(out=out, in_=in_, func=func, scale=scale, bias=bias), r, w)

        def tt(eng, out, in0, in1, op, r, w):
            P.op(eng, lambda e: e.tensor_tensor(out=out, in0=in0, in1=in1, op=op), r, w)

        def ts(eng, out, in0, s1, s2, op0, op1, r, w):
            if s2 is None:
                P.op(eng, lambda e: e.tensor_scalar(out=out, in0=in0, scalar1=s1, scalar2=None, op0=op0), r, w)
            else:
                P.op(eng, lambda e: e.tensor_scalar(out=out, in0=in0, scalar1=s1, scalar2=s2, op0=op0, op1=op1), r, w)

        def stt(out, in0, scalar, in1, op0, op1, r, w):
            P.op('vector', lambda e: e.scalar_tensor_tensor(out=out, in0=in0, scalar=scalar, in1=in1, op0=op0, op1=op1), r, w)

        def cp(eng, out, in_, r, w):
            P.op(eng, lambda e: e.tensor_copy(out=out, in_=in_), r, w)

        def mm(out, lhsT, rhs, start, stop, r, w):
            P.op('tensor', lambda e: e.matmul(out, lhsT=lhsT, rhs=rhs, start=start, stop=stop), r, w)

        def tr(out, in_, ident, r, w):
            P.op('tensor', lambda e: e.transpose(out=out, in_=in_, identity=ident), r, w)

        def recip(out, in_, r, w):
            P.op('vector', lambda e: e.reciprocal(out=out, in_=in_), r, w)

        def rsqrt_to(out, in_, n, scale, r, w):
            act(out, in_, AF.Ln, r, w, scale=scale, bias=vec[0:n, 0, 47:48])
            act(out, out, AF.Exp, w, w, scale=-0.5)

        identb = lambda n: cbf[0:n, C_ID:C_ID + n]
        identf = lambda n: cst[0:n, C_ID:C_ID + n]
        onesb = lambda k, m: cbf[0:k, C_ONE:C_ONE + m]

        def V(l, j, n=128):
            return vec[0:n, l, j:j + 1]

        P.dma('sync', cst[:], cst_d, writes=[cst])
        P.dma('sync', vec[:], vecs, writes=[vec])
        P.dma('sync', idx[:], ptab, writes=[idx])
        cp('vector', cbf[:], cst[:], [cst], [cbf])
        for ch in range(8):
            ts('vector', idx2[:, :, ch], idx[:], 8.0, float(ch), ALU.mult, ALU.add, [idx], [idx2])
        pass
        P.op('gpsimd', lambda e: e.memset(cnew[:, 128:136], 1.0), (), [cnew])
        pass
        for t in range(16):
            P.dma('sync', xtok[:, t, :], xp[t * 128:(t + 1) * 128, :], writes=[('x', t)])
        P.dma('sync', xtok[0:64, 16, :], xs, writes=[('x', 16)])
        mbf = merged[:, :, :].rearrange("p a b -> p (a b)").bitcast(BF16)
        junk = View(mbf[:, 2048:3072], "merged")
        hnb = View(mbf[:, 3072:4096], "merged")
        P.op('gpsimd', lambda e: e.memset(lbv[:], 0.0), (), [lbv])
        tt('vector', lbv[:, 1, 0:4], vec[:, 0, 32:36], vec[:, 0, 28:32], ALU.subtract, [vec], [lbv])
        act(lbv[:, 1, 0:4], lbv[:, 1, 0:4], AF.Sigmoid, [lbv], [lbv])
        ts('vector', lbv[:, :, 4:8], lbv[:, :, 0:4], -1.0, 1.0, ALU.mult, ALU.add, [lbv], [lbv])
        P.op('gpsimd', lambda e: e.memset(wq_b[:], 0.0), (), [wq_b])
        P.op('gpsimd', lambda e: e.memset(wuke[:], 0.0), (), [wuke])

        wctr = [0]

        def WSTK(k):
            return [('wst', k, i) for i in range(8)]

        def wk(wb):
            return [(wb.name, 0), (wb.name, 1)]

        def stage(parts, nk, W):
            k = wctr[0] % 3
            wctr[0] += 1
            wb = wbfs[k]
            hk = nk // 2
            for hf in range(2):
                wsv = wsts[hf][:, 0:hk * W].rearrange("p (k n) -> p k n", n=W)
                for i, (off, n, src) in enumerate(parts):
                    P.dma('sync', wsv[:, :, off:off + n], src[:, hf * hk:(hf + 1) * hk, :], writes=[('wst', hf, i)])
            act(wb[:, 0:hk * W], wsts[0][:, 0:hk * W], AF.Copy, WSTK(0), [(wb.name, 0)])
            cp('vector', wb[:, hk * W:nk * W], wsts[1][:, 0:hk * W], WSTK(1), [(wb.name, 1)])
            return wb[:, 0:nk * W].rearrange("p (k n) -> p k n", n=W), wk(wb)

        def stage_rows(src2d, nk, ncols):
            return stage([(0, ncols, src2d.rearrange("(k p) n -> p k n", p=128))], nk, ncols)

        def tile_plan(l):
            pl = []
            for c in range(4):
                pl.append(('win', l, ((O_CH + c * 128, 128), (O_CB + c * 128, 128), (O_CC + c * 128, 128), (O_SG + c * 128, 128))))
            for h in range(4):
                pl.append(('win', l, ((O_HQ + h * 128, 128), (O_HF + h * 128, 128), (O_HI + h * 128, 128), (O_SG + 512 + h * 128, 128))))
            pl.append(('win', l, ((O_QL, 192), (O_KV, 128), (O_KPE, 32), (O_KPE + 16, 16), (O_KPE, 16))))
            pl.append(('win', l, ((O_SG + 1024, 512),)))
            pl.append(('win', l, ((O_MQ, 512),)))
            pl.append(('win', l, ((O_SG + 1536, 512),)))
            for n in range(4):
                for half in range(2):
                    pl.append(('wbo', l, n, half))
                    pl.append(('win', l, ((O_MG + n * 1024 + half * 512, 512),)))
            for half in range(2):
                pl.append(('wo', l, half))
            return pl

        PLAN = []
        for l_ in range(2):
            PLAN += [('mwk', l_), ('mwv', l_)]
            for t_ in range(NTI + 1):
                PLAN += tile_plan(l_)
        wp_state = {'i': 0, 'issued': {}}

        def wp_issue(j):
            if j < len(PLAN) and j not in wp_state['issued']:
                d = PLAN[j]
                if d[0] == 'win':
                    r_ = stage_win_now(d[1], list(d[2]))
                elif d[0] == 'wbo':
                    r_ = stage_rows(w_bo[d[1], d[2]][:, d[3] * 512:(d[3] + 1) * 512], 4, 512)
                elif d[0] == 'wo':
                    r_ = stage_rows(w_o[d[1]][:, d[2] * 512:(d[2] + 1) * 512], 8, 512)
                elif d[0] == 'mwk':
                    r_ = stage_rows(mwk[d[1]], 8, 512)
                else:
                    r_ = stage_rows(mwv[d[1]], 8, 512)
                wp_state['issued'][j] = r_

        def wnext(desc):
            j = wp_state['i']
            wp_state['i'] += 1
            assert PLAN[j] == desc, (j, PLAN[j], desc)
            wp_issue(j)
            wp_issue(j + 1)
            return wp_state['issued'].pop(j)

        def stage_win(l, segs):
            return wnext(('win', l, tuple(segs)))

        def stage_win_now(l, segs):
            W = sum(n for _, n in segs)
            src = w_in[l].rearrange("(k p) n -> p k n", p=128)
            parts = []
            off = 0
            for (c0, n) in segs:
                parts.append((off, n, src[:, :, c0:c0 + n]))
                off += n
            return stage(parts, 8, W)

        def make_hnT(xin, n, j, l, gcol, xkeys):
            act(junk[0:n, :], xin, AF.Square, xkeys, [junk, 'ssq'], scale=1.0)
            P.op('vector', lambda e: e.tensor_reduce(out=small[0:n, 0:1], in_=junk[0:n, :], axis=AX.X, op=ALU.add), [junk], ['ssq'])
            rsqrt_to(small[0:n, 0:1], small[0:n, 0:1], n, 1.0 / D, ['ssq'], ['ssq'])
            ts('vector', hnb[0:n, :], xin, small[0:n, 0:1], None, ALU.mult, None, xkeys + ['ssq'], [hnb])
            for kc in range(8):
                tr(pbt[:, kc * 128:kc * 128 + n], hnb[0:n, kc * 128:(kc + 1) * 128], identb(n), [hnb], [pbt])
            for kc in range(8):
                o_ = hnT[:, kc, j * 128:j * 128 + n]
                i_ = pbt[:, kc * 128:kc * 128 + n]
                if kc % 2 == 0:
                    ts('vector', o_, i_, V(l, gcol + kc), None, ALU.mult, None, [pbt], [hnT])
                else:
                    act(o_, i_, AF.Copy, [pbt], [hnT], scale=V(l, gcol + kc))

        def load_small(l):
            k = 0
            wst = wsts[0]
            WST = WSTK(0)
            parts = [(wst[:, 0:384], w_uq[l, 0:128, :]), (wst[0:64, 384:768], w_uq[l, 128:192, :]),
                     (wst[:, 768:1024], w_uk[l]), (wst[:, 1024:1536], w_uv[l])]
            for i, (d_, s_) in enumerate(parts):
                P.dma('sync', d_, s_, writes=[('wst', k, i)])
            cp('vector', wq_b[:, 0, 0:384], wst[:, 0:384], WST, [wq_b])
            cp('vector', wq_b[0:64, 1, 0:384], wst[0:64, 384:768], WST, [wq_b])
            for kc, n in ((0, 128), (1, 64)):
                sv = wst[0:n, kc * 384:(kc + 1) * 384].rearrange("p (h d) -> p h d", d=96)
                dv = wq_b[0:n, kc, 384:768].rearrange("p (h d) -> p h d", d=96)
                cp('vector', dv[:, :, 64:80], sv[:, :, 80:96], WST, [wq_b])
                cp('vector', dv[:, :, 80:96], sv[:, :, 64:80], WST, [wq_b])
            cp('vector', wuk_b[:], wst[:, 768:1024], WST, [wuk_b])
            cp('vector', wuke[:, :, 0:64], wst[:, 768:1024].rearrange("p (h d) -> p h d", d=64), WST, [wuke])
            cp('vector', wuv_b[:], wst[:, 1024:1536], WST, [wuv_b])
            for h in range(4):
                tr(pbt[0:64, h * 128:(h + 1) * 128], wuk_b[:, h * 64:(h + 1) * 64], identb(128), [wuk_b], [pbt])
            cp('vector', wukT[:], pbt[0:64, 0:512].rearrange("p (h r) -> p h r", r=128), [pbt], [wukT])

        def mem_kv(l):
            mt = merged[:, 0:4, :].rearrange("p a b -> p (a b)")
            for j in range(2):
                P.dma('sync', mt, memp[j * 128:(j + 1) * 128, :], writes=[merged])
                make_hnT(mt, 128, j, l, 8, [merged])
            wv, wkeys = wnext(('mwk', l))
            for h in range(4):
                for kc in range(8):
                    mm(pb[0][:, 0:256], wv[:, kc, h * 128:(h + 1) * 128], hnT[:, kc, 0:256], kc == 0, kc == 7, wkeys + [hnT], [pb[0]])
                act(sh[0][:, 0:256], pb[0][:, 0:256], AF.Square, [pb[0]], [sh[0]])
                mm(pb7[:, 0:256], onesb(128, 128), sh[0][:, 0:256], True, True, [sh[0]], [pb7])
                rsqrt_to(sc[0][:, 0:256], pb7[:, 0:256], 128, 1.0 / 128, [pb7], [sc[0]])
                stt(sc[1][:, 0:256], pb[0][:, 0:256], V(l, 46), sc[0][:, 0:256], ALU.mult, ALU.mult, [pb[0], sc[0]], [sc[1]])
                cp('gpsimd', mkT[:, h, :], sc[1][:, 0:256], [sc[1]], [mkT])
                for mc in range(2):
                    tr(pb[1][:, mc * 128:(mc + 1) * 128], sc[1][:, mc * 128:(mc + 1) * 128], identf(128), [sc[1]], [pb[1]])
                cp('vector', sc[2][:, 0:256], pb[1][:, 0:256], [pb[1]], [sc[2]])
                for mc in range(2):
                    P.dma('gpsimd', o_mkp[l, mc * 128:(mc + 1) * 128, h * 128:(h + 1) * 128], sc[2][:, mc * 128:(mc + 1) * 128], reads=[sc[2]])
            wv, wkeys = wnext(('mwv', l))
            for mc in range(2):
                for kc in range(8):
                    mm(pb[0][:, :], hnT[:, kc, mc * 128:(mc + 1) * 128], wv[:, kc, :], kc == 0, kc == 7, wkeys + [hnT], [pb[0]])
                for hf in range(2):
                    cp('vector', sc[3 + hf][:, :], pb[0][:, hf * 256:(hf + 1) * 256], [pb[0]], [sc[3 + hf]])
                    cp('gpsimd', mvtok[:, mc, hf * 256:(hf + 1) * 256], sc[3 + hf][:, :], [sc[3 + hf]], [mvtok])
                    P.dma('gpsimd', o_mvp[l, mc * 128:(mc + 1) * 128, hf * 256:(hf + 1) * 256], sc[3 + hf][:, :], reads=[sc[3 + hf]])
        KnS = sb("KnS", [96, 4, NS], BF16)
        MLA_SCALE = 96 ** -0.5
        MEM_SCALE = 128 ** -0.5

        def tile_pass(l, kind, ti):
            N = NT if kind == 'p' else NS
            smp = kind == 's'
            p0 = ti * NT if not smp else SEQ
            P.dma('sync', ropet[:, :, 0:N], rope_d[:, :, p0:p0 + N], writes=[ropet])
            if not smp:
                subt = [(ti * TPT + j, 128) for j in range(TPT)]
            else:
                subt = [(16, 64)]
            for j, (t, n) in enumerate(subt):
                make_hnT(xtok[0:n, t, :], n, j, l, 0, [('x', t)])
            zb = [0]

            def zchunk(wv, wkeys, off, m):
                p_ = pb[zb[0] % 2]
                zb[0] += 1
                for kc in range(8):
                    mm(p_[0:m, 0:N], wv[:, kc, off:off + m], hnT[:, kc, 0:N], kc == 0, kc == 7, wkeys + [hnT], [p_])
                return p_

            def v3(ap):
                return ap.rearrange("p (b s) -> p b s", s=4)

            for c in range(4):
                wv, wkeys = stage_win(l, [(O_CH + c * 128, 128), (O_CB + c * 128, 128), (O_CC + c * 128, 128), (O_SG + c * 128, 128)])
                ph = zchunk(wv, wkeys, 0, 128)
                act(sc[0][:, 0:N], ph[:, 0:N], AF.Copy, [ph], [sc[0]])
                pc_ = zchunk(wv, wkeys, 256, 128)
                ukey = ('u', c)
                if not smp:
                    if ti == 0:
                        P.op('gpsimd', lambda e, c=c: e.memset(uext[:, c, 0:2], 0.0), (), [ukey])
                    tt('vector', uext[:, c, 2:2 + N], pc_[:, 0:N], sc[0][:, 0:N], ALU.mult, [pc_, sc[0]], [ukey])
                    u0, u1, u2 = uext[:, c, 0:N], uext[:, c, 1:N + 1], uext[:, c, 2:N + 2]
                    cv, yv = sc[1][:, 0:N], sc[2][:, 0:N]
                else:
                    uv = uext[:, c, 0:96].rearrange("p (b s) -> p b s", s=6)
                    if c == 0:
                        mh = merged[0:32, 0:2, :].rearrange("p a b -> p (a b)")
                        mh2 = merged[0:32, 2:4, :].rearrange("p a b -> p (a b)")
                        P.dma('sync', mh, scv[l].rearrange("b j c -> (b j) c"), writes=[('mh', 0)])
                    tr(pb[2][:, 0:32], mh[:, c * 128:(c + 1) * 128], identf(32), [('mh', 0)], [pb[2]])
                    cp('vector', uv[:, :, 0:2], pb[2][:, 0:32].rearrange("p (b j) -> p b j", j=2), [pb[2]], [('uh', c, 0)])
                    tt('vector', uv[:, :, 2:6], v3(pc_[:, 0:N]), v3(sc[0][:, 0:N]), ALU.mult, [pc_, sc[0], ('uh', c, 0)], [ukey])
                    u0, u1, u2 = uv[:, :, 0:4], uv[:, :, 1:5], uv[:, :, 2:6]
                    cv, yv = v3(sc[1][:, 0:N]), v3(sc[2][:, 0:N])
                ts('vector', cv, u0, V(l, 16 + c), None, ALU.mult, None, [ukey], [sc[1]])
                stt(cv, u1, V(l, 20 + c), cv, ALU.mult, ALU.add, [ukey, sc[1]], [sc[1]])
                stt(cv, u2, V(l, 24 + c), cv, ALU.mult, ALU.add, [ukey, sc[1]], [sc[1]])
                pb_ = zchunk(wv, wkeys, 128, 128)
                tt('vector', sc[2][:, 0:N], pb_[:, 0:N], sc[1][:, 0:N], ALU.mult, [pb_, sc[1]], [sc[2]])
                psg = zchunk(wv, wkeys, 384, 128)
                act(sc[3][:, 0:N], psg[:, 0:N], AF.Silu, [psg], [sc[3]])
                tt('gpsimd', ys[:, c, 0:N], sc[2][:, 0:N], sc[3][:, 0:N], ALU.mult, [sc[2], sc[3]], [ys])
                if not smp:
                    if ti == NTI - 1:
                        P.dma('gpsimd', o_cvp[l][:, c * 128:(c + 1) * 128].rearrange("j p -> p j"), uext[:, c, N:N + 2],
                              reads=[ukey], noncontig=True)
                    else:
                        cp('gpsimd', uext[:, c, 0:2], uext[:, c, N:N + 2], [ukey], [ukey])
                else:
                    cp('vector', sc[8][:, 0:32].rearrange("p (b j) -> p b j", j=2), uv[:, :, 4:6], [ukey], [sc[8]])
                    tr(pb[3][0:32, c * 128:(c + 1) * 128], sc[8][:, 0:32], identf(128), [sc[8]], [pb[3]])
                    if c == 3:
                        cp('vector', mh2, pb[3][0:32, 0:512], [pb[3]], [('mh', 1)])
                        P.dma('gpsimd', o_cvs[l].rearrange("b j c -> (b j) c"), mh2, reads=[('mh', 1)])

            if smp:
                chk('s%d_conv' % l)
            for h in range(4):
                wv, wkeys = stage_win(l, [(O_HQ + h * 128, 128), (O_HF + h * 128, 128), (O_HI + h * 128, 128), (O_SG + 512 + h * 128, 128)])
                pq = zchunk(wv, wkeys, 0, 128)
                act(sc[0][:, 0:N], pq[:, 0:N], AF.Copy, [pq], [sc[0]], scale=128 ** -0.5)
                pf = zchunk(wv, wkeys, 128, 128)
                act(sc[1][:, 0:N], pf[:, 0:N], AF.Sigmoid, [pf], [sc[1]])
                ts('vector', sc[1][:, 0:N], sc[1][:, 0:N], lbv[:, l, 4 + h:5 + h], lbv[:, l, h:h + 1], ALU.mult, ALU.add, [sc[1], lbv], [sc[1]])
                act(sc[2][:, 0:N], sc[1][:, 0:N], AF.Ln, [sc[1]], [sc[2]])
                ts('gpsimd', sc[1][:, 0:N], sc[1][:, 0:N], -1.0, 1.0, ALU.mult, ALU.add, [sc[1], sc[2]], [sc[1]])
                segm = cst[:, C_SEGP:C_SEGP + N] if not smp else cst[:, C_SEGS:C_SEGS + N]
                P.op('vector', lambda e, segm=segm: e.tensor_tensor_scan(out=sc[3][:, 0:N], data0=segm, data1=sc[2][:, 0:N], initial=0.0,
                                                                         op0=ALU.mult, op1=ALU.add), [sc[2]], [sc[3]])
                act(sc[4][:, 0:N], sc[3][:, 0:N], AF.Exp, [sc[3]], [sc[4]])
                if not smp:
                    bv = sc[3][:, 0:N].rearrange("p (c s) -> p c s", s=64)
                    dv = sc[2][:, 0:N].rearrange("p (c s) -> p c s", s=64)
                    tt('vector', dv, bv, bv[:, :, 31:32].broadcast_to([128, N // 64, 64]), ALU.subtract, [sc[3]], [sc[2]])
                    act(sc[5][:, 0:N], sc[2][:, 0:N], AF.Exp, [sc[2]], [sc[5]])
                    act(sc[6][:, 0:N], sc[2][:, 0:N], AF.Exp, [sc[2]], [sc[6]], scale=-1.0)
                    E1 = sc[5]
                else:
                    E1 = sc[4]
                    act(sc[6][:, 0:N], sc[3][:, 0:N], AF.Exp, [sc[3]], [sc[6]], scale=-1.0)
                    bv = v3(sc[3][:, 0:N])
                    tt('vector', v3(sc[2][:, 0:N]), bv[:, :, 3:4].broadcast_to([128, 16, 4]), bv, ALU.subtract, [sc[3]], [sc[2]])
                    act(sc[5][:, 0:N], sc[2][:, 0:N], AF.Exp, [sc[2]], [sc[5]])
                    tt('gpsimd', sh[3][:, 0:N], sc[1][:, 0:N], sc[5][:, 0:N], ALU.mult, [sc[1], sc[5]], [sh[3]])
                tt('vector', sh[0][:, 0:N], sc[0][:, 0:N], E1[:, 0:N], ALU.mult, [sc[0], E1], [sh[0]])
                tt('gpsimd', sh[1][:, 0:N], sc[0][:, 0:N], sc[4][:, 0:N], ALU.mult, [sc[0], sc[4]], [sh[1]])
                tt('vector', sh[2][:, 0:N], sc[1][:, 0:N], sc[6][:, 0:N], ALU.mult, [sc[1], sc[6]], [sh[2]])
                nch = N // 64
                for half in range((nch + 3) // 4):
                    ncc = min(4, nch - half * 4)
                    for cc in range(ncc):
                        c_ = half * 4 + cc
                        for kc in range(8):
                            mm(pb[2][0:64, cc * 128:(cc + 1) * 128], hnT[:, kc, c_ * 64:(c_ + 1) * 64], wv[:, kc, 256:384], kc == 0, kc == 7,
                               wkeys + [hnT], [pb[2]])
                    act(vtok[:, half * 4:half * 4 + ncc, :], pb[2][0:64, 0:ncc * 128].rearrange("p (c v) -> p c v", v=128), AF.Copy, [pb[2]], [vtok])
                ksrc = sh[2] if not smp else sh[3]
                for c_ in range(nch):
                    tr(pbt[0:64, c_ * 128:(c_ + 1) * 128], ksrc[:, c_ * 64:(c_ + 1) * 64], identb(128), [ksrc], [pbt])
                cp('vector', ktok[:, 0:nch, :], pbt[0:64, 0:nch * 128].rearrange("p (c v) -> p c v", v=128), [pbt], [ktok])
                oT = pb[4]
                if not smp:
                    for c_ in range(N // 64):
                        first = (ti == 0 and c_ == 0)
                        cs_ = slice(c_ * 64, (c_ + 1) * 64)
                        mm(pb[3][0:64, 0:64], sh[2][:, cs_], sh[0][:, cs_], True, True, [sh[2], sh[0]], [pb[3]])
                        tt('vector', aTm[:, :], pb[3][0:64, 0:64], cst[0:64, C_T64:C_T64 + 64], ALU.mult, [pb[3]], [aTm])
                        if not first:
                            mm(oT[:, cs_], Sb[h][:, :], sh[1][:, cs_], True, False, [Sb[h], sh[1]], [oT])
                        mm(oT[:, cs_], vtok[:, c_, :], aTm[:, :], first, True, [vtok, aTm], [oT])
                        mm(pb[5][:, 0:128], ktok[:, c_, :], vtok[:, c_, :], True, True, [ktok, vtok], [pb[5]])
                        eL = sc[4][:, c_ * 64 + 63:c_ * 64 + 64]
                        eLm = E1[:, c_ * 64 + 63:c_ * 64 + 64]
                        if first:
                            ts('vector', S[h][:, :], pb[5][:, 0:128], eLm, None, ALU.mult, None, [pb[5], E1], [S[h]])
                        else:
                            ts('vector', S[h][:, :], S[h][:, :], eL, None, ALU.mult, None, [S[h], sc[4]], [S[h]])
                            stt(S[h][:, :], pb[5][:, 0:128], eLm, S[h][:, :], ALU.mult, ALU.add, [pb[5], E1, S[h]], [S[h]])
                        act(Sb[h][:, :], S[h][:, :], AF.Copy, [S[h]], [Sb[h]])
                    if ti == NTI - 1:
                        P.dma('gpsimd', o_hgp[l, h], S[h][:, :], reads=[S[h]])
                else:
                    mm(pb[3][0:64, 0:64], sh[2][:, 0:64], sh[0][:, 0:64], True, True, [sh[2], sh[0]], [pb[3]])
                    tt('vector', aTm[:, :], pb[3][0:64, 0:64], cst[0:64, C_BD:C_BD + 64], ALU.mult, [pb[3]], [aTm])
                    mm(oT[:, 0:64], vtok[:, 0, :], aTm[:, :], True, False, [vtok, aTm], [oT])
                    for b in range(16):
                        slot = b % 4
                        k0, k0b = ('s0', slot), ('s0b', slot)
                        P.dma('sync', s0[:, slot, :], shg[l, b, h], writes=[k0])
                        cp('gpsimd', s0b[:, slot, :], s0[:, slot, :], [k0], [k0b])
                        mm(oT[:, 4 * b:4 * b + 4], s0b[:, slot, :], sh[1][:, 4 * b:4 * b + 4], False, b == 15, [k0b, sh[1]], [oT])
                        ts('gpsimd', khm[:, :], ktok[:, 0, :], cst[0:64, C_IND + b:C_IND + b + 1], None, ALU.mult, None, [ktok], [khm])
                        mm(pb[5][:, 0:128], khm[:, :], vtok[:, 0, :], True, True, [khm, vtok], [pb[5]])
                        stt(s0[:, slot, :], s0[:, slot, :], sc[4][:, 4 * b + 3:4 * b + 4], pb[5][:, 0:128], ALU.mult, ALU.add,
                            [k0, sc[4], pb[5]], [k0])
                        P.dma('gpsimd', o_hgs[l, b, h], s0[:, slot, :], reads=[k0])
                act(sc[7][:, 0:N], oT[:, 0:N], AF.Copy, [oT], [sc[7]])
                act(sh[4][:, 0:N], oT[:, 0:N], AF.Square, [oT], [sh[4]])
                mm(pb7[:, 0:N], onesb(128, 128), sh[4][:, 0:N], True, True, [sh[4]], [pb7])
                rsqrt_to(sc[8][:, 0:N], pb7[:, 0:N], 128, 1.0 / 128, [pb7], [sc[8]])
                stt(sc[9][:, 0:N], sc[7][:, 0:N], V(l, 36 + h), sc[8][:, 0:N], ALU.mult, ALU.mult, [sc[7], sc[8]], [sc[9]])
                psg = zchunk(wv, wkeys, 384, 128)
                act(sc[7][:, 0:N], psg[:, 0:N], AF.Silu, [psg], [sc[7]])
                tt('gpsimd', ys[:, 4 + h, 0:N], sc[9][:, 0:N], sc[7][:, 0:N], ALU.mult, [sc[9], sc[7]], [ys])

            if smp:
                chk('s%d_hgrn' % l)
            wv, wkeys = stage_win(l, [(O_QL, 192), (O_KV, 128), (O_KPE, 32), (O_KPE + 16, 16), (O_KPE, 16)])
            p_ = zchunk(wv, wkeys, 0, 128)
            act(sc[0][:, 0:N], p_[:, 0:N], AF.Copy, [p_], [sc[0]])
            act(sh[0][:, 0:N], p_[:, 0:N], AF.Square, [p_], [sh[0]])
            p_ = zchunk(wv, wkeys, 128, 64)
            act(sc[1][0:64, 0:N], p_[0:64, 0:N], AF.Copy, [p_], [sc[1]])
            act(sh[1][0:64, 0:N], p_[0:64, 0:N], AF.Square, [p_], [sh[1]])
            mm(pb7[:, 0:N], onesb(128, 128), sh[0][:, 0:N], True, False, [sh[0]], [pb7])
            mm(pb7[:, 0:N], onesb(64, 128), sh[1][0:64, 0:N], False, True, [sh[1]], [pb7])
            rsqrt_to(sc[2][:, 0:N], pb7[:, 0:N], 128, 1.0 / 192, [pb7], [sc[2]])
            stt(sh[0][:, 0:N], sc[0][:, 0:N], V(l, 40), sc[2][:, 0:N], ALU.mult, ALU.mult, [sc[0], sc[2]], [sh[0]])
            stt(sh[1][0:64, 0:N], sc[1][0:64, 0:N], V(l, 41, 64), sc[2][0:64, 0:N], ALU.mult, ALU.mult, [sc[1], sc[2]], [sh[1]])
            for h in range(4):
                for (dst, woff) in ((pb[2], 0), (pb[3], 384)):
                    mm(dst[0:96, 0:N], wq_b[:, 0, woff + h * 96:woff + (h + 1) * 96], sh[0][:, 0:N], True, False, [wq_b, sh[0]], [dst])
                    mm(dst[0:96, 0:N], wq_b[0:64, 1, woff + h * 96:woff + (h + 1) * 96], sh[1][0:64, 0:N], False, True, [wq_b, sh[1]], [dst])
                tt('vector', sc[3][0:96, 0:N], pb[2][0:96, 0:N], ropet[0:96, 0, 0:N], ALU.mult, [pb[2], ropet], [sc[3]])
                tt('vector', sc[4][0:96, 0:N], pb[3][0:96, 0:N], ropet[0:96, 1, 0:N], ALU.mult, [pb[3], ropet], [sc[4]])
                tt('gpsimd', sc[3][0:96, 0:N], sc[3][0:96, 0:N], sc[4][0:96, 0:N], ALU.add, [sc[3], sc[4]], [sc[3]])
                act(sh[2][0:96, 0:N], sc[3][0:96, 0:N], AF.Square, [sc[3]], [sh[2]])
                mm(pb7[0:96, 0:N], onesb(96, 96), sh[2][0:96, 0:N], True, True, [sh[2]], [pb7])
                rsqrt_to(sc[4][0:96, 0:N], pb7[0:96, 0:N], 96, 1.0 / 96, [pb7], [sc[4]])
                stt(QnT[:, h, 0:N], sc[3][0:96, 0:N], V(l, 43, 96), sc[4][0:96, 0:N], ALU.mult, ALU.mult, [sc[3], sc[4]], [QnT])
            p_ = zchunk(wv, wkeys, 192, 128)
            act(sc[0][:, 0:N], p_[:, 0:N], AF.Copy, [p_], [sc[0]])
            act(sh[2][:, 0:N], p_[:, 0:N], AF.Square, [p_], [sh[2]])
            mm(pb7[:, 0:N], onesb(128, 128), sh[2][:, 0:N], True, True, [sh[2]], [pb7])
            rsqrt_to(sc[1][:, 0:N], pb7[:, 0:N], 128, 1.0 / 128, [pb7], [sc[1]])
            stt(sc[2][:, 0:N], sc[0][:, 0:N], V(l, 42), sc[1][:, 0:N], ALU.mult, ALU.mult, [sc[0], sc[1]], [sc[2]])
            cp('gpsimd', sh[3][:, 0:N], sc[2][:, 0:N], [sc[2]], [sh[3]])
            nsub = len(subt)
            nn = subt[0][1]
            for j in range(nsub):
                tr(pb[2][0:nn, j * 128:(j + 1) * 128], sc[2][:, j * 128:j * 128 + nn], identf(128), [sc[2]], [pb[2]])
            cp('vector', sc[5][0:nn, 0:nsub * 128], pb[2][0:nn, 0:nsub * 128], [pb[2]], [sc[5]])
            c3 = sc[5][0:nn, 0:nsub * 128].rearrange("p (j r) -> p j r", r=128)
            if not smp:
                P.dma('gpsimd', o_latp[l, p0:p0 + N, :].rearrange("(j p) r -> p j r", p=128), c3, reads=[sc[5]])
                cp('gpsimd', ctokAll[:, ti * TPT:ti * TPT + TPT, 0:128], c3, [sc[5]], [ctokAll])
            else:
                P.dma('gpsimd', o_lats[l], sc[5][0:64, 0:128], reads=[sc[5]])
                cp('gpsimd', cnew[:, 0:128], sc[5][0:64, 0:128], [sc[5]], [cnew])
            pk = zchunk(wv, wkeys, 320, 32)
            pks = zchunk(wv, wkeys, 352, 32)
            tt('vector', sc[6][0:32, 0:N], pk[0:32, 0:N], ropet[0:32, 2, 0:N], ALU.mult, [pk, ropet], [sc[6]])
            tt('vector', sc[7][0:32, 0:N], pks[0:32, 0:N], ropet[0:32, 3, 0:N], ALU.mult, [pks, ropet], [sc[7]])
            tt('gpsimd', sc[6][0:32, 0:N], sc[6][0:32, 0:N], sc[7][0:32, 0:N], ALU.add, [sc[6], sc[7]], [sc[6]])
            cp('gpsimd', sh[4][0:32, 0:N], sc[6][0:32, 0:N], [sc[6]], [sh[4]])
            for j in range(nsub):
                tr(pb[3][0:nn, j * 32:(j + 1) * 32], sc[6][0:32, j * 128:j * 128 + nn], identf(32), [sc[6]], [pb[3]])
            cp('vector', sc[7][0:nn, 0:nsub * 32], pb[3][0:nn, 0:nsub * 32], [pb[3]], [sc[7]])
            if not smp:
                P.dma('gpsimd', o_ropep[l, p0:p0 + N, :].rearrange("(j p) r -> p j r", p=128),
                      sc[7][:, 0:TPT * 32].rearrange("p (j r) -> p j r", r=32), reads=[sc[7]])
            else:
                P.dma('gpsimd', o_ropes[l], sc[7][0:64, 0:32], reads=[sc[7]])
            for h in range(4):
                mm(pb[2][0:96, 0:N], wuke[:, h, :], sh[3][:, 0:N], True, False, [wuke, sh[3]], [pb[2]])
                mm(pb[2][0:96, 0:N], cbf[0:32, C_E:C_E + 96], sh[4][0:32, 0:N], False, True, [sh[4]], [pb[2]])
                act(sh[5][0:96, 0:N], pb[2][0:96, 0:N], AF.Square, [pb[2]], [sh[5]])
                mm(pb7[0:96, 0:N], onesb(96, 96), sh[5][0:96, 0:N], True, True, [sh[5]], [pb7])
                rsqrt_to(sc[8][0:96, 0:N], pb7[0:96, 0:N], 96, 1.0 / 96, [pb7], [sc[8]])
                dstK = KnAll[:, h, p0:p0 + N] if not smp else KnS[:, h, :]
                stt(dstK, pb[2][0:96, 0:N], V(l, 44, 96), sc[8][0:96, 0:N], ALU.mult, ALU.mult, [pb[2], sc[8]], [KnAll if not smp else KnS])
            if not smp:
                nk = TPT * ti + TPT
                for h in range(4):
                    for j in range(nk):
                        lo = max(0, j - TPT * ti) * 128
                        spb = pb[2 + (j % 2)]
                        pT = sh[j % 2]
                        mm(spb[:, lo:N], KnAll[:, h, j * 128:(j + 1) * 128], QnT[:, h, lo:N], True, True, [KnAll, QnT], [spb])
                        act(pT[:, lo:N], spb[:, lo:N], AF.Exp, [spb], [pT], scale=MLA_SCALE)
                        if j >= TPT * ti:
                            tt('gpsimd', pT[:, lo:lo + 128], pT[:, lo:lo + 128], cbf[:, C_TRI:C_TRI + 128], ALU.mult, [pT], [pT])
                        mm(pb[4][:, lo:N], ctokAll[:, j, 0:128], pT[:, lo:N], j == 0, j == nk - 1, [ctokAll, pT], [pb[4]])
                        mm(pb[5][:, lo:N], onesb(128, 128), pT[:, lo:N], j == 0, j == nk - 1, [pT], [pb[5]])
                    recip(sc[9][:, 0:N], pb[5][:, 0:N], [pb[5]], [sc[9]])
                    tt('vector', onT[:, h, 0:N], pb[4][:, 0:N], sc[9][:, 0:N], ALU.mult, [pb[4], sc[9]], [onT])
            else:
                chk('s%d_mla0' % l)
                P.barrier()
                mla_sample(l)
                P.barrier()
                chk('s%d_mla1' % l)
            for h in range(4):
                mm(pb[2][:, 0:N], wuv_b[:, h * 128:(h + 1) * 128], onT[:, h, 0:N], True, True, [wuv_b, onT], [pb[2]])
                act(sc[h][:, 0:N], pb[2][:, 0:N], AF.Copy, [pb[2]], [sc[h]])
            wv, wkeys = stage_win(l, [(O_SG + 1024, 512)])
            for h in range(4):
                psg = zchunk(wv, wkeys, h * 128, 128)
                act(sc[4 + h % 2][:, 0:N], psg[:, 0:N], AF.Silu, [psg], [sc[4 + h % 2]])
                tt('gpsimd', ys[:, 8 + h, 0:N], sc[h][:, 0:N], sc[4 + h % 2][:, 0:N], ALU.mult, [sc[h], sc[4 + h % 2]], [ys])

            wv, wkeys = stage_win(l, [(O_MQ, 512)])
            for h in range(4):
                p_ = zchunk(wv, wkeys, h * 128, 128)
                act(sc[0][:, 0:N], p_[:, 0:N], AF.Copy, [p_], [sc[0]])
                act(sh[0][:, 0:N], p_[:, 0:N], AF.Square, [p_], [sh[0]])
                mm(pb7[:, 0:N], onesb(128, 128), sh[0][:, 0:N], True, True, [sh[0]], [pb7])
                rsqrt_to(sc[1][:, 0:N], pb7[:, 0:N], 128, 1.0 / 128, [pb7], [sc[1]])
                stt(mqT[:, h, 0:N], sc[0][:, 0:N], V(l, 45), sc[1][:, 0:N], ALU.mult, ALU.mult, [sc[0], sc[1]], [mqT])
            if not smp:
                for h in range(4):
                    for mc in range(2):
                        spb = pb[2 + mc]
                        mm(spb[:, 0:N], mkT[:, h, mc * 128:(mc + 1) * 128], mqT[:, h, 0:N], True, True, [mkT, mqT], [spb])
                        act(sh[1 + mc][:, 0:N], spb[:, 0:N], AF.Exp, [spb], [sh[1 + mc]], scale=MEM_SCALE)
                        mm(pb[4][:, 0:N], mvtok[:, mc, h * 128:(h + 1) * 128], sh[1 + mc][:, 0:N], mc == 0, mc == 1, [mvtok, sh[1 + mc]], [pb[4]])
                        mm(pb[5][:, 0:N], onesb(128, 128), sh[1 + mc][:, 0:N], mc == 0, mc == 1, [sh[1 + mc]], [pb[5]])
                    recip(sc[2][:, 0:N], pb[5][:, 0:N], [pb[5]], [sc[2]])
                    tt('vector', sc[4 + h][:, 0:N], pb[4][:, 0:N], sc[2][:, 0:N], ALU.mult, [pb[4], sc[2]], [sc[4 + h]])
            else:
                chk('s%d_mem0' % l)
                P.barrier()
                mem_sample(l)
                P.barrier()
                chk('s%d_mem1' % l)
            wv, wkeys = stage_win(l, [(O_SG + 1536, 512)])
            for h in range(4):
                psg = zchunk(wv, wkeys, h * 128, 128)
                act(sc[h % 2][:, 0:N], psg[:, 0:N], AF.Silu, [psg], [sc[h % 2]])
                tt('gpsimd', ys[:, 12 + h, 0:N], sc[4 + h][:, 0:N], sc[h % 2][:, 0:N], ALU.mult, [sc[4 + h], sc[h % 2]], [ys])

            for n in range(4):
                for half in range(2):
                    wbv, wbk = wnext(('wbo', l, n, half))
                    wv, wkeys = stage_win(l, [(O_MG + n * 1024 + half * 512, 512)])
                    for dc in range(4):
                        dch = half * 4 + dc
                        pg = zchunk(wv, wkeys, dc * 128, 128)
                        g_, t_ = sc[2 * (dc % 2)], sc[2 * (dc % 2) + 1]
                        act(g_[:, 0:N], pg[:, 0:N], AF.Sigmoid, [pg], [g_])
                        pp = pb[2 + dc % 2]
                        for kc in range(4):
                            mm(pp[:, 0:N], wbv[:, kc, dc * 128:(dc + 1) * 128], ys[:, n * 4 + kc, 0:N], kc == 0, kc == 3, wbk + [ys], [pp])
                        if n == 0:
                            tt('vector', merged[:, dch, 0:N], pp[:, 0:N], g_[:, 0:N], ALU.mult, [pp, g_], [merged])
                        else:
                            tt('vector', t_[:, 0:N], pp[:, 0:N], g_[:, 0:N], ALU.mult, [pp, g_], [t_])
                            tt('gpsimd', merged[:, dch, 0:N], merged[:, dch, 0:N], t_[:, 0:N], ALU.add, [merged, t_], [merged])
            cp('vector', hnT[:, 0:4, 0:N], merged[:, 0:4, 0:N], [merged], [hnT])
            cp('gpsimd', hnT[:, 4:8, 0:N], merged[:, 4:8, 0:N], [merged], [hnT])
            for half in range(2):
                wov, wok = wnext(('wo', l, half))
                for j, (t, n) in enumerate(subt):
                    pp = pb[2 + j % 2]
                    for kc in range(8):
                        mm(pp[0:n, :], hnT[:, kc, j * 128:j * 128 + n], wov[:, kc, :], kc == 0, kc == 7, wok + [hnT], [pp])
                    xt_ = xtok[0:n, t, half * 512:(half + 1) * 512]
                    tt('vector', xt_, pp[0:n, :], xt_, ALU.add, [pp, ('x', t)], [('x', t)])
            if l == 1:
                for j, (t, n) in enumerate(subt):
                    if not smp:
                        P.dma('gpsimd', yp[t * 128:(t + 1) * 128, :], xtok[:, t, :], reads=[('x', t)])
                    else:
                        P.dma('gpsimd', ysm, xtok[0:64, 16, :], reads=[('x', 16)])
        def mla_sample(l):
            N = NS
            clat_l = clats[l]
            crope_l = cropes[l]
            mq_ = [merged[:, 2 * i:2 * i + 2, :].rearrange("p a b -> p (a b)") for i in range(3)]
            mk_ = [('mgs', i) for i in range(3)]
            for h in range(4):
                mm(pb[2][0:64, 0:64], KnS[:, h, :], QnT[:, h, 0:N], True, True, [KnS, QnT], [pb[2]])
                act(sc[0][0:64, 0:64], pb[2][0:64, 0:64], AF.Exp, [pb[2]], [sc[0]], scale=MLA_SCALE)
                tt('vector', pTn[:, :, h * 8:(h + 1) * 8], sc[0][0:64, 0:64].rearrange("p (a t) -> p a t", t=8),
                   cst[0:64, C_BD:C_BD + 64].rearrange("p (a t) -> p a t", t=8), ALU.mult, [sc[0]], [pTn])
                ts('vector', QgT[:, h, :], QnT[:, h, 0:N], V(l, 44, 96), None, ALU.mult, None, [QnT], [QgT])
                mm(pb[3][:, 0:64], wukT[:, h, :], QgT[0:64, h, :], True, True, [wukT, QgT], [pb[3]])
                cp('vector', QaC[:, :, h * 8:(h + 1) * 8], pb[3][:, 0:64].rearrange("p (a t) -> p a t", t=8), [pb[3]], [QaC])
                for i in range(4):
                    mm(pb[3][:, 64:128], cbf[64:96, C_Z + 160 - 32 * i:C_Z + 288 - 32 * i], QgT[64:96, h, :], True, True, [QgT], [pb[3]])
                    cp('vector', QaR[:, :, i, h * 8:(h + 1) * 8], pb[3][:, 64:128].rearrange("p (a t) -> p a t", t=8), [pb[3]], [QaR])
            P.op('gpsimd', lambda e: e.memset(gbf[:, :, 128:136], 1.0), (), [gbf])
            chk('x1')
            for bp in range(8):
                for ch in range(8):
                    P.dma('gpsimd', graw[:, :], clat_l, reads=[idx2], writes=[graw],
                          indirect=bass.IndirectOffsetOnAxis(ap=idx2[:, bp, ch:ch + 1], axis=0))
                    P.dma('gpsimd', rraw[:, :], crope_l, reads=[idx2], writes=[rraw],
                          indirect=bass.IndirectOffsetOnAxis(ap=idx2[:, bp, ch:ch + 1], axis=0))
                    g3 = graw[:, :].rearrange("p (a r) -> p a r", r=128)
                    act(gbf[:, 0:7, 0:128], g3[:, 0:7, :], AF.Copy, [graw], [gbf])
                    cp('vector', gbf[:, 7:16, 0:128], g3[:, 7:16, :], [graw], [gbf])
                    cp('vector', rbf[:, :, :], rraw[:, :].rearrange("p (a r) -> p a r", r=32), [rraw], [rbf])
                    tt('gpsimd', mq_[0], rraw[:, :], rraw[:, :], ALU.mult, [rraw], [mk_[0]])
                    P.op('vector', lambda e: e.tensor_reduce(out=stat[:, 0:16], in_=mq_[0].rearrange("p (a r) -> p a r", r=32),
                                                             axis=AX.X, op=ALU.add), [mk_[0]], [('stat', 0)])
                    chk('x2')
                    for g in range(4):
                        for i in range(4):
                            row = g * 4 + i
                            tr(pbt[:, i * 128:(i + 1) * 128], gbf[:, row, 0:128], identb(128), [gbf], [pbt])
                        tr(pbt[:, 512:640], rbf[:, g * 4:(g + 1) * 4, :].rearrange("p a r -> p (a r)"), identb(128), [rbf], [pbt])
                        cp('vector', cTs[:, :, :], pbt[:, 0:512].rearrange("p (a r) -> p a r", r=128), [pbt], [cTs])
                        cp('vector', rTs[:, :], pbt[:, 512:640], [pbt], [rTs])
                        chk('g1')
                        for i in range(4):
                            pk_ = pb[i // 2]
                            mm(pk_[:, (i % 2) * 256:(i % 2) * 256 + 256], cTs[:, i, :], wuk_b[:, :], True, True, [cTs, wuk_b], [pk_])
                            if i == 0:
                                mm(pb[2][:, 0:128], rTs[:, :], QaR[:, bp, :, :].rearrange("p a t -> p (a t)"), True, False, [rTs, QaR], [pb[2]])
                            mm(pb[2][:, i * 32:(i + 1) * 32], cTs[:, i, :], QaC[:, bp, :], False, i == 3, [cTs, QaC], [pb[2]])
                        chk('g2')
                        act(mq_[1], pb[0][:, 0:512], AF.Square, [pb[0]], [mk_[1]])
                        act(mq_[2], pb[1][:, 0:512], AF.Square, [pb[1]], [mk_[2]])
                        P.op('vector', lambda e: e.tensor_reduce(out=stat[:, 16:24], in_=mq_[1].rearrange("p (a r) -> p a r", r=64),
                                                                 axis=AX.X, op=ALU.add), [mk_[1]], [('stat', 1)])
                        P.op('vector', lambda e: e.tensor_reduce(out=stat[:, 24:32], in_=mq_[2].rearrange("p (a r) -> p a r", r=64),
                                                                 axis=AX.X, op=ALU.add), [mk_[2]], [('stat', 1)])
                        tt('vector', stat[:, 32:48].rearrange("p (a h) -> p a h", h=4), stat[:, 16:32].rearrange("p (a h) -> p a h", h=4),
                           stat[:, g * 4:(g + 1) * 4].rearrange("p (a o) -> p a o", o=1).broadcast_to([128, 4, 4]), ALU.add,
                           [('stat', 0), ('stat', 1)], [('stat', 2)])
                        rsqrt_to(stat[:, 32:48], stat[:, 32:48], 128, 1.0 / 96, [('stat', 2)], [('stat', 2)])
                        tt('vector', sc[4][:, 0:128].rearrange("p (a t) -> p a t", t=8), pb[2][:, 0:128].rearrange("p (a t) -> p a t", t=8),
                           stat[:, 32:48].rearrange("p (a o) -> p a o", o=1).broadcast_to([128, 16, 8]), ALU.mult,
                           [pb[2], ('stat', 2)], [sc[4]])
                        chk('g3')
                        act(sc[5][:, 0:128], sc[4][:, 0:128], AF.Exp, [sc[4]], [sc[5]], scale=MLA_SCALE)
                        tt('gpsimd', pTs[:, :, :], sc[5][:, 0:128].rearrange("p (a t) -> p a t", t=32),
                           cst[:, C_PM:C_PM + 32].rearrange("p (o t) -> p o t", o=1).broadcast_to([128, 4, 32]), ALU.mult, [sc[5]], [pTs])
                        chk('g4')
                        for i in range(4):
                            row = g * 4 + i
                            mm(pb[4][0:32, 0:130], pTs[:, i, :], gbf[:, row, 0:130], ch == 0 and g == 0 and i == 0, False, [pTs, gbf], [pb[4]])
                        chk('x3')
                mm(pb[4][0:32, 0:130], pTn[:, bp, :], cnew[:, 0:130], False, True, [pTn, cnew], [pb[4]])
                recip(small[0:32, 1:2], pb[4][0:32, 128:129], [pb[4]], ['sm1'])
                ts('vector', sc[6][0:32, 0:128], pb[4][0:32, 0:128], small[0:32, 1:2], None, ALU.mult, None, [pb[4], 'sm1'], [sc[6]])
                tr(pb[5][:, 0:32], sc[6][0:32, 0:128], identf(32), [sc[6]], [pb[5]])
                cp('vector', onT[:, :, bp * 8:(bp + 1) * 8], pb[5][:, 0:32].rearrange("p (h t) -> p h t", t=8), [pb[5]], [onT])
                chk('x4')

        def mem_sample(l):
            for b in range(16):
                P.dma('sync', mraw[:, :, :], cmk[l, b].rearrange("(mc p) d -> p mc d", p=128), writes=[mraw])
                act(mkb[:], mraw[:], AF.Copy, [mraw], [mkb])
                P.dma('sync', mraw[:, :, :], cmv[l, b].rearrange("(mc p) d -> p mc d", p=128), writes=[mraw])
                cp('vector', mvb[:], mraw[:], [mraw], [mvb])
                for h in range(4):
                    for mc in range(2):
                        tr(pbt[:, (h * 2 + mc) * 128:(h * 2 + mc + 1) * 128], mkb[:, mc, h * 128:(h + 1) * 128], identb(128), [mkb], [pbt])
                cp('vector', mkTs[:, :, :], pbt[:, 0:1024].rearrange("p (h m) -> p h m", m=256), [pbt], [mkTs])
                for h in range(4):
                    for mc in range(2):
                        o4 = (h * 2 + mc) * 4
                        mm(pb[2][:, o4:o4 + 4], mkTs[:, h, mc * 128:(mc + 1) * 128], mqT[:, h, 4 * b:4 * b + 4], True, True, [mkTs, mqT], [pb[2]])
                act(pms[:, 0:32], pb[2][:, 0:32], AF.Exp, [pb[2]], [pms], scale=MEM_SCALE)
                for h in range(4):
                    for mc in range(2):
                        o4 = (h * 2 + mc) * 4
                        mm(pb[4][:, h * 64 + 4 * b:h * 64 + 4 * b + 4], mvb[:, mc, h * 128:(h + 1) * 128], pms[:, o4:o4 + 4], mc == 0, mc == 1,
                           [mvb, pms], [pb[4]])
                    for mc in range(2):
                        o4 = (h * 2 + mc) * 4
                        mm(pb[5][:, h * 64 + 4 * b:h * 64 + 4 * b + 4], onesb(128, 128), pms[:, o4:o4 + 4], mc == 0, mc == 1, [pms], [pb[5]])
            for h in range(4):
                recip(sc[2][:, 0:64], pb[5][:, h * 64:(h + 1) * 64], [pb[5]], [sc[2]])
                tt('vector', sc[4 + h][:, 0:64], pb[4][:, h * 64:(h + 1) * 64], sc[2][:, 0:64], ALU.mult, [pb[4], sc[2]], [sc[4 + h]])

        P.barrier()
        class _Stop(Exception):
            pass

        def chk(tag):
            if STOP == tag:
                raise _Stop()
        try:
            for l in range(2):
                load_small(l)
                mem_kv(l)
                chk('memkv%d' % l)
                for ti in range(NTI):
                    tile_pass(l, 'p', ti)
                    chk('p%d_%d' % (l, ti))
                P.barrier()
                tile_pass(l, 's', 0)
                P.barrier()
                chk('s%d' % l)
        except _Stop:
            pass
        P.finish()
        print('[build] inst counts', P.cnt, 'dma vals', max(P.dma_val), flush=True)
    return nc


_NC_CACHE = {}


def kernel(x_prompt, x_sample, mem_prompt, cache_mla_latent, cache_mla_rope, page_table, state_hgrn, state_conv,
           cache_mem_k, cache_mem_v, norm_gain, w_in, conv_w, hgrn_lb, hgrn_norm, mla_q_norm, mla_w_uq,
           mla_kv_norm, mla_w_uk, mla_w_uv, mla_q_gain, mla_k_gain, mem_norm, mem_w_k, mem_w_v, mem_q_gain,
           mem_k_gain, w_branch_out, w_out):
    f = lambda a: np.ascontiguousarray(np.asarray(a, dtype=np.float32))
    cst, rope = make_consts()
    vecs = np.zeros((128, 2, 48), np.float32)
    ng, mn, cw, lb, hn = f(norm_gain), f(mem_norm), f(conv_w), f(hgrn_lb), f(hgrn_norm)
    for l in range(2):
        vecs[:, l, 0:8] = ng[l].reshape(8, 128).T
        vecs[:, l, 8:16] = mn[l].reshape(8, 128).T
        for j in range(3):
            vecs[:, l, 16 + 4 * j:20 + 4 * j] = cw[l, j].reshape(4, 128).T
        vecs[:, l, 28:32] = lb[0].reshape(4, 128).T
        vecs[:, l, 32:36] = lb[1].reshape(4, 128).T
        vecs[:, l, 36:40] = hn[l].reshape(4, 128).T
        qn = f(mla_q_norm)[l]
        vecs[:, l, 40] = qn[0:128]
        vecs[0:64, l, 41] = qn[128:192]
        vecs[:, l, 42] = f(mla_kv_norm)[l]
        vecs[0:96, l, 43] = f(mla_q_gain)[l]
        vecs[0:96, l, 44] = f(mla_k_gain)[l]
        vecs[:, l, 45] = f(mem_q_gain)[l]
        vecs[:, l, 46] = f(mem_k_gain)[l]
        vecs[:, l, 47] = EPS
    clat = f(cache_mla_latent).reshape(2, NPHYS * 8, 2048)
    crope = f(cache_mla_rope).reshape(2, NPHYS * 8, 512)
    pt = np.asarray(page_table, dtype=np.int32)
    shared = dict(clat0=clat[0], clat1=clat[1], crope0=crope[0], crope1=crope[1],
                  w_in=f(w_in), w_bo=f(w_branch_out), w_o=f(w_out), w_uq=f(mla_w_uq), w_uk=f(mla_w_uk), w_uv=f(mla_w_uv),
                  mwk=f(mem_w_k), mwv=f(mem_w_v), vecs=vecs, cst=cst, rope=rope)
    xp_, xs_, mp_ = f(x_prompt), f(x_sample), f(mem_prompt)
    shg_, scv_, cmk_, cmv_ = f(state_hgrn), f(state_conv), f(cache_mem_k), f(cache_mem_v)
    in_maps = []
    for c in range(8):
        bs = slice(16 * c, 16 * c + 16)
        m = dict(shared)
        m.update(xp=xp_[c], xs=xs_[bs].reshape(64, D), memp=mp_[c],
                 ptab=np.ascontiguousarray(pt[bs].reshape(8, 128).T),
                 shg=np.ascontiguousarray(shg_[:, bs]), scv=np.ascontiguousarray(scv_[:, bs]),
                 cmk=np.ascontiguousarray(cmk_[:, bs].reshape(2, 16, 256, 512)),
                 cmv=np.ascontiguousarray(cmv_[:, bs].reshape(2, 16, 256, 512)))
        in_maps.append(m)
    if 'nc' not in _NC_CACHE:
        _NC_CACHE['nc'] = build_nc()
    ncr = CFG['ncores']
    res = run_bass_kernel_spmd(_NC_CACHE['nc'], in_maps[:ncr], core_ids=list(range(ncr)))
    R = list(res.results)
    while len(R) < 8:
        R.append(R[0])
    cat = lambda k: np.stack([r[k] for r in R], 0)
    y_prompt = cat("yp")
    y_sample = cat("ysm").reshape(128, 4, D)
    latp = cat("o_latp").transpose(1, 0, 2, 3)
    ropep = cat("o_ropep").transpose(1, 0, 2, 3)
    hgp = cat("o_hgp").transpose(1, 0, 2, 3, 4)
    cvp = cat("o_cvp").transpose(1, 0, 2, 3)
    mkp = cat("o_mkp").transpose(1, 0, 2, 3).reshape(2, 8, 256, 4, 128)
    mvp = cat("o_mvp").transpose(1, 0, 2, 3).reshape(2, 8, 256, 4, 128)
    lats = cat("o_lats").transpose(1, 0, 2, 3).reshape(2, 128, 4, 128)
    ropes = cat("o_ropes").transpose(1, 0, 2, 3).reshape(2, 128, 4, 32)
    hgs = cat("o_hgs").transpose(1, 0, 2, 3, 4, 5).reshape(2, 128, 4, 128, 128)
    cvs = cat("o_cvs").transpose(1, 0, 2, 3, 4).reshape(2, 128, 2, 512)
    outs = (y_prompt, y_sample, latp, ropep, hgp, cvp, mkp, mvp, lats, ropes, hgs, cvs)
    return tuple(np.ascontiguousarray(o, dtype=np.float32) for o in outs)
```

```python
import numpy as np
from contextlib import ExitStack
import concourse.bass as bass
import concourse.mybir as mybir
from concourse.bass_utils import run_bass_kernel_spmd

F32 = mybir.dt.float32
BF16 = mybir.dt.bfloat16
I32 = mybir.dt.int32
AF = mybir.ActivationFunctionType
ALU = mybir.AluOpType
AX = mybir.AxisListType
ENGS = ['sync', 'scalar', 'vector', 'gpsimd', 'tensor']


class Prog:
    def __init__(self, nc, ndma=44):
        self.nc = nc
        self.q = {e: [] for e in ENGS}
        self.cnt = {e: 0 for e in ENGS}
        self.waited = {e: {} for e in ENGS}
        self.lastw = {}
        self.readers = {}
        self.ndma = ndma
        self.dma_val = [0] * ndma
        self.dma_i = 0
        self.es = None
        self.sems = {}
        self.all_dma_events = {}

    def sb(self, name, shape, dt):
        return self.es.enter_context(self.nc.sbuf_tensor(name, list(shape), dt))

    def ps(self, name, shape, dt):
        return self.es.enter_context(self.nc.psum_tensor(name, list(shape), dt))

    @staticmethod
    def _key(t):
        if isinstance(t, (str, tuple)):
            return t
        return t.name

    def _deps(self, eng, reads, writes):
        deps = []
        for k in reads:
            k = self._key(k)
            if k in self.lastw:
                deps.append(self.lastw[k])
        for k in writes:
            k = self._key(k)
            if k in self.lastw:
                deps.append(self.lastw[k])
            deps += self.readers.get(k, [])
        waits = {}
        for (s, v) in deps:
            if s == 'tensor' and eng == 'tensor':
                continue
            if self.waited[eng].get(s, 0) < v:
                waits[s] = max(waits.get(s, 0), v)
        for s, v in waits.items():
            self.waited[eng][s] = v
        return waits

    def _commit(self, ev, reads, writes):
        for k in reads:
            k = self._key(k)
            self.readers.setdefault(k, []).append(ev)
        for k in writes:
            k = self._key(k)
            self.lastw[k] = ev
            self.readers[k] = []

    def op(self, eng, fn, reads=(), writes=()):
        waits = self._deps(eng, reads, writes)
        self.cnt[eng] += 1
        ev = (eng, self.cnt[eng])
        self.q[eng].append((waits, fn, eng, 1))
        self._commit(ev, reads, writes)

    def dma(self, eng, out, in_, reads=(), writes=(), indirect=None, noncontig=False, eoff=0):
        waits = self._deps(eng, reads, writes)
        i = self.dma_i
        self.dma_i = (self.dma_i + 1) % self.ndma
        s = ('d', i)
        if self.dma_val[i] > 0 and self.waited[eng].get(s, 0) < self.dma_val[i]:
            waits[s] = self.dma_val[i]
            self.waited[eng][s] = self.dma_val[i]
        self.dma_val[i] += 16
        ev = (s, self.dma_val[i])
        self.all_dma_events[s] = self.dma_val[i]
        if indirect is not None:
            fn = lambda e: e.indirect_dma_start(out=out, out_offset=None, in_=in_, in_offset=indirect, element_offset=eoff)
        elif noncontig:
            fn = lambda e: e.dma_start(out=out, in_=in_, allow_slow_non_contiguous=True)
        else:
            fn = lambda e: e.dma_start(out=out, in_=in_)
        self.q[eng].append((waits, fn, s, 16))
        self._commit(ev, reads, writes)

    def barrier(self):
        for e in ENGS:
            w = {}
            for s_, v in self.all_dma_events.items():
                if self.waited[e].get(s_, 0) < v:
                    w[s_] = v
                    self.waited[e][s_] = v
            if w:
                self.q[e].append((w, None, None, 0))
        for e in ENGS:
            for o in ENGS:
                if o != e and self.cnt[o] > self.waited[e].get(o, 0):
                    self.q[e].append(({o: self.cnt[o]}, None, None, 0))
                    self.waited[e][o] = self.cnt[o]

    def finish(self):
        nc = self.nc
        fin = dict(self.all_dma_events)
        for e in ENGS:
            if e != 'sync' and self.cnt[e] > 0:
                fin[e] = self.cnt[e]
        self.q['sync'].append((fin, None, None, 0))
        for e in ENGS:
            self.sems[e] = self.es.enter_context(nc.semaphore("s_" + e))
        for i in range(self.ndma):
            self.sems[('d', i)] = self.es.enter_context(nc.semaphore("s_d%d" % i))
        block = self.es.enter_context(nc.Block())

        def replay(name):
            def f(eng):
                for (waits, fn, incs, incv) in self.q[name]:
                    for s, v in waits.items():
                        eng.wait_ge(self.sems[s], v)
                    if fn is not None:
                        ins = fn(eng)
                        ins.then_inc(self.sems[incs], incv)
            return f
        block.sync(replay('sync'))
        block.scalar(replay('scalar'))
        block.vector(replay('vector'))
        block.gpsimd(replay('gpsimd'))
        block.tensor(replay('tensor'))


D = 1024
SEQ = 2048
NT = 256
TPT = NT // 128
NTI = 2048 // NT
NS = 64
BRW = 512
EPS = 1e-6
O_CH, O_CB, O_CC, O_HQ, O_HF, O_HI, O_QL, O_KV, O_KPE, O_MQ, O_SG, O_MG = (
    0, 512, 1024, 1536, 2048, 2560, 3072, 3264, 3392, 3424, 3936, 5984)
NIN = 10080
NPHYS = 10240
PAST = 8192
C_ID = 0
C_TRI = 128
C_T64 = 256
C_BD = 320
C_PM = 384
C_IND = 416
C_SEGP = 432
C_SEGS = 944
C_E = 1008
C_SEL = 1104
C_ONE = 1136
C_Z = 1264
CW = 1552


def make_consts():
    c = np.zeros((128, CW), np.float32)
    c[:, C_ID:C_ID + 128] = np.eye(128)
    k = np.arange(128)
    c[:, C_TRI:C_TRI + 128] = (k[:, None] <= k[None, :])
    s = np.arange(64)
    c[:64, C_T64:C_T64 + 64] = (s[:, None] <= s[None, :])
    c[:64, C_BD:C_BD + 64] = (s[:, None] <= s[None, :]) & ((s[:, None] // 4) == (s[None, :] // 4))
    pm = np.zeros((128, 4, 2, 4), np.float32)
    pm[:64, :, 0, :] = 1
    pm[64:, :, 1, :] = 1
    c[:, C_PM:C_PM + 32] = pm.reshape(128, 32)
    c[:64, C_IND:C_IND + 16] = ((s[:, None] // 4) == np.arange(16)[None, :])
    c[:, C_SEGP:C_SEGP + 512] = (np.arange(512) % 64 != 0)[None, :]
    c[:, C_SEGS:C_SEGS + 64] = (np.arange(64) % 4 != 0)[None, :]
    for j in range(32):
        c[j, C_E + 64 + j] = 1
        c[64 + j, C_SEL + j] = 1
    c[:, C_ONE:C_ONE + 128] = 1
    for p_ in range(128):
        c[p_, C_Z + 96 + p_] = 1
    half = 16
    freqs = (10000.0 ** (-np.arange(half, dtype=np.float32) / half)).astype(np.float32)
    pos = np.concatenate([np.arange(SEQ), np.tile(PAST + np.arange(4), 16)]).astype(np.float32)
    ang = (pos[None, :] * freqs[:, None]).astype(np.float32)
    cos, sin = np.cos(ang).astype(np.float32), np.sin(ang).astype(np.float32)
    cc = np.concatenate([cos, cos], 0)
    ss = np.concatenate([-sin, sin], 0)
    r = np.zeros((128, 4, SEQ + NS), np.float32)
    r[:64, 0] = 1
    r[64:96, 0] = cc
    r[64:96, 1] = ss
    r[:32, 2] = cc
    r[:32, 3] = ss
    return c, r


CFG = {'nphys': NPHYS, 'stop': None, 'ncores': 8}


def build_nc():
    nc = bass.Bass("TRN2", target_bir_lowering=False)
    NPH = CFG['nphys']
    STOP = CFG['stop']

    def din(name, shape, dt=F32):
        return nc.dram_tensor(name, list(shape), dt, kind="ExternalInput").ap()

    def dout(name, shape):
        return nc.dram_tensor(name, list(shape), F32, kind="ExternalOutput").ap()

    xp = din("xp", [SEQ, D]); xs = din("xs", [NS, D]); memp = din("memp", [256, D])
    clats = [din("clat%d" % i, [NPH * 8, 2048]) for i in range(2)]; cropes = [din("crope%d" % i, [NPH * 8, 512]) for i in range(2)]
    ptab = din("ptab", [128, 8], I32)
    shg = din("shg", [2, 16, 4, 128, 128]); scv = din("scv", [2, 16, 2, 512])
    cmk = din("cmk", [2, 16, 256, 512]); cmv = din("cmv", [2, 16, 256, 512])
    w_in = din("w_in", [2, D, NIN]); w_bo = din("w_bo", [2, 4, 512, D]); w_o = din("w_o", [2, D, D])
    w_uq = din("w_uq", [2, 192, 384]); w_uk = din("w_uk", [2, 128, 256]); w_uv = din("w_uv", [2, 128, 512])
    mwk = din("mwk", [2, D, 512]); mwv = din("mwv", [2, D, 512])
    vecs = din("vecs", [128, 2, 48])
    cst_d = din("cst", [128, CW]); rope_d = din("rope", [128, 4, SEQ + NS])

    yp = dout("yp", [SEQ, D]); ysm = dout("ysm", [NS, D])
    o_latp = dout("o_latp", [2, SEQ, 128]); o_ropep = dout("o_ropep", [2, SEQ, 32])
    o_hgp = dout("o_hgp", [2, 4, 128, 128]); o_cvp = dout("o_cvp", [2, 2, 512])
    o_mkp = dout("o_mkp", [2, 256, 512]); o_mvp = dout("o_mvp", [2, 256, 512])
    o_lats = dout("o_lats", [2, NS, 128]); o_ropes = dout("o_ropes", [2, NS, 32])
    o_hgs = dout("o_hgs", [2, 16, 4, 128, 128]); o_cvs = dout("o_cvs", [2, 16, 2, 512])

    P = Prog(nc)
    with ExitStack() as es:
        P.es = es
        sb, ps = P.sb, P.ps
        xtok = sb("xtok", [128, 17, D], F32)
        cst = sb("cstt", [128, CW], F32)
        cbf = sb("cbf", [128, CW], BF16)
        ropet = sb("ropet", [128, 4, NT], F32)
        vec = sb("vec", [128, 2, 48], F32)
        lbv = sb("lbv", [128, 2, 8], F32)
        wsts = [sb("wstA", [128, 2048], F32), sb("wstB", [128, 2048], F32)]
        wbfs = [sb("wbf0", [128, 4096], BF16), sb("wbf1", [128, 4096], BF16), sb("wbf2", [128, 4096], BF16)]
        hnT = sb("hnT", [128, 8, NT], BF16)
        ys = sb("ys", [128, 16, NT], BF16)
        merged = sb("merged", [128, 8, NT], F32)
        un = sb("un", [128, 10368], BF16)
        unf = un[:, :].bitcast(F32)

        class View:
            def __init__(self, ap, name):
                self.ap = ap
                self.name = name

            def __getitem__(self, k):
                return self.ap[k]
        KnAll = View(un[0:96, 0:8192].rearrange("p (h t) -> p h t", t=SEQ), "KnAll")
        ctokAll = View(un[:, 8192:8192 + 2176].rearrange("p (j r) -> p j r", r=136), "ctokAll")
        S = [sb("Sst%d" % i, [128, 128], F32) for i in range(4)]
        Sb = [sb("Sbf%d" % i, [128, 128], BF16) for i in range(4)]
        sc = [sb("sc%d" % i, [128, NT], F32) for i in range(10)]
        sh = [sb("sh%d" % i, [128, NT], BF16) for i in range(6)]
        uext = sb("uext", [128, 4, NT + 2], F32)
        vtok = sb("vtok", [64, 8, 128], BF16)
        ktok = sb("ktok", [64, 8, 128], BF16)
        aTm = sb("aTm", [64, 64], BF16)
        small = sb("small", [128, 64], F32)
        mkT = sb("mkT", [128, 4, 256], BF16)
        mvtok = sb("mvtok", [128, 2, 512], BF16)
        wq_b = sb("wq_b", [128, 2, 768], BF16)
        wuke = sb("wuke", [128, 4, 96], BF16)
        wuk_b = sb("wuk_b", [128, 256], BF16)
        wukT = sb("wukT", [64, 4, 128], BF16)
        wuv_b = sb("wuv_b", [128, 512], BF16)
        QnT = sb("QnT", [96, 4, NT], BF16)
        onT = sb("onT", [128, 4, NT], BF16)
        mqT = sb("mqT", [128, 4, NT], BF16)
        graw = View(unf[:, 0:2048], "graw")
        rraw = View(unf[:, 2048:2560], "rraw")
        gbf = View(un[:, 5120:5120 + 2176].rearrange("p (a r) -> p a r", r=136), "gbf")
        rbf = View(un[:, 7296:7296 + 512].rearrange("p (a r) -> p a r", r=32), "rbf")
        cTs = View(un[:, 7808:7808 + 512].rearrange("p (a r) -> p a r", r=128), "cTs")
        rTs = View(un[:, 8320:8320 + 128], "rTs")
        mraw = View(unf[:, 0:1024].rearrange("p (a r) -> p a r", r=512), "mraw")
        mkb = View(un[:, 2048:3072].rearrange("p (a r) -> p a r", r=512), "mkb")
        mvb = View(un[:, 3072:4096].rearrange("p (a r) -> p a r", r=512), "mvb")
        mkTs = View(un[:, 4096:5120].rearrange("p (a r) -> p a r", r=256), "mkTs")
        s0 = View(unf[:, 4416:4416 + 512].rearrange("p (a r) -> p a r", r=128), "s0")
        s0b = View(un[:, 9856:9856 + 512].rearrange("p (a r) -> p a r", r=128), "s0b")
        idx = sb("idx", [128, 8], I32)
        idx2 = sb("idx2", [128, 8, 8], I32)
        QaC = sb("QaC", [128, 8, 32], BF16)
        QaR = sb("QaR", [128, 8, 4, 32], BF16)
        QgT = sb("QgT", [96, 4, NS], BF16)
        pTn = sb("pTn", [64, 8, 32], BF16)
        cnew = sb("cnew", [64, 136], BF16)
        pTs = sb("pTs", [128, 4, 32], BF16)
        stat = sb("stat", [128, 64], F32)
        khm = sb("khm", [64, 128], BF16)
        pms = sb("pms", [128, 32], BF16)
        pb = [ps("pb%d" % i, [128, 512], F32) for i in range(6)]
        pbt = ps("pbt", [128, 1024], BF16)
        pb7 = ps("pb7", [128, 512], F32)

        def act(out, in_, func, r, w, scale=1.0, bias=None):
            if bias is None:
                P.op('scalar', lambda e: e.activation(out=out, in_=in_, func=func, scale=scale), r, w)
            else:
                P.op('scalar', lambda e: e.activation(out=out, in_=in_, func=func, scale=scale, bias=bias), r, w)

        def tt(eng, out, in0, in1, op, r, w):
            P.op(eng, lambda e: e.tensor_tensor(out=out, in0=in0, in1=in1, op=op), r, w)

        def ts(eng, out, in0, s1, s2, op0, op1, r, w):
            if s2 is None:
                P.op(eng, lambda e: e.tensor_scalar(out=out, in0=in0, scalar1=s1, scalar2=None, op0=op0), r, w)
            else:
                P.op(eng, lambda e: e.tensor_scalar(out=out, in0=in0, scalar1=s1, scalar2=s2, op0=op0, op1=op1), r, w)

        def stt(out, in0, scalar, in1, op0, op1, r, w):
            P.op('vector', lambda e: e.scalar_tensor_tensor(out=out, in0=in0, scalar=scalar, in1=in1, op0=op0, op1=op1), r, w)

        def cp(eng, out, in_, r, w):
            P.op(eng, lambda e: e.tensor_copy(out=out, in_=in_), r, w)

        def mm(out, lhsT, rhs, start, stop, r, w):
            P.op('tensor', lambda e: e.matmul(out, lhsT=lhsT, rhs=rhs, start=start, stop=stop), r, w)

        def tr(out, in_, ident, r, w):
            P.op('tensor', lambda e: e.transpose(out=out, in_=in_, identity=ident), r, w)

        def recip(out, in_, r, w):
            P.op('vector', lambda e: e.reciprocal(out=out, in_=in_), r, w)

        def rsqrt_to(out, in_, n, scale, r, w):
            act(out, in_, AF.Ln, r, w, scale=scale, bias=vec[0:n, 0, 47:48])
            act(out, out, AF.Exp, w, w, scale=-0.5)

        identb = lambda n: cbf[0:n, C_ID:C_ID + n]
        identf = lambda n: cst[0:n, C_ID:C_ID + n]
        onesb = lambda k, m: cbf[0:k, C_ONE:C_ONE + m]

        def V(l, j, n=128):
            return vec[0:n, l, j:j + 1]

        P.dma('sync', cst[:], cst_d, writes=[cst])
        P.dma('sync', vec[:], vecs, writes=[vec])
        P.dma('sync', idx[:], ptab, writes=[idx])
        cp('vector', cbf[:], cst[:], [cst], [cbf])
        for ch in range(8):
            ts('vector', idx2[:, :, ch], idx[:], 8.0, float(ch), ALU.mult, ALU.add, [idx], [idx2])
        pass
        P.op('gpsimd', lambda e: e.memset(cnew[:, 128:136], 1.0), (), [cnew])
        pass
        for t in range(16):
            P.dma('sync', xtok[:, t, :], xp[t * 128:(t + 1) * 128, :], writes=[('x', t)])
        P.dma('sync', xtok[0:64, 16, :], xs, writes=[('x', 16)])
        mbf = merged[:, :, :].rearrange("p a b -> p (a b)").bitcast(BF16)
        junk = View(mbf[:, 2048:3072], "merged")
        hnb = View(mbf[:, 3072:4096], "merged")
        P.op('gpsimd', lambda e: e.memset(lbv[:], 0.0), (), [lbv])
        tt('vector', lbv[:, 1, 0:4], vec[:, 0, 32:36], vec[:, 0, 28:32], ALU.subtract, [vec], [lbv])
        act(lbv[:, 1, 0:4], lbv[:, 1, 0:4], AF.Sigmoid, [lbv], [lbv])
        ts('vector', lbv[:, :, 4:8], lbv[:, :, 0:4], -1.0, 1.0, ALU.mult, ALU.add, [lbv], [lbv])
        P.op('gpsimd', lambda e: e.memset(wq_b[:], 0.0), (), [wq_b])
        P.op('gpsimd', lambda e: e.memset(wuke[:], 0.0), (), [wuke])

        wctr = [0]

        def WSTK(k):
            return [('wst', k, i) for i in range(8)]

        def wk(wb):
            return [(wb.name, 0), (wb.name, 1)]

        def stage(parts, nk, W):
            k = wctr[0] % 3
            wctr[0] += 1
            wb = wbfs[k]
            hk = nk // 2
            for hf in range(2):
                wsv = wsts[hf][:, 0:hk * W].rearrange("p (k n) -> p k n", n=W)
                for i, (off, n, src) in enumerate(parts):
                    P.dma('sync', wsv[:, :, off:off + n], src[:, hf * hk:(hf + 1) * hk, :], writes=[('wst', hf, i)])
            act(wb[:, 0:hk * W], wsts[0][:, 0:hk * W], AF.Copy, WSTK(0), [(wb.name, 0)])
            cp('vector', wb[:, hk * W:nk * W], wsts[1][:, 0:hk * W], WSTK(1), [(wb.name, 1)])
            return wb[:, 0:nk * W].rearrange("p (k n) -> p k n", n=W), wk(wb)

        def stage_rows(src2d, nk, ncols):
            return stage([(0, ncols, src2d.rearrange("(k p) n -> p k n", p=128))], nk, ncols)

        def tile_plan(l):
            pl = []
            for c in range(4):
                pl.append(('win', l, ((O_CH + c * 128, 128), (O_CB + c * 128, 128), (O_CC + c * 128, 128), (O_SG + c * 128, 128))))
            for h in range(4):
                pl.append(('win', l, ((O_HQ + h * 128, 128), (O_HF + h * 128, 128), (O_HI + h * 128, 128), (O_SG + 512 + h * 128, 128))))
            pl.append(('win', l, ((O_QL, 192), (O_KV, 128), (O_KPE, 32), (O_KPE + 16, 16), (O_KPE, 16))))
            pl.append(('win', l, ((O_SG + 1024, 512),)))
            pl.append(('win', l, ((O_MQ, 512),)))
            pl.append(('win', l, ((O_SG + 1536, 512),)))
            for n in range(4):
                for half in range(2):
                    pl.append(('wbo', l, n, half))
                    pl.append(('win', l, ((O_MG + n * 1024 + half * 512, 512),)))
            for half in range(2):
                pl.append(('wo', l, half))
            return pl

        PLAN = []
        for l_ in range(2):
            PLAN += [('mwk', l_), ('mwv', l_)]
            for t_ in range(NTI + 1):
                PLAN += tile_plan(l_)
        wp_state = {'i': 0, 'issued': {}}

        def wp_issue(j):
            if j < len(PLAN) and j not in wp_state['issued']:
                d = PLAN[j]
                if d[0] == 'win':
                    r_ = stage_win_now(d[1], list(d[2]))
                elif d[0] == 'wbo':
                    r_ = stage_rows(w_bo[d[1], d[2]][:, d[3] * 512:(d[3] + 1) * 512], 4, 512)
                elif d[0] == 'wo':
                    r_ = stage_rows(w_o[d[1]][:, d[2] * 512:(d[2] + 1) * 512], 8, 512)
                elif d[0] == 'mwk':
                    r_ = stage_rows(mwk[d[1]], 8, 512)
                else:
                    r_ = stage_rows(mwv[d[1]], 8, 512)
                wp_state['issued'][j] = r_

        def wnext(desc):
            j = wp_state['i']
            wp_state['i'] += 1
            assert PLAN[j] == desc, (j, PLAN[j], desc)
            wp_issue(j)
            wp_issue(j + 1)
            return wp_state['issued'].pop(j)

        def stage_win(l, segs):
            return wnext(('win', l, tuple(segs)))

        def stage_win_now(l, segs):
            W = sum(n for _, n in segs)
            src = w_in[l].rearrange("(k p) n -> p k n", p=128)
            parts = []
            off = 0
            for (c0, n) in segs:
                parts.append((off, n, src[:, :, c0:c0 + n]))
                off += n
            return stage(parts, 8, W)

        def make_hnT(xin, n, j, l, gcol, xkeys):
            act(junk[0:n, :], xin, AF.Square, xkeys, [junk, 'ssq'], scale=1.0)
            P.op('vector', lambda e: e.tensor_reduce(out=small[0:n, 0:1], in_=junk[0:n, :], axis=AX.X, op=ALU.add), [junk], ['ssq'])
            rsqrt_to(small[0:n, 0:1], small[0:n, 0:1], n, 1.0 / D, ['ssq'], ['ssq'])
            ts('vector', hnb[0:n, :], xin, small[0:n, 0:1], None, ALU.mult, None, xkeys + ['ssq'], [hnb])
            for kc in range(8):
                tr(pbt[:, kc * 128:kc * 128 + n], hnb[0:n, kc * 128:(kc + 1) * 128], identb(n), [hnb], [pbt])
            for kc in range(8):
                o_ = hnT[:, kc, j * 128:j * 128 + n]
                i_ = pbt[:, kc * 128:kc * 128 + n]
                if kc % 2 == 0:
                    ts('vector', o_, i_, V(l, gcol + kc), None, ALU.mult, None, [pbt], [hnT])
                else:
                    act(o_, i_, AF.Copy, [pbt], [hnT], scale=V(l, gcol + kc))

        def load_small(l):
            k = 0
            wst = wsts[0]
            WST = WSTK(0)
            parts = [(wst[:, 0:384], w_uq[l, 0:128, :]), (wst[0:64, 384:768], w_uq[l, 128:192, :]),
                     (wst[:, 768:1024], w_uk[l]), (wst[:, 1024:1536], w_uv[l])]
            for i, (d_, s_) in enumerate(parts):
                P.dma('sync', d_, s_, writes=[('wst', k, i)])
            cp('vector', wq_b[:, 0, 0:384], wst[:, 0:384], WST, [wq_b])
            cp('vector', wq_b[0:64, 1, 0:384], wst[0:64, 384:768], WST, [wq_b])
            for kc, n in ((0, 128), (1, 64)):
                sv = wst[0:n, kc * 384:(kc + 1) * 384].rearrange("p (h d) -> p h d", d=96)
                dv = wq_b[0:n, kc, 384:768].rearrange("p (h d) -> p h d", d=96)
                cp('vector', dv[:, :, 64:80], sv[:, :, 80:96], WST, [wq_b])
                cp('vector', dv[:, :, 80:96], sv[:, :, 64:80], WST, [wq_b])
            cp('vector', wuk_b[:], wst[:, 768:1024], WST, [wuk_b])
            cp('vector', wuke[:, :, 0:64], wst[:, 768:1024].rearrange("p (h d) -> p h d", d=64), WST, [wuke])
            cp('vector', wuv_b[:], wst[:, 1024:1536], WST, [wuv_b])
            for h in range(4):
                tr(pbt[0:64, h * 128:(h + 1) * 128], wuk_b[:, h * 64:(h + 1) * 64], identb(128), [wuk_b], [pbt])
            cp('vector', wukT[:], pbt[0:64, 0:512].rearrange("p (h r) -> p h r", r=128), [pbt], [wukT])

        def mem_kv(l):
            mt = merged[:, 0:4, :].rearrange("p a b -> p (a b)")
            for j in range(2):
                P.dma('sync', mt, memp[j * 128:(j + 1) * 128, :], writes=[merged])
                make_hnT(mt, 128, j, l, 8, [merged])
            wv, wkeys = wnext(('mwk', l))
            for h in range(4):
                for kc in range(8):
                    mm(pb[0][:, 0:256], wv[:, kc, h * 128:(h + 1) * 128], hnT[:, kc, 0:256], kc == 0, kc == 7, wkeys + [hnT], [pb[0]])
                act(sh[0][:, 0:256], pb[0][:, 0:256], AF.Square, [pb[0]], [sh[0]])
                mm(pb7[:, 0:256], onesb(128, 128), sh[0][:, 0:256], True, True, [sh[0]], [pb7])
                rsqrt_to(sc[0][:, 0:256], pb7[:, 0:256], 128, 1.0 / 128, [pb7], [sc[0]])
                stt(sc[1][:, 0:256], pb[0][:, 0:256], V(l, 46), sc[0][:, 0:256], ALU.mult, ALU.mult, [pb[0], sc[0]], [sc[1]])
                cp('gpsimd', mkT[:, h, :], sc[1][:, 0:256], [sc[1]], [mkT])
                for mc in range(2):
                    tr(pb[1][:, mc * 128:(mc + 1) * 128], sc[1][:, mc * 128:(mc + 1) * 128], identf(128), [sc[1]], [pb[1]])
                cp('vector', sc[2][:, 0:256], pb[1][:, 0:256], [pb[1]], [sc[2]])
                for mc in range(2):
                    P.dma('gpsimd', o_mkp[l, mc * 128:(mc + 1) * 128, h * 128:(h + 1) * 128], sc[2][:, mc * 128:(mc + 1) * 128], reads=[sc[2]])
            wv, wkeys = wnext(('mwv', l))
            for mc in range(2):
                for kc in range(8):
                    mm(pb[0][:, :], hnT[:, kc, mc * 128:(mc + 1) * 128], wv[:, kc, :], kc == 0, kc == 7, wkeys + [hnT], [pb[0]])
                for hf in range(2):
                    cp('vector', sc[3 + hf][:, :], pb[0][:, hf * 256:(hf + 1) * 256], [pb[0]], [sc[3 + hf]])
                    cp('gpsimd', mvtok[:, mc, hf * 256:(hf + 1) * 256], sc[3 + hf][:, :], [sc[3 + hf]], [mvtok])
                    P.dma('gpsimd', o_mvp[l, mc * 128:(mc + 1) * 128, hf * 256:(hf + 1) * 256], sc[3 + hf][:, :], reads=[sc[3 + hf]])
        KnS = sb("KnS", [96, 4, NS], BF16)
        MLA_SCALE = 96 ** -0.5
        MEM_SCALE = 128 ** -0.5

        def tile_pass(l, kind, ti):
            N = NT if kind == 'p' else NS
            smp = kind == 's'
            p0 = ti * NT if not smp else SEQ
            P.dma('sync', ropet[:, :, 0:N], rope_d[:, :, p0:p0 + N], writes=[ropet])
            if not smp:
                subt = [(ti * TPT + j, 128) for j in range(TPT)]
            else:
                subt = [(16, 64)]
            for j, (t, n) in enumerate(subt):
                make_hnT(xtok[0:n, t, :], n, j, l, 0, [('x', t)])
            zb = [0]

            def zchunk(wv, wkeys, off, m):
                p_ = pb[zb[0] % 2]
                zb[0] += 1
                for kc in range(8):
                    mm(p_[0:m, 0:N], wv[:, kc, off:off + m], hnT[:, kc, 0:N], kc == 0, kc == 7, wkeys + [hnT], [p_])
                return p_

            def v3(ap):
                return ap.rearrange("p (b s) -> p b s", s=4)

            for c in range(4):
                wv, wkeys = stage_win(l, [(O_CH + c * 128, 128), (O_CB + c * 128, 128), (O_CC + c * 128, 128), (O_SG + c * 128, 128)])
                ph = zchunk(wv, wkeys, 0, 128)
                act(sc[0][:, 0:N], ph[:, 0:N], AF.Copy, [ph], [sc[0]])
                pc_ = zchunk(wv, wkeys, 256, 128)
                ukey = ('u', c)
                if not smp:
                    if ti == 0:
                        P.op('gpsimd', lambda e, c=c: e.memset(uext[:, c, 0:2], 0.0), (), [ukey])
                    tt('vector', uext[:, c, 2:2 + N], pc_[:, 0:N], sc[0][:, 0:N], ALU.mult, [pc_, sc[0]], [ukey])
                    u0, u1, u2 = uext[:, c, 0:N], uext[:, c, 1:N + 1], uext[:, c, 2:N + 2]
                    cv, yv = sc[1][:, 0:N], sc[2][:, 0:N]
                else:
                    uv = uext[:, c, 0:96].rearrange("p (b s) -> p b s", s=6)
                    if c == 0:
                        mh = merged[0:32, 0:2, :].rearrange("p a b -> p (a b)")
                        mh2 = merged[0:32, 2:4, :].rearrange("p a b -> p (a b)")
                        P.dma('sync', mh, scv[l].rearrange("b j c -> (b j) c"), writes=[('mh', 0)])
                    tr(pb[2][:, 0:32], mh[:, c * 128:(c + 1) * 128], identf(32), [('mh', 0)], [pb[2]])
                    cp('vector', uv[:, :, 0:2], pb[2][:, 0:32].rearrange("p (b j) -> p b j", j=2), [pb[2]], [('uh', c, 0)])
                    tt('vector', uv[:, :, 2:6], v3(pc_[:, 0:N]), v3(sc[0][:, 0:N]), ALU.mult, [pc_, sc[0], ('uh', c, 0)], [ukey])
                    u0, u1, u2 = uv[:, :, 0:4], uv[:, :, 1:5], uv[:, :, 2:6]
                    cv, yv = v3(sc[1][:, 0:N]), v3(sc[2][:, 0:N])
                ts('vector', cv, u0, V(l, 16 + c), None, ALU.mult, None, [ukey], [sc[1]])
                stt(cv, u1, V(l, 20 + c), cv, ALU.mult, ALU.add, [ukey, sc[1]], [sc[1]])
                stt(cv, u2, V(l, 24 + c), cv, ALU.mult, ALU.add, [ukey, sc[1]], [sc[1]])
                pb_ = zchunk(wv, wkeys, 128, 128)
                tt('vector', sc[2][:, 0:N], pb_[:, 0:N], sc[1][:, 0:N], ALU.mult, [pb_, sc[1]], [sc[2]])
                psg = zchunk(wv, wkeys, 384, 128)
                act(sc[3][:, 0:N], psg[:, 0:N], AF.Silu, [psg], [sc[3]])
                tt('gpsimd', ys[:, c, 0:N], sc[2][:, 0:N], sc[3][:, 0:N], ALU.mult, [sc[2], sc[3]], [ys])
                if not smp:
                    if ti == NTI - 1:
                        P.dma('gpsimd', o_cvp[l][:, c * 128:(c + 1) * 128].rearrange("j p -> p j"), uext[:, c, N:N + 2],
                              reads=[ukey], noncontig=True)
                    else:
                        cp('gpsimd', uext[:, c, 0:2], uext[:, c, N:N + 2], [ukey], [ukey])
                else:
                    cp('vector', sc[8][:, 0:32].rearrange("p (b j) -> p b j", j=2), uv[:, :, 4:6], [ukey], [sc[8]])
                    tr(pb[3][0:32, c * 128:(c + 1) * 128], sc[8][:, 0:32], identf(128), [sc[8]], [pb[3]])
                    if c == 3:
                        cp('vector', mh2, pb[3][0:32, 0:512], [pb[3]], [('mh', 1)])
                        P.dma('gpsimd', o_cvs[l].rearrange("b j c -> (b j) c"), mh2, reads=[('mh', 1)])

            if smp:
                chk('s%d_conv' % l)
            for h in range(4):
                wv, wkeys = stage_win(l, [(O_HQ + h * 128, 128), (O_HF + h * 128, 128), (O_HI + h * 128, 128), (O_SG + 512 + h * 128, 128)])
                pq = zchunk(wv, wkeys, 0, 128)
                act(sc[0][:, 0:N], pq[:, 0:N], AF.Copy, [pq], [sc[0]], scale=128 ** -0.5)
                pf = zchunk(wv, wkeys, 128, 128)
                act(sc[1][:, 0:N], pf[:, 0:N], AF.Sigmoid, [pf], [sc[1]])
                ts('vector', sc[1][:, 0:N], sc[1][:, 0:N], lbv[:, l, 4 + h:5 + h], lbv[:, l, h:h + 1], ALU.mult, ALU.add, [sc[1], lbv], [sc[1]])
                act(sc[2][:, 0:N], sc[1][:, 0:N], AF.Ln, [sc[1]], [sc[2]])
                ts('vector', sc[1][:, 0:N], sc[1][:, 0:N], -1.0, 1.0, ALU.mult, ALU.add, [sc[1], sc[2]], [sc[1]])
                segm = cst[:, C_SEGP:C_SEGP + N] if not smp else cst[:, C_SEGS:C_SEGS + N]
                P.op('vector', lambda e, segm=segm: e.tensor_tensor_scan(out=sc[3][:, 0:N], data0=segm, data1=sc[2][:, 0:N], initial=0.0,
                                                                         op0=ALU.mult, op1=ALU.add), [sc[2]], [sc[3]])
                act(sc[4][:, 0:N], sc[3][:, 0:N], AF.Exp, [sc[3]], [sc[4]])
                if not smp:
                    bv = sc[3][:, 0:N].rearrange("p (c s) -> p c s", s=64)
                    dv = sc[2][:, 0:N].rearrange("p (c s) -> p c s", s=64)
                    tt('vector', dv, bv, bv[:, :, 31:32].broadcast_to([128, N // 64, 64]), ALU.subtract, [sc[3]], [sc[2]])
                    act(sc[5][:, 0:N], sc[2][:, 0:N], AF.Exp, [sc[2]], [sc[5]])
                    act(sc[6][:, 0:N], sc[2][:, 0:N], AF.Exp, [sc[2]], [sc[6]], scale=-1.0)
                    E1 = sc[5]
                else:
                    E1 = sc[4]
                    act(sc[6][:, 0:N], sc[3][:, 0:N], AF.Exp, [sc[3]], [sc[6]], scale=-1.0)
                    bv = v3(sc[3][:, 0:N])
                    tt('vector', v3(sc[2][:, 0:N]), bv[:, :, 3:4].broadcast_to([128, 16, 4]), bv, ALU.subtract, [sc[3]], [sc[2]])
                    act(sc[5][:, 0:N], sc[2][:, 0:N], AF.Exp, [sc[2]], [sc[5]])
                    tt('gpsimd', sh[3][:, 0:N], sc[1][:, 0:N], sc[5][:, 0:N], ALU.mult, [sc[1], sc[5]], [sh[3]])
                tt('vector', sh[0][:, 0:N], sc[0][:, 0:N], E1[:, 0:N], ALU.mult, [sc[0], E1], [sh[0]])
                tt('gpsimd', sh[1][:, 0:N], sc[0][:, 0:N], sc[4][:, 0:N], ALU.mult, [sc[0], sc[4]], [sh[1]])
                tt('vector', sh[2][:, 0:N], sc[1][:, 0:N], sc[6][:, 0:N], ALU.mult, [sc[1], sc[6]], [sh[2]])
                nch = N // 64
                for half in range((nch + 3) // 4):
                    ncc = min(4, nch - half * 4)
                    for cc in range(ncc):
                        c_ = half * 4 + cc
                        for kc in range(8):
                            mm(pb[2][0:64, cc * 128:(cc + 1) * 128], hnT[:, kc, c_ * 64:(c_ + 1) * 64], wv[:, kc, 256:384], kc == 0, kc == 7,
                               wkeys + [hnT], [pb[2]])
                    act(vtok[:, half * 4:half * 4 + ncc, :], pb[2][0:64, 0:ncc * 128].rearrange("p (c v) -> p c v", v=128), AF.Copy, [pb[2]], [vtok])
                ksrc = sh[2] if not smp else sh[3]
                for c_ in range(nch):
                    tr(pbt[0:64, c_ * 128:(c_ + 1) * 128], ksrc[:, c_ * 64:(c_ + 1) * 64], identb(128), [ksrc], [pbt])
                cp('vector', ktok[:, 0:nch, :], pbt[0:64, 0:nch * 128].rearrange("p (c v) -> p c v", v=128), [pbt], [ktok])
                oT = pb[4]
                if not smp:
                    for c_ in range(N // 64):
                        first = (ti == 0 and c_ == 0)
                        cs_ = slice(c_ * 64, (c_ + 1) * 64)
                        mm(pb[3][0:64, 0:64], sh[2][:, cs_], sh[0][:, cs_], True, True, [sh[2], sh[0]], [pb[3]])
                        tt('vector', aTm[:, :], pb[3][0:64, 0:64], cst[0:64, C_T64:C_T64 + 64], ALU.mult, [pb[3]], [aTm])
                        if not first:
                            mm(oT[:, cs_], Sb[h][:, :], sh[1][:, cs_], True, False, [Sb[h], sh[1]], [oT])
                        mm(oT[:, cs_], vtok[:, c_, :], aTm[:, :], first, True, [vtok, aTm], [oT])
                        mm(pb[5][:, 0:128], ktok[:, c_, :], vtok[:, c_, :], True, True, [ktok, vtok], [pb[5]])
                        eL = sc[4][:, c_ * 64 + 63:c_ * 64 + 64]
                        eLm = E1[:, c_ * 64 + 63:c_ * 64 + 64]
                        if first:
                            ts('vector', S[h][:, :], pb[5][:, 0:128], eLm, None, ALU.mult, None, [pb[5], E1], [S[h]])
                        else:
                            ts('vector', S[h][:, :], S[h][:, :], eL, None, ALU.mult, None, [S[h], sc[4]], [S[h]])
                            stt(S[h][:, :], pb[5][:, 0:128], eLm, S[h][:, :], ALU.mult, ALU.add, [pb[5], E1, S[h]], [S[h]])
                        act(Sb[h][:, :], S[h][:, :], AF.Copy, [S[h]], [Sb[h]])
                    if ti == NTI - 1:
                        P.dma('gpsimd', o_hgp[l, h], S[h][:, :], reads=[S[h]])
                else:
                    mm(pb[3][0:64, 0:64], sh[2][:, 0:64], sh[0][:, 0:64], True, True, [sh[2], sh[0]], [pb[3]])
                    tt('vector', aTm[:, :], pb[3][0:64, 0:64], cst[0:64, C_BD:C_BD + 64], ALU.mult, [pb[3]], [aTm])
                    mm(oT[:, 0:64], vtok[:, 0, :], aTm[:, :], True, False, [vtok, aTm], [oT])
                    for b in range(16):
                        slot = b % 4
                        k0, k0b = ('s0', slot), ('s0b', slot)
                        P.dma('sync', s0[:, slot, :], shg[l, b, h], writes=[k0])
                        cp('gpsimd', s0b[:, slot, :], s0[:, slot, :], [k0], [k0b])
                        mm(oT[:, 4 * b:4 * b + 4], s0b[:, slot, :], sh[1][:, 4 * b:4 * b + 4], False, b == 15, [k0b, sh[1]], [oT])
                        ts('gpsimd', khm[:, :], ktok[:, 0, :], cst[0:64, C_IND + b:C_IND + b + 1], None, ALU.mult, None, [ktok], [khm])
                        mm(pb[5][:, 0:128], khm[:, :], vtok[:, 0, :], True, True, [khm, vtok], [pb[5]])
                        stt(s0[:, slot, :], s0[:, slot, :], sc[4][:, 4 * b + 3:4 * b + 4], pb[5][:, 0:128], ALU.mult, ALU.add,
                            [k0, sc[4], pb[5]], [k0])
                        P.dma('gpsimd', o_hgs[l, b, h], s0[:, slot, :], reads=[k0])
                act(sc[7][:, 0:N], oT[:, 0:N], AF.Copy, [oT], [sc[7]])
                act(sh[4][:, 0:N], oT[:, 0:N], AF.Square, [oT], [sh[4]])
                mm(pb7[:, 0:N], onesb(128, 128), sh[4][:, 0:N], True, True, [sh[4]], [pb7])
                rsqrt_to(sc[8][:, 0:N], pb7[:, 0:N], 128, 1.0 / 128, [pb7], [sc[8]])
                stt(sc[9][:, 0:N], sc[7][:, 0:N], V(l, 36 + h), sc[8][:, 0:N], ALU.mult, ALU.mult, [sc[7], sc[8]], [sc[9]])
                psg = zchunk(wv, wkeys, 384, 128)
                act(sc[7][:, 0:N], psg[:, 0:N], AF.Silu, [psg], [sc[7]])
                tt('gpsimd', ys[:, 4 + h, 0:N], sc[9][:, 0:N], sc[7][:, 0:N], ALU.mult, [sc[9], sc[7]], [ys])

            if smp:
                chk('s%d_hgrn' % l)
            wv, wkeys = stage_win(l, [(O_QL, 192), (O_KV, 128), (O_KPE, 32), (O_KPE + 16, 16), (O_KPE, 16)])
            p_ = zchunk(wv, wkeys, 0, 128)
            act(sc[0][:, 0:N], p_[:, 0:N], AF.Copy, [p_], [sc[0]])
            act(sh[0][:, 0:N], p_[:, 0:N], AF.Square, [p_], [sh[0]])
            p_ = zchunk(wv, wkeys, 128, 64)
            act(sc[1][0:64, 0:N], p_[0:64, 0:N], AF.Copy, [p_], [sc[1]])
            act(sh[1][0:64, 0:N], p_[0:64, 0:N], AF.Square, [p_], [sh[1]])
            mm(pb7[:, 0:N], onesb(128, 128), sh[0][:, 0:N], True, False, [sh[0]], [pb7])
            mm(pb7[:, 0:N], onesb(64, 128), sh[1][0:64, 0:N], False, True, [sh[1]], [pb7])
            rsqrt_to(sc[2][:, 0:N], pb7[:, 0:N], 128, 1.0 / 192, [pb7], [sc[2]])
            stt(sh[0][:, 0:N], sc[0][:, 0:N], V(l, 40), sc[2][:, 0:N], ALU.mult, ALU.mult, [sc[0], sc[2]], [sh[0]])
            stt(sh[1][0:64, 0:N], sc[1][0:64, 0:N], V(l, 41, 64), sc[2][0:64, 0:N], ALU.mult, ALU.mult, [sc[1], sc[2]], [sh[1]])
            for h in range(4):
                for (dst, woff) in ((pb[2], 0), (pb[3], 384)):
                    mm(dst[0:96, 0:N], wq_b[:, 0, woff + h * 96:woff + (h + 1) * 96], sh[0][:, 0:N], True, False, [wq_b, sh[0]], [dst])
                    mm(dst[0:96, 0:N], wq_b[0:64, 1, woff + h * 96:woff + (h + 1) * 96], sh[1][0:64, 0:N], False, True, [wq_b, sh[1]], [dst])
                tt('vector', sc[3][0:96, 0:N], pb[2][0:96, 0:N], ropet[0:96, 0, 0:N], ALU.mult, [pb[2], ropet], [sc[3]])
                tt('vector', sc[4][0:96, 0:N], pb[3][0:96, 0:N], ropet[0:96, 1, 0:N], ALU.mult, [pb[3], ropet], [sc[4]])
                tt('gpsimd', sc[3][0:96, 0:N], sc[3][0:96, 0:N], sc[4][0:96, 0:N], ALU.add, [sc[3], sc[4]], [sc[3]])
                act(sh[2][0:96, 0:N], sc[3][0:96, 0:N], AF.Square, [sc[3]], [sh[2]])
                mm(pb7[0:96, 0:N], onesb(96, 96), sh[2][0:96, 0:N], True, True, [sh[2]], [pb7])
                rsqrt_to(sc[4][0:96, 0:N], pb7[0:96, 0:N], 96, 1.0 / 96, [pb7], [sc[4]])
                stt(QnT[:, h, 0:N], sc[3][0:96, 0:N], V(l, 43, 96), sc[4][0:96, 0:N], ALU.mult, ALU.mult, [sc[3], sc[4]], [QnT])
            p_ = zchunk(wv, wkeys, 192, 128)
            act(sc[0][:, 0:N], p_[:, 0:N], AF.Copy, [p_], [sc[0]])
            act(sh[2][:, 0:N], p_[:, 0:N], AF.Square, [p_], [sh[2]])
            mm(pb7[:, 0:N], onesb(128, 128), sh[2][:, 0:N], True, True, [sh[2]], [pb7])
            rsqrt_to(sc[1][:, 0:N], pb7[:, 0:N], 128, 1.0 / 128, [pb7], [sc[1]])
            stt(sc[2][:, 0:N], sc[0][:, 0:N], V(l, 42), sc[1][:, 0:N], ALU.mult, ALU.mult, [sc[0], sc[1]], [sc[2]])
            cp('gpsimd', sh[3][:, 0:N], sc[2][:, 0:N], [sc[2]], [sh[3]])
            nsub = len(subt)
            nn = subt[0][1]
            for j in range(nsub):
                tr(pb[2][0:nn, j * 128:(j + 1) * 128], sc[2][:, j * 128:j * 128 + nn], identf(128), [sc[2]], [pb[2]])
            cp('vector', sc[5][0:nn, 0:nsub * 128], pb[2][0:nn, 0:nsub * 128], [pb[2]], [sc[5]])
            c3 = sc[5][0:nn, 0:nsub * 128].rearrange("p (j r) -> p j r", r=128)
            if not smp:
                P.dma('gpsimd', o_latp[l, p0:p0 + N, :].rearrange("(j p) r -> p j r", p=128), c3, reads=[sc[5]])
                cp('gpsimd', ctokAll[:, ti * TPT:ti * TPT + TPT, 0:128], c3, [sc[5]], [ctokAll])
            else:
                P.dma('gpsimd', o_lats[l], sc[5][0:64, 0:128], reads=[sc[5]])
                cp('gpsimd', cnew[:, 0:128], sc[5][0:64, 0:128], [sc[5]], [cnew])
            pk = zchunk(wv, wkeys, 320, 32)
            pks = zchunk(wv, wkeys, 352, 32)
            tt('vector', sc[6][0:32, 0:N], pk[0:32, 0:N], ropet[0:32, 2, 0:N], ALU.mult, [pk, ropet], [sc[6]])
            tt('vector', sc[7][0:32, 0:N], pks[0:32, 0:N], ropet[0:32, 3, 0:N], ALU.mult, [pks, ropet], [sc[7]])
            tt('gpsimd', sc[6][0:32, 0:N], sc[6][0:32, 0:N], sc[7][0:32, 0:N], ALU.add, [sc[6], sc[7]], [sc[6]])
            cp('gpsimd', sh[4][0:32, 0:N], sc[6][0:32, 0:N], [sc[6]], [sh[4]])
            for j in range(nsub):
                tr(pb[3][0:nn, j * 32:(j + 1) * 32], sc[6][0:32, j * 128:j * 128 + nn], identf(32), [sc[6]], [pb[3]])
            cp('vector', sc[7][0:nn, 0:nsub * 32], pb[3][0:nn, 0:nsub * 32], [pb[3]], [sc[7]])
            if not smp:
                P.dma('gpsimd', o_ropep[l, p0:p0 + N, :].rearrange("(j p) r -> p j r", p=128),
                      sc[7][:, 0:TPT * 32].rearrange("p (j r) -> p j r", r=32), reads=[sc[7]])
            else:
                P.dma('gpsimd', o_ropes[l], sc[7][0:64, 0:32], reads=[sc[7]])
            for h in range(4):
                mm(pb[2][0:96, 0:N], wuke[:, h, :], sh[3][:, 0:N], True, False, [wuke, sh[3]], [pb[2]])
                mm(pb[2][0:96, 0:N], cbf[0:32, C_E:C_E + 96], sh[4][0:32, 0:N], False, True, [sh[4]], [pb[2]])
                act(sh[5][0:96, 0:N], pb[2][0:96, 0:N], AF.Square, [pb[2]], [sh[5]])
                mm(pb7[0:96, 0:N], onesb(96, 96), sh[5][0:96, 0:N], True, True, [sh[5]], [pb7])
                rsqrt_to(sc[8][0:96, 0:N], pb7[0:96, 0:N], 96, 1.0 / 96, [pb7], [sc[8]])
                dstK = KnAll[:, h, p0:p0 + N] if not smp else KnS[:, h, :]
                stt(dstK, pb[2][0:96, 0:N], V(l, 44, 96), sc[8][0:96, 0:N], ALU.mult, ALU.mult, [pb[2], sc[8]], [KnAll if not smp else KnS])
            if not smp:
                nk = TPT * ti + TPT
                for h in range(4):
                    for j in range(nk):
                        lo = max(0, j - TPT * ti) * 128
                        spb = pb[2 + (j % 2)]
                        pT = sh[j % 2]
                        mm(spb[:, lo:N], KnAll[:, h, j * 128:(j + 1) * 128], QnT[:, h, lo:N], True, True, [KnAll, QnT], [spb])
                        act(pT[:, lo:N], spb[:, lo:N], AF.Exp, [spb], [pT], scale=MLA_SCALE)
                        if j >= TPT * ti:
                            tt('gpsimd', pT[:, lo:lo + 128], pT[:, lo:lo + 128], cbf[:, C_TRI:C_TRI + 128], ALU.mult, [pT], [pT])
                        mm(pb[4][:, lo:N], ctokAll[:, j, 0:128], pT[:, lo:N], j == 0, j == nk - 1, [ctokAll, pT], [pb[4]])
                        mm(pb[5][:, lo:N], onesb(128, 128), pT[:, lo:N], j == 0, j == nk - 1, [pT], [pb[5]])
                    recip(sc[9][:, 0:N], pb[5][:, 0:N], [pb[5]], [sc[9]])
                    tt('vector', onT[:, h, 0:N], pb[4][:, 0:N], sc[9][:, 0:N], ALU.mult, [pb[4], sc[9]], [onT])
            else:
                chk('s%d_mla0' % l)
                P.barrier()
                mla_sample(l)
                P.barrier()
                chk('s%d_mla1' % l)
            for h in range(4):
                mm(pb[2][:, 0:N], wuv_b[:, h * 128:(h + 1) * 128], onT[:, h, 0:N], True, True, [wuv_b, onT], [pb[2]])
                act(sc[h][:, 0:N], pb[2][:, 0:N], AF.Copy, [pb[2]], [sc[h]])
            wv, wkeys = stage_win(l, [(O_SG + 1024, 512)])
            for h in range(4):
                psg = zchunk(wv, wkeys, h * 128, 128)
                act(sc[4 + h % 2][:, 0:N], psg[:, 0:N], AF.Silu, [psg], [sc[4 + h % 2]])
                tt('gpsimd', ys[:, 8 + h, 0:N], sc[h][:, 0:N], sc[4 + h % 2][:, 0:N], ALU.mult, [sc[h], sc[4 + h % 2]], [ys])

            wv, wkeys = stage_win(l, [(O_MQ, 512)])
            for h in range(4):
                p_ = zchunk(wv, wkeys, h * 128, 128)
                act(sc[0][:, 0:N], p_[:, 0:N], AF.Copy, [p_], [sc[0]])
                act(sh[0][:, 0:N], p_[:, 0:N], AF.Square, [p_], [sh[0]])
                mm(pb7[:, 0:N], onesb(128, 128), sh[0][:, 0:N], True, True, [sh[0]], [pb7])
                rsqrt_to(sc[1][:, 0:N], pb7[:, 0:N], 128, 1.0 / 128, [pb7], [sc[1]])
                stt(mqT[:, h, 0:N], sc[0][:, 0:N], V(l, 45), sc[1][:, 0:N], ALU.mult, ALU.mult, [sc[0], sc[1]], [mqT])
            if not smp:
                for h in range(4):
                    for mc in range(2):
                        spb = pb[2 + mc]
                        mm(spb[:, 0:N], mkT[:, h, mc * 128:(mc + 1) * 128], mqT[:, h, 0:N], True, True, [mkT, mqT], [spb])
                        act(sh[1 + mc][:, 0:N], spb[:, 0:N], AF.Exp, [spb], [sh[1 + mc]], scale=MEM_SCALE)
                        mm(pb[4][:, 0:N], mvtok[:, mc, h * 128:(h + 1) * 128], sh[1 + mc][:, 0:N], mc == 0, mc == 1, [mvtok, sh[1 + mc]], [pb[4]])
                        mm(pb[5][:, 0:N], onesb(128, 128), sh[1 + mc][:, 0:N], mc == 0, mc == 1, [sh[1 + mc]], [pb[5]])
                    recip(sc[2][:, 0:N], pb[5][:, 0:N], [pb[5]], [sc[2]])
                    tt('vector', sc[4 + h][:, 0:N], pb[4][:, 0:N], sc[2][:, 0:N], ALU.mult, [pb[4], sc[2]], [sc[4 + h]])
            else:
                chk('s%d_mem0' % l)
                P.barrier()
                mem_sample(l)
                P.barrier()
                chk('s%d_mem1' % l)
            wv, wkeys = stage_win(l, [(O_SG + 1536, 512)])
            for h in range(4):
                psg = zchunk(wv, wkeys, h * 128, 128)
                act(sc[h % 2][:, 0:N], psg[:, 0:N], AF.Silu, [psg], [sc[h % 2]])
                tt('gpsimd', ys[:, 12 + h, 0:N], sc[4 + h][:, 0:N], sc[h % 2][:, 0:N], ALU.mult, [sc[4 + h], sc[h % 2]], [ys])

            for n in range(4):
                for half in range(2):
                    wbv, wbk = wnext(('wbo', l, n, half))
                    wv, wkeys = stage_win(l, [(O_MG + n * 1024 + half * 512, 512)])
                    for dc in range(4):
                        dch = half * 4 + dc
                        pg = zchunk(wv, wkeys, dc * 128, 128)
                        g_, t_ = sc[2 * (dc % 2)], sc[2 * (dc % 2) + 1]
                        act(g_[:, 0:N], pg[:, 0:N], AF.Sigmoid, [pg], [g_])
                        pp = pb[2 + dc % 2]
                        for kc in range(4):
                            mm(pp[:, 0:N], wbv[:, kc, dc * 128:(dc + 1) * 128], ys[:, n * 4 + kc, 0:N], kc == 0, kc == 3, wbk + [ys], [pp])
                        if n == 0:
                            tt('vector', merged[:, dch, 0:N], pp[:, 0:N], g_[:, 0:N], ALU.mult, [pp, g_], [merged])
                        else:
                            tt('vector', t_[:, 0:N], pp[:, 0:N], g_[:, 0:N], ALU.mult, [pp, g_], [t_])
                            tt('gpsimd', merged[:, dch, 0:N], merged[:, dch, 0:N], t_[:, 0:N], ALU.add, [merged, t_], [merged])
            cp('vector', hnT[:, 0:4, 0:N], merged[:, 0:4, 0:N], [merged], [hnT])
            cp('gpsimd', hnT[:, 4:8, 0:N], merged[:, 4:8, 0:N], [merged], [hnT])
            for half in range(2):
                wov, wok = wnext(('wo', l, half))
                for j, (t, n) in enumerate(subt):
                    pp = pb[2 + j % 2]
                    for kc in range(8):
                        mm(pp[0:n, :], hnT[:, kc, j * 128:j * 128 + n], wov[:, kc, :], kc == 0, kc == 7, wok + [hnT], [pp])
                    xt_ = xtok[0:n, t, half * 512:(half + 1) * 512]
                    tt('vector', xt_, pp[0:n, :], xt_, ALU.add, [pp, ('x', t)], [('x', t)])
            if l == 1:
                for j, (t, n) in enumerate(subt):
                    if not smp:
                        P.dma('gpsimd', yp[t * 128:(t + 1) * 128, :], xtok[:, t, :], reads=[('x', t)])
                    else:
                        P.dma('gpsimd', ysm, xtok[0:64, 16, :], reads=[('x', 16)])
        def mla_sample(l):
            N = NS
            clat_l = clats[l]
            crope_l = cropes[l]
            mq_ = [merged[:, 2 * i:2 * i + 2, :].rearrange("p a b -> p (a b)") for i in range(3)]
            mk_ = [('mgs', i) for i in range(3)]
            for h in range(4):
                mm(pb[2][0:64, 0:64], KnS[:, h, :], QnT[:, h, 0:N], True, True, [KnS, QnT], [pb[2]])
                act(sc[0][0:64, 0:64], pb[2][0:64, 0:64], AF.Exp, [pb[2]], [sc[0]], scale=MLA_SCALE)
                tt('vector', pTn[:, :, h * 8:(h + 1) * 8], sc[0][0:64, 0:64].rearrange("p (a t) -> p a t", t=8),
                   cst[0:64, C_BD:C_BD + 64].rearrange("p (a t) -> p a t", t=8), ALU.mult, [sc[0]], [pTn])
                ts('vector', QgT[:, h, :], QnT[:, h, 0:N], V(l, 44, 96), None, ALU.mult, None, [QnT], [QgT])
                mm(pb[3][:, 0:64], wukT[:, h, :], QgT[0:64, h, :], True, True, [wukT, QgT], [pb[3]])
                cp('vector', QaC[:, :, h * 8:(h + 1) * 8], pb[3][:, 0:64].rearrange("p (a t) -> p a t", t=8), [pb[3]], [QaC])
                for i in range(4):
                    mm(pb[3][:, 64:128], cbf[64:96, C_Z + 160 - 32 * i:C_Z + 288 - 32 * i], QgT[64:96, h, :], True, True, [QgT], [pb[3]])
                    cp('vector', QaR[:, :, i, h * 8:(h + 1) * 8], pb[3][:, 64:128].rearrange("p (a t) -> p a t", t=8), [pb[3]], [QaR])
            P.op('gpsimd', lambda e: e.memset(gbf[:, :, 128:136], 1.0), (), [gbf])
            chk('x1')
            for bp in range(8):
                for ch in range(8):
                    P.dma('gpsimd', graw[:, :], clat_l, reads=[idx2], writes=[graw],
                          indirect=bass.IndirectOffsetOnAxis(ap=idx2[:, bp, ch:ch + 1], axis=0))
                    P.dma('gpsimd', rraw[:, :], crope_l, reads=[idx2], writes=[rraw],
                          indirect=bass.IndirectOffsetOnAxis(ap=idx2[:, bp, ch:ch + 1], axis=0))
                    g3 = graw[:, :].rearrange("p (a r) -> p a r", r=128)
                    act(gbf[:, 0:7, 0:128], g3[:, 0:7, :], AF.Copy, [graw], [gbf])
                    cp('vector', gbf[:, 7:16, 0:128], g3[:, 7:16, :], [graw], [gbf])
                    cp('vector', rbf[:, :, :], rraw[:, :].rearrange("p (a r) -> p a r", r=32), [rraw], [rbf])
                    tt('gpsimd', mq_[0], rraw[:, :], rraw[:, :], ALU.mult, [rraw], [mk_[0]])
                    P.op('vector', lambda e: e.tensor_reduce(out=stat[:, 0:16], in_=mq_[0].rearrange("p (a r) -> p a r", r=32),
                                                             axis=AX.X, op=ALU.add), [mk_[0]], [('stat', 0)])
                    chk('x2')
                    for g in range(4):
                        for i in range(4):
                            row = g * 4 + i
                            tr(pbt[:, i * 128:(i + 1) * 128], gbf[:, row, 0:128], identb(128), [gbf], [pbt])
                        tr(pbt[:, 512:640], rbf[:, g * 4:(g + 1) * 4, :].rearrange("p a r -> p (a r)"), identb(128), [rbf], [pbt])
                        cp('vector', cTs[:, :, :], pbt[:, 0:512].rearrange("p (a r) -> p a r", r=128), [pbt], [cTs])
                        cp('vector', rTs[:, :], pbt[:, 512:640], [pbt], [rTs])
                        chk('g1')
                        for i in range(4):
                            pk_ = pb[i // 2]
                            mm(pk_[:, (i % 2) * 256:(i % 2) * 256 + 256], cTs[:, i, :], wuk_b[:, :], True, True, [cTs, wuk_b], [pk_])
                            if i == 0:
                                mm(pb[2][:, 0:128], rTs[:, :], QaR[:, bp, :, :].rearrange("p a t -> p (a t)"), True, False, [rTs, QaR], [pb[2]])
                            mm(pb[2][:, i * 32:(i + 1) * 32], cTs[:, i, :], QaC[:, bp, :], False, i == 3, [cTs, QaC], [pb[2]])
                        chk('g2')
                        act(mq_[1], pb[0][:, 0:512], AF.Square, [pb[0]], [mk_[1]])
                        act(mq_[2], pb[1][:, 0:512], AF.Square, [pb[1]], [mk_[2]])
                        P.op('vector', lambda e: e.tensor_reduce(out=stat[:, 16:24], in_=mq_[1].rearrange("p (a r) -> p a r", r=64),
                                                                 axis=AX.X, op=ALU.add), [mk_[1]], [('stat', 1)])
                        P.op('vector', lambda e: e.tensor_reduce(out=stat[:, 24:32], in_=mq_[2].rearrange("p (a r) -> p a r", r=64),
                                                                 axis=AX.X, op=ALU.add), [mk_[2]], [('stat', 1)])
                        tt('vector', stat[:, 32:48].rearrange("p (a h) -> p a h", h=4), stat[:, 16:32].rearrange("p (a h) -> p a h", h=4),
                           stat[:, g * 4:(g + 1) * 4].rearrange("p (a o) -> p a o", o=1).broadcast_to([128, 4, 4]), ALU.add,
                           [('stat', 0), ('stat', 1)], [('stat', 2)])
                        rsqrt_to(stat[:, 32:48], stat[:, 32:48], 128, 1.0 / 96, [('stat', 2)], [('stat', 2)])
                        tt('vector', sc[4][:, 0:128].rearrange("p (a t) -> p a t", t=8), pb[2][:, 0:128].rearrange("p (a t) -> p a t", t=8),
                           stat[:, 32:48].rearrange("p (a o) -> p a o", o=1).broadcast_to([128, 16, 8]), ALU.mult,
                           [pb[2], ('stat', 2)], [sc[4]])
                        chk('g3')
                        act(sc[5][:, 0:128], sc[4][:, 0:128], AF.Exp, [sc[4]], [sc[5]], scale=MLA_SCALE)
                        tt('vector', pTs[:, :, :], sc[5][:, 0:128].rearrange("p (a t) -> p a t", t=32),
                           cst[:, C_PM:C_PM + 32].rearrange("p (o t) -> p o t", o=1).broadcast_to([128, 4, 32]), ALU.mult, [sc[5]], [pTs])
                        chk('g4')
                        for i in range(4):
                            row = g * 4 + i
                            mm(pb[4][0:32, 0:130], pTs[:, i, :], gbf[:, row, 0:130], ch == 0 and g == 0 and i == 0, False, [pTs, gbf], [pb[4]])
                        chk('x3')
                mm(pb[4][0:32, 0:130], pTn[:, bp, :], cnew[:, 0:130], False, True, [pTn, cnew], [pb[4]])
                recip(small[0:32, 1:2], pb[4][0:32, 128:129], [pb[4]], ['sm1'])
                ts('vector', sc[6][0:32, 0:128], pb[4][0:32, 0:128], small[0:32, 1:2], None, ALU.mult, None, [pb[4], 'sm1'], [sc[6]])
                tr(pb[5][:, 0:32], sc[6][0:32, 0:128], identf(32), [sc[6]], [pb[5]])
                cp('vector', onT[:, :, bp * 8:(bp + 1) * 8], pb[5][:, 0:32].rearrange("p (h t) -> p h t", t=8), [pb[5]], [onT])
                chk('x4')

        def mem_sample(l):
            for b in range(16):
                P.dma('sync', mraw[:, :, :], cmk[l, b].rearrange("(mc p) d -> p mc d", p=128), writes=[mraw])
                act(mkb[:], mraw[:], AF.Copy, [mraw], [mkb])
                P.dma('sync', mraw[:, :, :], cmv[l, b].rearrange("(mc p) d -> p mc d", p=128), writes=[mraw])
                cp('vector', mvb[:], mraw[:], [mraw], [mvb])
                for h in range(4):
                    for mc in range(2):
                        tr(pbt[:, (h * 2 + mc) * 128:(h * 2 + mc + 1) * 128], mkb[:, mc, h * 128:(h + 1) * 128], identb(128), [mkb], [pbt])
                cp('vector', mkTs[:, :, :], pbt[:, 0:1024].rearrange("p (h m) -> p h m", m=256), [pbt], [mkTs])
                for h in range(4):
                    for mc in range(2):
                        o4 = (h * 2 + mc) * 4
                        mm(pb[2][:, o4:o4 + 4], mkTs[:, h, mc * 128:(mc + 1) * 128], mqT[:, h, 4 * b:4 * b + 4], True, True, [mkTs, mqT], [pb[2]])
                act(pms[:, 0:32], pb[2][:, 0:32], AF.Exp, [pb[2]], [pms], scale=MEM_SCALE)
                for h in range(4):
                    for mc in range(2):
                        o4 = (h * 2 + mc) * 4
                        mm(pb[4][:, h * 64 + 4 * b:h * 64 + 4 * b + 4], mvb[:, mc, h * 128:(h + 1) * 128], pms[:, o4:o4 + 4], mc == 0, mc == 1,
                           [mvb, pms], [pb[4]])
                    for mc in range(2):
                        o4 = (h * 2 + mc) * 4
                        mm(pb[5][:, h * 64 + 4 * b:h * 64 + 4 * b + 4], onesb(128, 128), pms[:, o4:o4 + 4], mc == 0, mc == 1, [pms], [pb[5]])
            for h in range(4):
                recip(sc[2][:, 0:64], pb[5][:, h * 64:(h + 1) * 64], [pb[5]], [sc[2]])
                tt('vector', sc[4 + h][:, 0:64], pb[4][:, h * 64:(h + 1) * 64], sc[2][:, 0:64], ALU.mult, [pb[4], sc[2]], [sc[4 + h]])

        P.barrier()
        class _Stop(Exception):
            pass

        def chk(tag):
            if STOP == tag:
                raise _Stop()
        try:
            for l in range(2):
                load_small(l)
                mem_kv(l)
                chk('memkv%d' % l)
                for ti in range(NTI):
                    tile_pass(l, 'p', ti)
                    chk('p%d_%d' % (l, ti))
                P.barrier()
                tile_pass(l, 's', 0)
                P.barrier()
                chk('s%d' % l)
        except _Stop:
            pass
        P.finish()
        print('[build] inst counts', P.cnt, 'dma vals', max(P.dma_val), flush=True)
    return nc


_NC_CACHE = {}


def kernel(x_prompt, x_sample, mem_prompt, cache_mla_latent, cache_mla_rope, page_table, state_hgrn, state_conv,
           cache_mem_k, cache_mem_v, norm_gain, w_in, conv_w, hgrn_lb, hgrn_norm, mla_q_norm, mla_w_uq,
           mla_kv_norm, mla_w_uk, mla_w_uv, mla_q_gain, mla_k_gain, mem_norm, mem_w_k, mem_w_v, mem_q_gain,
           mem_k_gain, w_branch_out, w_out):
    f = lambda a: np.ascontiguousarray(np.asarray(a, dtype=np.float32))
    cst, rope = make_consts()
    vecs = np.zeros((128, 2, 48), np.float32)
    ng, mn, cw, lb, hn = f(norm_gain), f(mem_norm), f(conv_w), f(hgrn_lb), f(hgrn_norm)
    for l in range(2):
        vecs[:, l, 0:8] = ng[l].reshape(8, 128).T
        vecs[:, l, 8:16] = mn[l].reshape(8, 128).T
        for j in range(3):
            vecs[:, l, 16 + 4 * j:20 + 4 * j] = cw[l, j].reshape(4, 128).T
        vecs[:, l, 28:32] = lb[0].reshape(4, 128).T
        vecs[:, l, 32:36] = lb[1].reshape(4, 128).T
        vecs[:, l, 36:40] = hn[l].reshape(4, 128).T
        qn = f(mla_q_norm)[l]
        vecs[:, l, 40] = qn[0:128]
        vecs[0:64, l, 41] = qn[128:192]
        vecs[:, l, 42] = f(mla_kv_norm)[l]
        vecs[0:96, l, 43] = f(mla_q_gain)[l]
        vecs[0:96, l, 44] = f(mla_k_gain)[l]
        vecs[:, l, 45] = f(mem_q_gain)[l]
        vecs[:, l, 46] = f(mem_k_gain)[l]
        vecs[:, l, 47] = EPS
    clat = f(cache_mla_latent).reshape(2, NPHYS * 8, 2048)
    crope = f(cache_mla_rope).reshape(2, NPHYS * 8, 512)
    pt = np.asarray(page_table, dtype=np.int32)
    shared = dict(clat0=clat[0], clat1=clat[1], crope0=crope[0], crope1=crope[1],
                  w_in=f(w_in), w_bo=f(w_branch_out), w_o=f(w_out), w_uq=f(mla_w_uq), w_uk=f(mla_w_uk), w_uv=f(mla_w_uv),
                  mwk=f(mem_w_k), mwv=f(mem_w_v), vecs=vecs, cst=cst, rope=rope)
    xp_, xs_, mp_ = f(x_prompt), f(x_sample), f(mem_prompt)
    shg_, scv_, cmk_, cmv_ = f(state_hgrn), f(state_conv), f(cache_mem_k), f(cache_mem_v)
    in_maps = []
    for c in range(8):
        bs = slice(16 * c, 16 * c + 16)
        m = dict(shared)
        m.update(xp=xp_[c], xs=xs_[bs].reshape(64, D), memp=mp_[c],
                 ptab=np.ascontiguousarray(pt[bs].reshape(8, 128).T),
                 shg=np.ascontiguousarray(shg_[:, bs]), scv=np.ascontiguousarray(scv_[:, bs]),
                 cmk=np.ascontiguousarray(cmk_[:, bs].reshape(2, 16, 256, 512)),
                 cmv=np.ascontiguousarray(cmv_[:, bs].reshape(2, 16, 256, 512)))
        in_maps.append(m)
    if 'nc' not in _NC_CACHE:
        _NC_CACHE['nc'] = build_nc()
    ncr = CFG['ncores']
    res = run_bass_kernel_spmd(_NC_CACHE['nc'], in_maps[:ncr], core_ids=list(range(ncr)))
    R = list(res.results)
    while len(R) < 8:
        R.append(R[0])
    cat = lambda k: np.stack([r[k] for r in R], 0)
    y_prompt = cat("yp")
    y_sample = cat("ysm").reshape(128, 4, D)
    latp = cat("o_latp").transpose(1, 0, 2, 3)
    ropep = cat("o_ropep").transpose(1, 0, 2, 3)
    hgp = cat("o_hgp").transpose(1, 0, 2, 3, 4)
    cvp = cat("o_cvp").transpose(1, 0, 2, 3)
    mkp = cat("o_mkp").transpose(1, 0, 2, 3).reshape(2, 8, 256, 4, 128)
    mvp = cat("o_mvp").transpose(1, 0, 2, 3).reshape(2, 8, 256, 4, 128)
    lats = cat("o_lats").transpose(1, 0, 2, 3).reshape(2, 128, 4, 128)
    ropes = cat("o_ropes").transpose(1, 0, 2, 3).reshape(2, 128, 4, 32)
    hgs = cat("o_hgs").transpose(1, 0, 2, 3, 4, 5).reshape(2, 128, 4, 128, 128)
    cvs = cat("o_cvs").transpose(1, 0, 2, 3, 4).reshape(2, 128, 2, 512)
    outs = (y_prompt, y_sample, latp, ropep, hgp, cvp, mkp, mvp, lats, ropes, hgs, cvs)
    return tuple(np.ascontiguousarray(o, dtype=np.float32) for o in outs)
```
